# Optimizing a Trainium2 kernel written in Bass

```python
import jax, jax.numpy as jnp
from jax import lax
import numpy as np

D_MODEL = 1024
BATCH = 8
SEQ = 4096
DEPTH = 2

PLE_DIM = 256
MIX_WIDTH = D_MODEL
GDN_WIDTH = MIX_WIDTH // 2
HGRN_WIDTH = MIX_WIDTH - GDN_WIDTH
HEAD_DIM = 128
GDN_HEADS = GDN_WIDTH // HEAD_DIM
HGRN_HEADS = HGRN_WIDTH // HEAD_DIM
CONV_WIDTH = 4
GDN_CHUNK = 64
HGRN_CHUNK = 16
D_FF = 4 * D_MODEL
EPS = 1e-6

OFF_GDN_QKV = 0
OFF_GDN_Z = OFF_GDN_QKV + 3 * GDN_WIDTH
OFF_GDN_A = OFF_GDN_Z + GDN_WIDTH
OFF_GDN_B = OFF_GDN_A + GDN_HEADS
OFF_HG_F = OFF_GDN_B + GDN_HEADS
OFF_HG_I = OFF_HG_F + HGRN_WIDTH
OFF_HG_Q = OFF_HG_I + HGRN_WIDTH
OFF_HG_G = OFF_HG_Q + HGRN_WIDTH
IN_COLS = OFF_HG_G + HGRN_WIDTH

kernel_name = 'hymba_gdn_hgrn2_sandwich_ple'


def rms_norm(x, w):
    xf = x.astype(jnp.float32)
    xf = xf * lax.rsqrt(jnp.mean(xf * xf, axis=-1, keepdims=True) + EPS)
    return (xf * w.astype(jnp.float32)).astype(x.dtype)


def gated_head_norm(o, w, z):
    o = o * lax.rsqrt(jnp.mean(o * o, axis=-1, keepdims=True) + EPS)
    return o * w.astype(jnp.float32) * jax.nn.silu(z)


def l2norm(t):
    return t * lax.rsqrt(jnp.sum(t * t, axis=-1, keepdims=True) + EPS)


def causal_depthwise_conv(x, w):
    k_w = w.shape[0]
    xp = jnp.pad(x, ((0, 0), (k_w - 1, 0), (0, 0)))
    return lax.conv_general_dilated(xp, w[:, None, :], window_strides=(1,), padding='VALID',
                                    dimension_numbers=('NWC', 'WIO', 'NWC'),
                                    feature_group_count=x.shape[-1])


def gated_delta_rule_chunked(q, k, v, g, beta):
    bsz, t_len, nh, dk = q.shape
    dv = v.shape[-1]
    c = GDN_CHUNK
    n = t_len // c
    q = q * (dk ** -0.5)
    q, k, v = (t.transpose(0, 2, 1, 3).reshape(bsz, nh, n, c, t.shape[-1]) for t in (q, k, v))
    g, beta = (t.transpose(0, 2, 1).reshape(bsz, nh, n, c) for t in (g, beta))
    gc = jnp.cumsum(g, axis=-1)
    causal = jnp.tril(jnp.ones((c, c), bool))
    strict = jnp.tril(jnp.ones((c, c), bool), -1)
    decay = jnp.exp(jnp.where(causal, gc[..., :, None] - gc[..., None, :], -jnp.inf))
    k_beta = k * beta[..., None]
    v_beta = v * beta[..., None]
    lmat = jnp.einsum('bhncd,bhnsd->bhncs', k_beta, k) * decay
    a_mat = jnp.eye(c, dtype=jnp.float32) + jnp.where(strict, lmat, 0.0)
    solve = lambda rhs: lax.linalg.triangular_solve(a_mat, rhs, left_side=True, lower=True,
                                                    unit_diagonal=True)
    u = solve(v_beta)
    w = solve(k_beta * jnp.exp(gc)[..., None])
    qk = jnp.einsum('bhncd,bhnsd->bhncs', q, k) * decay
    qg = q * jnp.exp(gc)[..., None]
    k_tail = k * jnp.exp(gc[..., -1:] - gc)[..., None]
    chunk_decay = jnp.exp(gc[..., -1])
    xs = tuple(jnp.moveaxis(t, 2, 0) for t in (u, w, qk, qg, k_tail, chunk_decay))

    def step(s, inp):
        u_n, w_n, qk_n, qg_n, kt_n, cd_n = inp
        v_new = u_n - jnp.einsum('bhcd,bhde->bhce', w_n, s)
        o = jnp.einsum('bhcd,bhde->bhce', qg_n, s) + jnp.einsum('bhcs,bhse->bhce', qk_n, v_new)
        s = s * cd_n[..., None, None] + jnp.einsum('bhcd,bhce->bhde', kt_n, v_new)
        return s, o

    s0 = jnp.zeros((bsz, nh, dk, dv), jnp.float32)
    _, o = lax.scan(step, s0, xs)
    return o.transpose(1, 0, 3, 2, 4).reshape(bsz, t_len, nh, dv)


def hgrn2_chunked(q, k, v, log_f):
    bsz, t_len, nh, dk = q.shape
    dv = v.shape[-1]
    c = HGRN_CHUNK
    n = t_len // c
    to_chunks = lambda t: jnp.moveaxis(t.reshape(bsz, n, c, nh, t.shape[-1]), 1, 0)
    bc = jnp.cumsum(to_chunks(log_f), axis=2)
    causal = jnp.tril(jnp.ones((c, c), bool))[None, :, :, None, None]

    def step(s, inp):
        q_n, k_n, v_n, b_n = inp
        pair = jnp.exp(jnp.where(causal, b_n[:, :, None] - b_n[:, None, :], -jnp.inf))
        att = jnp.einsum('bihd,bjhd,bijhd->bhij', q_n, k_n, pair)
        o = jnp.einsum('bhij,bjhe->bihe', att, v_n) + jnp.einsum('bihd,bhde->bihe', q_n * jnp.exp(b_n), s)
        b_last = b_n[:, -1]
        s = s * jnp.exp(b_last)[..., None] + jnp.einsum('bjhd,bjhe->bhde',
                                                         k_n * jnp.exp(b_last[:, None] - b_n), v_n)
        return s, o

    s0 = jnp.zeros((bsz, nh, dk, dv), jnp.float32)
    _, o = lax.scan(step, s0, (to_chunks(q), to_chunks(k), to_chunks(v), bc))
    return jnp.moveaxis(o, 0, 1).reshape(bsz, t_len, nh, dv)


def setup_inputs(seed: int = 0) -> dict:
    key = jax.random.key(seed)
    ks = jax.random.split(key, 20)
    f32 = jnp.float32
    nrm = lambda k, shape, scale: scale * jax.random.normal(k, shape, f32)
    gain = lambda k, shape: 1.0 + 0.02 * jax.random.normal(k, shape, f32)
    return {
        'x': jax.random.normal(ks[0], (BATCH, SEQ, D_MODEL), f32),
        'p': jax.random.normal(ks[1], (DEPTH, BATCH, SEQ, PLE_DIM), f32),
        'pre_mix_norm': gain(ks[2], (DEPTH, D_MODEL)),
        'w_in': nrm(ks[3], (DEPTH, D_MODEL, IN_COLS), D_MODEL ** -0.5),
        'gdn_conv': nrm(ks[4], (DEPTH, CONV_WIDTH, 3 * GDN_WIDTH), CONV_WIDTH ** -0.5),
        'gdn_a_log': jnp.log(jax.random.uniform(ks[5], (DEPTH, GDN_HEADS), f32, 1.0, 16.0)),
        'gdn_dt_bias': nrm(ks[6], (DEPTH, GDN_HEADS), 0.1),
        'gdn_norm': gain(ks[7], (DEPTH, HEAD_DIM)),
        'hgrn_lb_logits': nrm(ks[8], (DEPTH, HGRN_WIDTH), 1.0),
        'hgrn_norm': gain(ks[9], (DEPTH, HEAD_DIM)),
        'w_out': nrm(ks[10], (DEPTH, MIX_WIDTH, D_MODEL), MIX_WIDTH ** -0.5),
        'post_mix_norm': gain(ks[11], (DEPTH, D_MODEL)),
        'pre_mlp_norm': gain(ks[12], (DEPTH, D_MODEL)),
        'w_mlp_up': nrm(ks[13], (DEPTH, D_MODEL, D_FF), D_MODEL ** -0.5),
        'w_mlp_down': nrm(ks[14], (DEPTH, D_FF, D_MODEL), D_FF ** -0.5),
        'post_mlp_norm': gain(ks[15], (DEPTH, D_MODEL)),
        'w_ple_proj': nrm(ks[16], (DEPTH, PLE_DIM, D_MODEL), PLE_DIM ** -0.5),
        'ple_norm': gain(ks[17], (DEPTH, D_MODEL)),
        'w_ple_gate': nrm(ks[18], (DEPTH, D_MODEL, D_MODEL), D_MODEL ** -0.5),
    }


def reference(x, p, pre_mix_norm, w_in, gdn_conv, gdn_a_log, gdn_dt_bias, gdn_norm,
              hgrn_lb_logits, hgrn_norm, w_out, post_mix_norm, pre_mlp_norm, w_mlp_up,
              w_mlp_down, post_mlp_norm, w_ple_proj, ple_norm, w_ple_gate):
    f32 = jnp.float32
    bsz, t_len, _ = x.shape
    lb_cum = jnp.cumsum(jax.nn.softmax(hgrn_lb_logits.astype(f32), axis=0), axis=0)
    lower_bounds = lb_cum - lb_cum[0:1]
    h = x
    for i in range(DEPTH):
        hn = rms_norm(h, pre_mix_norm[i])
        proj = jnp.einsum('btd,dc->btc', hn, w_in[i]).astype(f32)
        qkv = proj[..., OFF_GDN_QKV:OFF_GDN_Z]
        z_gdn = proj[..., OFF_GDN_Z:OFF_GDN_A].reshape(bsz, t_len, GDN_HEADS, HEAD_DIM)
        a_gdn = proj[..., OFF_GDN_A:OFF_GDN_B]
        b_gdn = proj[..., OFF_GDN_B:OFF_HG_F]
        f_pre = proj[..., OFF_HG_F:OFF_HG_I].reshape(bsz, t_len, HGRN_HEADS, HEAD_DIM)
        i_hg = proj[..., OFF_HG_I:OFF_HG_Q].reshape(bsz, t_len, HGRN_HEADS, HEAD_DIM)
        q_hg = proj[..., OFF_HG_Q:OFF_HG_G].reshape(bsz, t_len, HGRN_HEADS, HEAD_DIM)
        g_hg = proj[..., OFF_HG_G:IN_COLS].reshape(bsz, t_len, HGRN_HEADS, HEAD_DIM)

        qkv = jax.nn.silu(causal_depthwise_conv(qkv, gdn_conv[i].astype(f32)))
        q_a, k_a, v_a = jnp.split(qkv, 3, axis=-1)
        q_a = l2norm(q_a.reshape(bsz, t_len, GDN_HEADS, HEAD_DIM))
        k_a = l2norm(k_a.reshape(bsz, t_len, GDN_HEADS, HEAD_DIM))
        v_a = v_a.reshape(bsz, t_len, GDN_HEADS, HEAD_DIM)
        beta = jax.nn.sigmoid(b_gdn)
        g_dec = -jnp.exp(gdn_a_log[i].astype(f32)) * jax.nn.softplus(a_gdn + gdn_dt_bias[i].astype(f32))
        o_a = gated_delta_rule_chunked(q_a, k_a, v_a, g_dec, beta)
        o_a = gated_head_norm(o_a, gdn_norm[i], z_gdn)

        lb = lower_bounds[i].reshape(HGRN_HEADS, HEAD_DIM)
        log_f = jnp.logaddexp(jax.nn.log_sigmoid(f_pre), jnp.log(lb) + jax.nn.log_sigmoid(-f_pre))
        k_b = (1.0 - lb) * jax.nn.sigmoid(-f_pre)
        o_b = hgrn2_chunked(jax.nn.silu(q_hg), k_b, i_hg, log_f)
        o_b = gated_head_norm(o_b, hgrn_norm[i], g_hg)

        mix = jnp.concatenate([o_a.reshape(bsz, t_len, GDN_WIDTH),
                               o_b.reshape(bsz, t_len, HGRN_WIDTH)], axis=-1).astype(h.dtype)
        mix_out = jnp.einsum('btc,cd->btd', mix, w_out[i])
        h = h + rms_norm(mix_out, post_mix_norm[i])

        u = jnp.einsum('btd,df->btf', rms_norm(h, pre_mlp_norm[i]), w_mlp_up[i])
        y = jnp.einsum('btf,fd->btd', jnp.square(jax.nn.relu(u)), w_mlp_down[i])
        h = h + rms_norm(y, post_mlp_norm[i])

        e = rms_norm(jnp.einsum('bte,ed->btd', p[i], w_ple_proj[i]), ple_norm[i])
        gate = jax.nn.sigmoid(jnp.einsum('btd,de->bte', h, w_ple_gate[i]))
        h = h + e * gate
    return h
```

```python
import contextlib
import numpy as np
import ml_dtypes
import concourse.bass as bass
import concourse.mybir as mybir
from concourse.bass_utils import run_bass_kernel_spmd

F32 = mybir.dt.float32
BF16 = mybir.dt.bfloat16
AF = mybir.ActivationFunctionType
ALU = mybir.AluOpType
AX = mybir.AxisListType

D = 1024
T = 4096
NLAYER = 2
TB = 512
NT = TB // 128
EPS = 1e-6
DFF = 4096
PLE = 256
INC = 4104


class Region:
    __slots__ = ("name", "w", "r", "excl")

    def __init__(self, name, excl=False):
        self.name = name
        self.w = None
        self.r = []
        self.excl = excl


class _Rec:
    def __getattr__(self, name):
        def f(*a, **k):
            return (name, a, k)
        return f


_REC = _Rec()


class Prog:
    ENG = ("pe", "act", "dve", "pool", "sp")

    def __init__(self, nc, stack):
        self.nc = nc
        self.stack = stack
        self.q = {e: [] for e in self.ENG}
        self.sem = {e: stack.enter_context(nc.semaphore("s_" + e)) for e in self.ENG}
        self.cnt = {e: 0 for e in self.ENG}
        self.seen = {e: {} for e in self.ENG}

    def new_sem(self, name):
        return self.stack.enter_context(self.nc.semaphore(name))

    def _deps(self, eng, reads, writes):
        deps = {}

        def add(tok):
            if tok is None:
                return
            sem, val, teng = tok
            if teng == "pe" and eng == "pe":
                return
            k = id(sem)
            if k not in deps or deps[k][1] < val:
                deps[k] = (sem, val)
        for b in reads:
            add(b.w)
            if b.excl:
                for t in b.r:
                    if t[2] != eng:
                        add(t)
        for b in writes:
            add(b.w)
            for t in b.r:
                add(t)
        out = []
        seen = self.seen[eng]
        for k, (sem, val) in deps.items():
            if k in seen and seen[k] >= val:
                continue
            seen[k] = val
            out.append((sem, val))
        return out

    def _mark(self, tok, reads, writes):
        for b in reads:
            b.r.append(tok)
            if len(b.r) > 64:
                best = {}
                for t in b.r:
                    k = id(t[0])
                    if k not in best or best[k][1] < t[1]:
                        best[k] = t
                b.r = list(best.values())
        for b in writes:
            b.w = tok
            b.r = []

    def op(self, eng, fn, reads=(), writes=()):
        waits = self._deps(eng, reads, writes)
        self.cnt[eng] += 1
        tok = (self.sem[eng], self.cnt[eng], eng)
        self.q[eng].append((waits, fn(_REC), self.sem[eng], 1))
        self._mark(tok, reads, writes)
        return tok

    def dma(self, eng, fn, dsem, dcount, reads=(), writes=()):
        waits = self._deps(eng, reads, writes)
        tok = (dsem, dcount * 16, "dma")
        self.q[eng].append((waits, fn(_REC), dsem, 16))
        self._mark(tok, reads, writes)
        return tok

    def wait_all(self, eng, toks):
        self.q[eng].append(([(s, v) for (s, v, _) in toks], None, None, 0))

    def emit(self):
        nc = self.nc
        engmap = {"pe": "tensor", "act": "scalar", "dve": "vector", "pool": "gpsimd", "sp": "sync"}
        with nc.Block() as block:
            for e in self.ENG:
                q = self.q[e]

                def body(engine, q=q):
                    for waits, fn, sem, inc in q:
                        for (s, v) in waits:
                            engine.wait_ge(s, v)
                        if fn is not None:
                            name, a, k = fn
                            getattr(engine, name)(*a, **k).then_inc(sem, inc)
                getattr(block, engmap[e])(body)


class Ring:
    def __init__(self, alloc, name, shape, dtype, n):
        self.items = [(alloc(f"{name}{i}", shape, dtype), Region(f"{name}{i}")) for i in range(n)]
        self.i = 0

    def next(self):
        it = self.items[self.i % len(self.items)]
        self.i += 1
        return it


class DmaBuf:
    def __init__(self, P, tensor, name):
        self.t = tensor
        self.reg = Region(name)
        self.sem = P.new_sem("d_" + name)
        self.n = 0


def build(nblk=T // TB, layers=(0, 1), dbg=False):
    nc = bass.Bass("TRN2", target_bir_lowering=False)
    NL = NLAYER
    dram = lambda n, s, d, k="ExternalInput": nc.dram_tensor(n, s, d, kind=k).ap()
    x_d = dram("x", [T, D], F32)
    p_d = dram("p", [NL, T, PLE], F32)
    w_in_d = dram("w_in", [NL, D, INC], F32)
    w_out_d = dram("w_out", [NL, D, D], F32)
    w_up_d = dram("w_up", [NL, D, DFF], F32)
    w_dn_d = dram("w_dn", [NL, DFF, D], F32)
    w_ple_d = dram("w_ple", [NL, PLE, D], F32)
    w_gate_d = dram("w_gate", [NL, D, D], F32)
    gains_d = dram("gains", [NL * 5, D], F32)
    hw_d = dram("hw", [NL * 2, 512], F32)
    sm_d = dram("sm", [NL * 2, 4], F32)
    convw_d = dram("convw", [128, NL * 48], F32)
    lbl_d = dram("lbl", [128, NL * 4], F32)
    c_identb = dram("c_identb", [128, 128], BF16)
    c_f32 = dram("c_f32", [128, 3 * 128 + 512], F32)
    out_d = dram("out", [T, D], F32, "ExternalOutput")
    dbg_d = dram("dbg", [128, 4096], F32, "ExternalOutput") if dbg else None

    with contextlib.ExitStack() as st:
        P = Prog(nc, st)
        sb = lambda n, s, d: st.enter_context(nc.sbuf_tensor(n, s, d))
        banks = [st.enter_context(nc.psum_tensor(f"bank{i}", [128, 512], F32)) for i in range(8)]
        bank_reg = [Region(f"bank{i}", excl=True) for i in range(8)]
        pool_i = {"dense": 0, "mix": 0}

        def psum(kind):
            if kind == "dense":
                i = pool_i["dense"] % 3
                pool_i["dense"] += 1
            else:
                i = 3 + pool_i["mix"] % 5
                pool_i["mix"] += 1
            return banks[i], bank_reg[i]

        def psb(bank):
            return bank[:].bitcast(BF16)[:, 0:512].rearrange("p (h n) -> p h n", h=4)

        def ps4(bank):
            return bank[:].rearrange("p (h n) -> p h n", h=4)

        identb = sb("identb", [128, 128], BF16)
        cf = sb("cf", [128, 3 * 128 + 512], F32)
        Umat = cf[:, 0:128]
        SUmat = cf[:, 128:256]
        ones = cf[:, 256:384]
        rst = cf[:, 384:896]
        convw = sb("convw_sb", [128, NL * 48], F32)
        lbl = sb("lbl_sb", [128, NL * 4], F32)
        smb = sb("smb", [128, NL * 2 * 4], F32)
        hwb = sb("hwb", [128, NL * 2 * 512], F32)
        negA = sb("negA", [128, NL * 4], F32)
        lb = sb("lb", [128, NL * 4], F32)
        oml = sb("oml", [128, NL * 4], F32)
        cst_sem = P.new_sem("cst")
        ncst = 0
        cst_toks = []

        def cload(dst, src):
            nonlocal ncst
            ncst += 1
            cst_toks.append(P.dma("sp", lambda e: e.dma_start(out=dst, in_=src), cst_sem, ncst))
        cload(identb[:], c_identb)
        cload(cf[:], c_f32)
        cload(convw[:], convw_d)
        cload(lbl[:], lbl_d)
        cload(smb[:], sm_d.rearrange("a b -> (a b)").partition_broadcast(128))
        cload(hwb[:], hw_d.rearrange("a b -> (a b)").partition_broadcast(128))
        for e in ("pe", "act", "dve", "pool"):
            P.wait_all(e, [cst_toks[-1]])
        R_c = Region("consts_derived")
        for l in range(NL):
            P.op("act", lambda e, l=l: e.activation(out=negA[:, l * 4:l * 4 + 4], in_=smb[:, l * 8:l * 8 + 4], func=AF.Exp),
                 writes=[R_c])
        P.op("dve", lambda e: e.tensor_scalar(out=negA[:], in0=negA[:], scalar1=-1.0, scalar2=None, op0=ALU.mult),
             reads=[R_c], writes=[R_c])
        elb = sb("elb", [128, NL * 4], F32)
        slb = sb("slb", [128, 4], F32)
        P.op("act", lambda e: e.activation(out=elb[:], in_=lbl[:], func=AF.Exp), writes=[R_c])
        P.op("dve", lambda e: e.tensor_copy(out=slb[:], in_=elb[:, 0:4]), reads=[R_c], writes=[R_c])
        for l in range(1, NL):
            P.op("dve", lambda e, l=l: e.tensor_tensor(out=slb[:], in0=slb[:], in1=elb[:, l * 4:l * 4 + 4], op=ALU.add),
                 reads=[R_c], writes=[R_c])
        P.op("dve", lambda e: e.reciprocal(out=slb[:], in_=slb[:]), reads=[R_c], writes=[R_c])
        P.op("dve", lambda e: e.memset(lb[:, 0:4], 0.0), reads=[R_c], writes=[R_c])
        for l in range(1, NL):
            P.op("dve", lambda e, l=l: e.tensor_tensor(out=elb[:, l * 4:l * 4 + 4], in0=elb[:, l * 4:l * 4 + 4], in1=slb[:], op=ALU.mult),
                 reads=[R_c], writes=[R_c])
            P.op("dve", lambda e, l=l: e.tensor_tensor(out=lb[:, l * 4:l * 4 + 4], in0=lb[:, (l - 1) * 4:l * 4], in1=elb[:, l * 4:l * 4 + 4], op=ALU.add),
                 reads=[R_c], writes=[R_c])
        P.op("dve", lambda e: e.tensor_scalar(out=oml[:], in0=lb[:], scalar1=-1.0, scalar2=1.0, op0=ALU.mult, op1=ALU.add),
             reads=[R_c], writes=[R_c])
        for e in ("act", "pool"):
            P.op(e, (lambda en: en.memset(slb[:, 0:1], 0.0)) if e == "pool" else
                 (lambda en: en.activation(out=slb[:, 0:1], in_=slb[:, 0:1], func=AF.Copy)), reads=[R_c], writes=[R_c])

        h = sb("h", [128, NT, D], F32)
        h_reg = [Region(f"h{t}") for t in range(NT)]
        h_ld = [DmaBuf(P, None, f"hld{t}") for t in range(NT)]
        actT = sb("actT", [128, 8, TB], BF16)
        actT_reg = [Region(f"actT{t}") for t in range(NT)]
        big = sb("big", [128, 5136], F32)
        yacc = big[:, 0:4096].rearrange("p (t d) -> p t d", t=NT)
        yacc_reg = [Region(f"yacc{t}") for t in range(NT)]
        mix = sb("mix", [128, NT, D], BF16)
        mix_reg = [Region(f"mix{t}") for t in range(NT)]
        NSLOT = 3
        wslots = [DmaBuf(P, sb(f"wslot{i}", [128, 4096], BF16), f"wslot{i}") for i in range(NSLOT)]
        wslot_i = [0]
        wab = DmaBuf(P, sb("wab", [128, 8, 8], BF16), "wab")
        gring = [DmaBuf(P, sb(f"gain{i}", [128, D], F32), f"gain{i}") for i in range(2)]
        gring_i = [0]
        pbuf = DmaBuf(P, sb("pbuf", [128, NT, PLE], BF16), "pbuf")
        out_sem = P.new_sem("outst")
        out_n = [0]
        st_sem = [P.new_sem(f"st{t}") for t in range(NT)]
        st_n = [0] * NT

        raw = big[:, 0:4 * (TB + 3)].rearrange("p (a b) -> p a b", a=4)
        raw_reg = Region("raw")
        halo = sb("halo", [128, NL * 12, 3], F32)
        halo_reg = Region("halo")
        cs = big[:, 2064:2064 + 3072].bitcast(BF16).rearrange("p (a b) -> p a b", a=12)
        cs_reg = [Region(f"cs{g}") for g in range(3)]
        AL_MIX = [raw_reg] + cs_reg
        AL_MLP = yacc_reg
        zw = sb("zw", [128, NT, 512], BF16)
        zw_reg = [Region(f"zw{t}") for t in range(NT)]
        gw = sb("gw", [128, NT, 512], BF16)
        gw_reg = [Region(f"gw{t}") for t in range(NT)]
        iv = sb("iv", [128, NT, 512], BF16)
        iv_reg = [Region(f"iv{t}") for t in range(NT)]
        ab = sb("ab", [128, NT, 8], F32)
        ab_reg = [Region(f"ab{t}") for t in range(NT)]
        qhs = sb("qhs", [128, 4, TB], BF16)
        qhs_reg = [Region(f"qhs{i}") for i in range(4)]
        big2 = sb("big2", [128, 4096], BF16)
        qtT = big2[:, 0:2048].rearrange("p (a b) -> p a b", a=4)
        ktT = big2[:, 2048:4096].rearrange("p (a b) -> p a b", a=4)
        khT = sb("khT", [128, 4, TB], BF16)
        qh0 = sb("qh0", [128, 4, NT, 128], BF16)
        qh1 = sb("qh1", [128, 4, NT, 128], BF16)
        dS = sb("dS", [128, 4, 8], F32)
        hg_reg = [Region(f"hg{i}") for i in range(4)]
        Sg = sb("Sg", [128, NL, 512], F32)
        Sh = sb("Sh", [128, NL, 512], F32)
        Sg_reg = [Region(f"Sg{l}") for l in range(NL)]
        Sh_reg = [Region(f"Sh{l}") for l in range(NL)]
        Sgb = sb("Sgb", [128, NL, 512], BF16)
        Sgb_reg = [Region(f"Sgb{l}") for l in range(NL)]
        Shb = [[(sb(f"Shb{l}_{i}", [128, 512], BF16), Region(f"Shb{l}_{i}")) for i in range(2)] for l in range(NL)]
        Shb_cur = [None] * NL
        uT_items = [(big2[:, 0:2048].rearrange("p (a b) -> p a b", a=4), Region("uT0")),
                    (big2[:, 2048:4096].rearrange("p (a b) -> p a b", a=4), Region("uT1"))]
        uT_i = [0]
        AL_UT = [uT_items[0][1], uT_items[1][1]]
        tf = Ring(sb, "tf", [128, 512], F32, 5)
        tbf = Ring(sb, "tb", [128, 512], BF16, 10)
        gd = {n: (sb("gd_" + n, [128, 512], BF16), Region("gd_" + n)) for n in
              ("qs", "kh", "kb", "kt", "vs", "khT", "kbT", "qsT", "EM", "EMT", "ECT", "qkT", "r", "vn")}
        hgf = {n: (sb("hgf_" + n, [128, 512], F32), Region("hgf_" + n)) for n in ("sg", "lf", "b", "bm", "bt")}
        tsm = Ring(sb, "ts", [128, 16], F32, 24)
        junk = Ring(sb, "junk", [128, D], BF16, 1)
        hsb = Ring(sb, "hsb", [128, D], BF16, 2)
        gU = Ring(sb, "gU", [128, 512], F32, 2)

        R0 = Region("init")
        P.op("pool", lambda e: e.memset(halo[:], 0.0), writes=[halo_reg])
        P.op("pool", lambda e: e.memset(Sg[:], 0.0), writes=Sg_reg)
        P.op("pool", lambda e: e.memset(Sh[:], 0.0), writes=Sh_reg)
        P.op("pool", lambda e: e.memset(Sgb[:], 0.0), writes=Sgb_reg)
        P.op("pool", lambda e: e.memset(qh0[:], 0.0), writes=hg_reg)
        P.op("pool", lambda e: e.memset(qh1[:], 0.0), writes=hg_reg)
        for l in range(NL):
            t_, r_ = Shb[l][0]
            P.op("pool", lambda e, t_=t_: e.memset(t_[:], 0.0), writes=[r_])
            Shb_cur[l] = 0

        class Load:
            def __init__(self, src_ap):
                self.src = src_ap
                self.slot = None
                self.finished = False

        loads = []
        issued = [0]

        def _try_issue():
            while issued[0] < len(loads):
                k = issued[0]
                if k >= NSLOT and not loads[k - NSLOT].finished:
                    return
                ld = loads[k]
                s = wslots[k % NSLOT]
                s.n += 1
                src_ap = ld.src
                n_el = 1
                for v in src_ap.shape[1:]:
                    n_el *= v
                dst = s.t[:, 0:n_el].rearrange("p (a b) -> p a b", a=src_ap.shape[1])
                P.dma("pool", lambda e: e.dma_start(out=dst, in_=src_ap), s.sem, s.n, writes=[s.reg])
                ld.slot = s
                issued[0] += 1

        def wreq(src_ap):
            ld = Load(src_ap)
            loads.append(ld)
            return ld

        def wget(ld):
            _try_issue()
            assert ld.slot is not None, "weight slot ring too small for this access pattern"
            return ld.slot

        def wdone(ld):
            ld.finished = True
            _try_issue()

        def gload(row):
            g = gring[gring_i[0] % 2]
            gring_i[0] += 1
            g.n += 1
            P.dma("sp", lambda e: e.dma_start(out=g.t[:], in_=gains_d[row, :].partition_broadcast(128)), g.sem, g.n,
                  writes=[g.reg])
            return g

        def rstd_from_ss(ss_ap, n, dim, rss):
            P.op("act", lambda e: e.activation(out=ss_ap, in_=ss_ap, func=AF.Ln, scale=1.0 / dim, bias=EPS),
                 reads=[rss], writes=[rss])
            P.op("act", lambda e: e.activation(out=ss_ap, in_=ss_ap, func=AF.Exp, scale=-0.5),
                 reads=[rss], writes=[rss])

        def head_scale(eng, out4, in4, sc, reads, writes, nh=4):
            for hh in range(nh):
                if eng == "act":
                    P.op("act", lambda e, hh=hh: e.activation(out=out4[:, hh, :], in_=in4[:, hh, :], func=AF.Copy,
                                                              scale=sc[:, hh:hh + 1]), reads=reads, writes=writes)
                else:
                    P.op(eng, lambda e, hh=hh: e.tensor_scalar(out=out4[:, hh, :], in0=in4[:, hh, :],
                                                               scalar1=sc[:, hh:hh + 1], scalar2=None, op0=ALU.mult),
                         reads=reads, writes=writes)

        def to_actT(src_ap_fn, src_regs, gain=None, norm=True):
            for t in range(NT):
                src = src_ap_fn(t)
                hs_t, hs_r = hsb.next()
                if norm:
                    ss_t, ss_r = tsm.next()
                    jk, jr = junk.next()
                    P.op("act", lambda e, src=src, jk=jk, ss_t=ss_t: e.activation(out=jk[:], in_=src, func=AF.Square,
                                                                                accum_out=ss_t[:, 0:1]),
                         reads=[src_regs[t]], writes=[jr, ss_r])
                    rstd_from_ss(ss_t[:, 0:1], 1, D, ss_r)
                    P.op("dve", lambda e, src=src, hs_t=hs_t, ss_t=ss_t: e.scalar_tensor_tensor(
                        out=hs_t[:], in0=src, scalar=ss_t[:, 0:1], in1=gain.t[:], op0=ALU.mult, op1=ALU.mult),
                        reads=[src_regs[t], ss_r, gain.reg], writes=[hs_r])
                else:
                    P.op("act", lambda e, src=src, hs_t=hs_t: e.activation(out=hs_t[:], in_=src, func=AF.Copy),
                         reads=[src_regs[t]], writes=[hs_r])
                bk, br = psum("dense")
                pv = bk[:].bitcast(BF16).rearrange("p (c n) -> p c n", c=8)
                for c in range(8):
                    P.op("pe", lambda e, c=c, pv=pv, hs_t=hs_t: e.transpose(pv[:, c, :], hs_t[:, c * 128:(c + 1) * 128], identb[:]),
                         reads=[hs_r], writes=[br])
                eng = "act" if t % 2 == 0 else "dve"
                if eng == "act":
                    P.op("act", lambda e, pv=pv, t=t: e.activation(out=actT[:, :, t * 128:(t + 1) * 128], in_=pv, func=AF.Copy),
                         reads=[br], writes=[actT_reg[t]])
                else:
                    P.op("dve", lambda e, pv=pv, t=t: e.tensor_copy(out=actT[:, :, t * 128:(t + 1) * 128], in_=pv),
                         reads=[br], writes=[actT_reg[t]])

        def mm_fm(slot, w3, cchunk, out_fn):
            bk, br = psum("dense")
            for kc in range(8):
                P.op("pe", lambda e, kc=kc, bk=bk: e.matmul(bk[:], lhsT=w3[:, kc, cchunk * 128:(cchunk + 1) * 128],
                                                          rhs=actT[:, kc, :], start=(kc == 0), stop=(kc == 7)),
                     reads=[slot.reg] + actT_reg, writes=[br])
            out_fn(bk, br)

        def mm_tm(slot, w3, t, ncols, out_fn, kch=8, lhs=None, lhs_regs=None, col0=0):
            bk, br = psum("dense")
            for kc in range(kch):
                lt = actT[:, kc, t * 128:(t + 1) * 128] if lhs is None else lhs(kc)
                P.op("pe", lambda e, kc=kc, bk=bk, lt=lt: e.matmul(bk[:, 0:ncols], lhsT=lt, rhs=w3[:, kc, col0:col0 + ncols],
                                                                  start=(kc == 0), stop=(kc == kch - 1)),
                     reads=[slot.reg] + ([actT_reg[t]] if lhs is None else lhs_regs), writes=[br])
            out_fn(bk, br)

        dbg_n = [0]

        def dbg_dump(ap, regs, width):
            if not dbg:
                return
            c0 = dbg_n[0]
            dbg_n[0] += width
            tt_, tr_ = tf.next()
            P.op("dve", lambda e: e.tensor_copy(out=tt_[:, 0:width], in_=ap), reads=regs, writes=[tr_])
            out_n[0] += 1
            P.dma("sp", lambda e: e.dma_start(out=dbg_d[:, c0:c0 + width], in_=tt_[:, 0:width]), out_sem, out_n[0],
                  reads=[tr_])

        for blk in range(nblk):
            tok0 = blk * TB
            for t in range(NT):
                hb = h_ld[t]
                hb.n += 1
                P.dma("sp", lambda e, t=t: e.dma_start(out=h[:, t, :], in_=x_d[tok0 + t * 128: tok0 + (t + 1) * 128, :]),
                      hb.sem, hb.n, writes=[h_reg[t]])
            for l in layers:
                g_pre = gload(l * 5 + 0)
                to_actT(lambda t: h[:, t, :], h_reg, gain=g_pre)
                wabn = wab
                wabn.n += 1
                P.dma("pool", lambda e, l=l: e.dma_start(
                    out=wab.t[:], in_=w_in_d[l, :, 2048:2056].rearrange("(c p) n -> p c n", p=128)),
                    wab.sem, wab.n, writes=[wab.reg])
                win = lambda c0: w_in_d[l, :, c0:c0 + 512].rearrange("(c p) n -> p c n", p=128)
                w8 = lambda ap: ap.rearrange("(c p) n -> p c n", p=128)
                L_qkv = [wreq(win(g_ * 512)) for g_ in range(3)]
                L_z = wreq(win(1536))
                L_qh = wreq(win(3080))
                L_f = wreq(win(2056))
                L_i = wreq(win(2568))
                L_g = wreq(win(3592))
                L_o = [wreq(w8(w_out_d[l, :, 0:512])), wreq(w8(w_out_d[l, :, 512:1024]))]
                L_ud = []
                for g_ in range(8):
                    L_ud.append((wreq(w8(w_up_d[l, :, g_ * 512:(g_ + 1) * 512])), wreq(w8(w_dn_d[l, g_ * 512:(g_ + 1) * 512, :]))))
                L_p = wreq(w8(w_ple_d[l]))
                L_g0 = wreq(w8(w_gate_d[l, :, 0:512]))
                L_g1 = wreq(w8(w_gate_d[l, :, 512:1024]))
                for grp in range(3):
                    s = wget(L_qkv[grp])
                    w3 = s.t[:].rearrange("p (a b) -> p a b", a=8)
                    P.op("pool", lambda e, grp=grp: e.tensor_copy(out=raw[:, :, 0:3], in_=halo[:, l * 12 + grp * 4:l * 12 + grp * 4 + 4, :]),
                         reads=[halo_reg], writes=[raw_reg] + AL_MLP)
                    for hh in range(4):
                        def ev(bk, br, hh=hh):
                            P.op("act", lambda e: e.activation(out=raw[:, hh, 3:3 + TB], in_=bk[:], func=AF.Copy),
                                 reads=[br], writes=[raw_reg])
                        mm_fm(s, w3, hh, ev)
                    wdone(L_qkv[grp])
                    P.op("pool", lambda e, grp=grp: e.tensor_copy(out=halo[:, l * 12 + grp * 4:l * 12 + grp * 4 + 4, :], in_=raw[:, :, TB:TB + 3]),
                         reads=[raw_reg], writes=[halo_reg])
                    for hh in range(4):
                        ch = grp * 4 + hh
                        acc_t, acc_r = tf.next()
                        wc = lambda k: convw[:, l * 48 + ch * 4 + k: l * 48 + ch * 4 + k + 1]
                        P.op("dve", lambda e, hh=hh, acc_t=acc_t, wc=wc: e.tensor_scalar(
                            out=acc_t[:], in0=raw[:, hh, 0:TB], scalar1=wc(0), scalar2=None, op0=ALU.mult),
                            reads=[raw_reg], writes=[acc_r])
                        for k in range(1, 4):
                            P.op("dve", lambda e, hh=hh, k=k, acc_t=acc_t, wc=wc: e.scalar_tensor_tensor(
                                out=acc_t[:], in0=raw[:, hh, k:k + TB], scalar=wc(k), in1=acc_t[:], op0=ALU.mult, op1=ALU.add),
                                reads=[raw_reg, acc_r], writes=[acc_r])
                        P.op("act", lambda e, ch=ch, acc_t=acc_t: e.activation(out=cs[:, ch, :], in_=acc_t[:], func=AF.Silu),
                             reads=[acc_r], writes=[cs_reg[grp]] + AL_MLP)
                s = wget(L_z)
                w3 = s.t[:].rearrange("p (a b) -> p a b", a=8)
                for t in range(NT):
                    def ev(bk, br, t=t):
                        zt, zr = tf.next()
                        P.op("act", lambda e: e.activation(out=zt[:], in_=bk[:], func=AF.Silu), reads=[br], writes=[zr])
                        P.op("pool", lambda e: e.tensor_tensor(out=zw[:, t, :], in0=zt[:], in1=hwb[:, (l * 2) * 512:(l * 2 + 1) * 512], op=ALU.mult),
                             reads=[zr], writes=[zw_reg[t]])
                    mm_tm(s, w3, t, 512, ev)
                wdone(L_z)
                for t in range(NT):
                    def ev(bk, br, t=t):
                        P.op("act", lambda e: e.activation(out=ab[:, t, :], in_=bk[:, 0:8], func=AF.Copy), reads=[br], writes=[ab_reg[t]])
                    mm_tm(wab, wab.t[:], t, 8, ev)
                s_q = wget(L_qh)
                w3q = s_q.t[:].rearrange("p (a b) -> p a b", a=8)
                for hh in range(4):
                    def ev(bk, br, hh=hh):
                        P.op("act", lambda e: e.activation(out=qhs[:, hh, :], in_=bk[:], func=AF.Silu), reads=[br], writes=[qhs_reg[hh]])
                    mm_fm(s_q, w3q, hh, ev)
                wdone(L_qh)
                s_f = wget(L_f)
                w3f = s_f.t[:].rearrange("p (a b) -> p a b", a=8)
                for hh in range(4):
                    def ev(bk, br, hh=hh):
                        lcol = l * 4 + hh
                        sg, sgr = hgf["sg"]
                        lf, lfr = hgf["lf"]
                        b_, b_r = hgf["b"]
                        bm, bmr = hgf["bm"]
                        bt, btr = hgf["bt"]
                        P.op("act", lambda e: e.activation(out=sg[:], in_=bk[:], func=AF.Sigmoid), reads=[br], writes=[sgr])
                        P.op("dve", lambda e: e.tensor_scalar(out=sg[:], in0=sg[:], scalar1=oml[:, lcol:lcol + 1], scalar2=lb[:, lcol:lcol + 1],
                                                              op0=ALU.mult, op1=ALU.add), reads=[sgr], writes=[sgr])
                        P.op("act", lambda e: e.activation(out=lf[:], in_=sg[:], func=AF.Ln), reads=[sgr], writes=[lfr])
                        P.op("pool", lambda e: e.tensor_scalar(out=sg[:], in0=sg[:], scalar1=-1.0, scalar2=1.0, op0=ALU.mult, op1=ALU.add),
                             reads=[sgr, lfr], writes=[sgr])
                        P.op("dve", lambda e: e.tensor_tensor_scan(out=b_[:], data0=rst, data1=lf[:], initial=0.0, op0=ALU.mult, op1=ALU.add),
                             reads=[lfr], writes=[b_r])
                        b3 = b_[:].rearrange("p (c n) -> p c n", n=64)
                        eb, ebr = lf, lfr
                        P.op("act", lambda e: e.activation(out=eb[:], in_=b_[:], func=AF.Exp), reads=[b_r], writes=[ebr])
                        eb4 = eb[:].rearrange("p (t c n) -> p t c n", t=NT, c=2)
                        q4 = qhs[:, hh, :].rearrange("p (t c n) -> p t c n", t=NT, c=2)
                        P.op("dve", lambda e: e.tensor_tensor(out=qh0[:, hh, :, 0:64], in0=q4[:, :, 0, :], in1=eb4[:, :, 0, :], op=ALU.mult),
                             reads=[ebr, qhs_reg[hh]], writes=[hg_reg[hh]])
                        P.op("dve", lambda e: e.tensor_tensor(out=qh1[:, hh, :, 64:128], in0=q4[:, :, 1, :], in1=eb4[:, :, 1, :], op=ALU.mult),
                             reads=[ebr, qhs_reg[hh]], writes=[hg_reg[hh]])
                        P.op("act", lambda e: e.activation(out=dS[:, hh, :], in_=b3[:, :, 63], func=AF.Exp), reads=[b_r], writes=[hg_reg[hh]])
                        bm3 = bm[:].rearrange("p (c n) -> p c n", n=64)
                        bt3 = bt[:].rearrange("p (c n) -> p c n", n=64)
                        for c in range(8):
                            P.op("pool", lambda e, c=c: e.tensor_scalar(out=bm3[:, c, :], in0=b3[:, c, :], scalar1=b3[:, c, 31:32], scalar2=None, op0=ALU.subtract),
                                 reads=[b_r], writes=[bmr])
                            P.op("pool", lambda e, c=c: e.tensor_scalar(out=bt3[:, c, :], in0=b3[:, c, :], scalar1=b3[:, c, 63:64], scalar2=None, op0=ALU.subtract),
                                 reads=[b_r], writes=[btr])
                        P.op("act", lambda e: e.activation(out=bt[:], in_=bt[:], func=AF.Exp, scale=-1.0), reads=[btr], writes=[btr])
                        P.op("dve", lambda e: e.tensor_tensor(out=khT[:, hh, :], in0=sg[:], in1=bt[:], op=ALU.mult),
                             reads=[btr, sgr], writes=[hg_reg[hh]])
                        P.op("act", lambda e: e.activation(out=bt[:], in_=bm[:], func=AF.Exp), reads=[bmr, btr], writes=[btr])
                        P.op("dve", lambda e: e.tensor_tensor(out=qtT[:, hh, :], in0=qhs[:, hh, :], in1=bt[:], op=ALU.mult),
                             reads=[btr, qhs_reg[hh]], writes=[hg_reg[hh]] + AL_UT)
                        P.op("act", lambda e: e.activation(out=bm[:], in_=bm[:], func=AF.Exp, scale=-1.0), reads=[bmr], writes=[bmr])
                        P.op("dve", lambda e: e.tensor_tensor(out=ktT[:, hh, :], in0=sg[:], in1=bm[:], op=ALU.mult),
                             reads=[bmr, sgr], writes=[hg_reg[hh]] + AL_UT)
                    mm_fm(s_f, w3f, hh, ev)
                wdone(L_f)
                s_i = wget(L_i)
                w3i = s_i.t[:].rearrange("p (a b) -> p a b", a=8)
                for t in range(NT):
                    def ev(bk, br, t=t):
                        P.op("act", lambda e: e.activation(out=iv[:, t, :], in_=bk[:], func=AF.Copy), reads=[br], writes=[iv_reg[t]])
                    mm_tm(s_i, w3i, t, 512, ev)
                wdone(L_i)
                s = wget(L_g)
                w3 = s.t[:].rearrange("p (a b) -> p a b", a=8)
                for t in range(NT):
                    def ev(bk, br, t=t):
                        zt, zr = tf.next()
                        P.op("act", lambda e: e.activation(out=zt[:], in_=bk[:], func=AF.Silu), reads=[br], writes=[zr])
                        P.op("pool", lambda e: e.tensor_tensor(out=gw[:, t, :], in0=zt[:], in1=hwb[:, (l * 2 + 1) * 512:(l * 2 + 2) * 512], op=ALU.mult),
                             reads=[zr], writes=[gw_reg[t]])
                    mm_tm(s, w3, t, 512, ev)
                wdone(L_g)

                for t in range(NT):
                    tsl = slice(t * 128, (t + 1) * 128)
                    pq_b, pq_r = psum("mix")
                    pk_b, pk_r = psum("mix")
                    pv_b, pv_r = psum("mix")
                    for grp, (pb, pr) in enumerate(((pq_b, pq_r), (pk_b, pk_r), (pv_b, pv_r))):
                        for hh in range(4):
                            P.op("pe", lambda e, grp=grp, hh=hh, pb=pb: e.transpose(psb(pb)[:, hh, :], cs[:, grp * 4 + hh, tsl], identb[:]),
                                 reads=[cs_reg[grp]], writes=[pr])
                    sc, scr = tsm.next()
                    for pb, pr, c0 in ((pq_b, pq_r, 0), (pk_b, pk_r, 4)):
                        sq_t, sq_r = tf.next()
                        P.op("act", lambda e, pb=pb, sq_t=sq_t: e.activation(out=sq_t[:].rearrange("p (h n) -> p h n", h=4), in_=psb(pb), func=AF.Square),
                             reads=[pr], writes=[sq_r])
                        P.op("dve", lambda e, sq_t=sq_t, c0=c0: e.tensor_reduce(out=sc[:, c0:c0 + 4], in_=sq_t[:].rearrange("p (h n) -> p h n", h=4),
                                                                                axis=AX.X, op=ALU.add), reads=[sq_r], writes=[scr])
                    rstd_from_ss(sc[:, 0:8], 8, 1.0, scr)
                    g1, g1r = tsm.next()
                    P.op("act", lambda e: e.activation(out=g1[:, 0:4], in_=ab[:, t, 4:8], func=AF.Exp, scale=-1.0), reads=[ab_reg[t]], writes=[g1r])
                    P.op("dve", lambda e: e.tensor_scalar(out=g1[:, 0:4], in0=g1[:, 0:4], scalar1=1.0, scalar2=None, op0=ALU.add), reads=[g1r], writes=[g1r])
                    P.op("dve", lambda e: e.reciprocal(out=g1[:, 0:4], in_=g1[:, 0:4]), reads=[g1r], writes=[g1r])
                    P.op("dve", lambda e: e.tensor_tensor(out=g1[:, 8:12], in0=ab[:, t, 0:4], in1=smb[:, l * 8 + 4:l * 8 + 8], op=ALU.add),
                         reads=[ab_reg[t], g1r], writes=[g1r])
                    P.op("act", lambda e: e.activation(out=g1[:, 8:12], in_=g1[:, 8:12], func=AF.Exp), reads=[g1r], writes=[g1r])
                    P.op("act", lambda e: e.activation(out=g1[:, 8:12], in_=g1[:, 8:12], func=AF.Ln, bias=1.0), reads=[g1r], writes=[g1r])
                    P.op("dve", lambda e: e.tensor_tensor(out=g1[:, 4:8], in0=g1[:, 8:12], in1=negA[:, l * 4:l * 4 + 4], op=ALU.mult),
                         reads=[g1r], writes=[g1r])
                    pg_b, pg_r = psum("mix")
                    for i_, m_ in enumerate((Umat, SUmat, ones)):
                        P.op("pe", lambda e, i_=i_, m_=m_: e.matmul(pg_b[:, i_ * 4:i_ * 4 + 4], lhsT=m_, rhs=g1[:, 4:8], start=True, stop=True),
                             reads=[g1r], writes=[pg_r])
                    eg, egr = tsm.next()
                    P.op("act", lambda e: e.activation(out=eg[:, 0:12], in_=pg_b[:, 0:12], func=AF.Exp), reads=[pg_r], writes=[egr])
                    P.op("dve", lambda e: e.tensor_scalar(out=eg[:, 12:16], in0=eg[:, 0:4], scalar1=-1.0, scalar2=None, op0=ALU.mult), reads=[egr], writes=[egr])
                    s2, s2r = tsm.next()
                    P.op("dve", lambda e: e.tensor_scalar(out=s2[:, 0:4], in0=sc[:, 0:4], scalar1=128.0 ** -0.5, scalar2=None, op0=ALU.mult), reads=[scr], writes=[s2r])
                    P.op("dve", lambda e: e.scalar_tensor_tensor(out=s2[:, 4:8], in0=sc[:, 4:8], scalar=-1.0, in1=g1[:, 0:4], op0=ALU.mult, op1=ALU.mult),
                         reads=[scr, g1r, s2r], writes=[s2r])
                    P.op("dve", lambda e: e.tensor_tensor(out=s2[:, 8:12], in0=sc[:, 4:8], in1=eg[:, 4:8], op=ALU.mult), reads=[scr, egr, s2r], writes=[s2r])
                    qs, qsr = gd["qs"]
                    kh, khr = gd["kh"]
                    kb, kbr = gd["kb"]
                    kt, ktr = gd["kt"]
                    vs, vsr = gd["vs"]
                    v4 = lambda tt_: tt_[:].rearrange("p (h n) -> p h n", h=4)
                    head_scale("act", v4(qs), psb(pq_b), s2[:, 0:4], [pq_r, s2r], [qsr])
                    head_scale("dve", v4(kh), psb(pk_b), sc[:, 4:8], [pk_r, scr], [khr])
                    head_scale("act", v4(kb), psb(pk_b), s2[:, 4:8], [pk_r, s2r], [kbr])
                    head_scale("dve", v4(kt), psb(pk_b), s2[:, 8:12], [pk_r, s2r], [ktr])
                    P.op("act", lambda e: e.activation(out=v4(vs), in_=psb(pv_b), func=AF.Copy), reads=[pv_r], writes=[vsr])
                    fm = []
                    for (src, srcr), dn in (((kh, khr), "khT"), ((kb, kbr), "kbT"), ((qs, qsr), "qsT")):
                        pb, pr = psum("mix")
                        for hh in range(4):
                            P.op("pe", lambda e, hh=hh, pb=pb, src=src: e.transpose(psb(pb)[:, hh, :], src[:, hh * 128:(hh + 1) * 128], identb[:]),
                                 reads=[srcr], writes=[pr])
                        dst, dstr = gd[dn]
                        P.op("dve" if len(fm) != 1 else "act",
                             (lambda e, dst=dst, pb=pb: e.tensor_copy(out=v4(dst), in_=psb(pb))) if len(fm) != 1 else
                             (lambda e, dst=dst, pb=pb: e.activation(out=v4(dst), in_=psb(pb), func=AF.Copy)),
                             reads=[pr], writes=[dstr])
                        fm.append((dst, dstr))
                    (khT_, khTr), (kbT_, kbTr), (qsT_, qsTr) = fm
                    pG_b, pG_r = psum("mix")
                    pGT_b, pGT_r = psum("mix")
                    pQK_b, pQK_r = psum("mix")
                    for hh in range(4):
                        hs_ = slice(hh * 128, (hh + 1) * 128)
                        P.op("pe", lambda e, hh=hh, hs_=hs_: e.matmul(ps4(pG_b)[:, hh, :], lhsT=khT_[:, hs_], rhs=kbT_[:, hs_], start=True, stop=True),
                             reads=[khTr, kbTr], writes=[pG_r])
                        P.op("pe", lambda e, hh=hh, hs_=hs_: e.matmul(ps4(pGT_b)[:, hh, :], lhsT=kbT_[:, hs_], rhs=khT_[:, hs_], start=True, stop=True),
                             reads=[khTr, kbTr], writes=[pGT_r])
                        P.op("pe", lambda e, hh=hh, hs_=hs_: e.matmul(ps4(pQK_b)[:, hh, :], lhsT=khT_[:, hs_], rhs=qsT_[:, hs_], start=True, stop=True),
                             reads=[khTr, qsTr], writes=[pQK_r])
                    gu, gur = gU.next()
                    for hh in range(4):
                        P.op("pool", lambda e, hh=hh: e.tensor_scalar(out=gu[:, hh * 128:(hh + 1) * 128], in0=Umat, scalar1=g1[:, 4 + hh:5 + hh], scalar2=None, op0=ALU.mult),
                             reads=[g1r], writes=[gur])
                    pD_b, pD_r = psum("mix")
                    pDT_b, pDT_r = psum("mix")
                    for hh in range(4):
                        hs_ = slice(hh * 128, (hh + 1) * 128)
                        P.op("pe", lambda e, hh=hh, hs_=hs_: e.matmul(ps4(pD_b)[:, hh, :], lhsT=gu[:, hs_], rhs=SUmat, start=True, stop=True),
                             reads=[gur], writes=[pD_r])
                        P.op("pe", lambda e, hh=hh, hs_=hs_: e.matmul(ps4(pDT_b)[:, hh, :], lhsT=SUmat, rhs=gu[:, hs_], start=True, stop=True),
                             reads=[gur], writes=[pDT_r])
                    E_, Er = tf.next()
                    ET_, ETr = tf.next()
                    P.op("act", lambda e: e.activation(out=E_[:], in_=pD_b[:], func=AF.Exp), reads=[pD_r], writes=[Er])
                    P.op("act", lambda e: e.activation(out=ET_[:], in_=pDT_b[:], func=AF.Exp), reads=[pDT_r], writes=[ETr])
                    EM, EMr = gd["EM"]
                    EMT, EMTr = gd["EMT"]
                    ECT, ECTr = gd["ECT"]
                    P.op("pool", lambda e: e.affine_select(out=v4(EM), in_=E_[:].rearrange("p (h n) -> p h n", h=4), pattern=[[0, 4], [-1, 128]],
                                                           compare_op=ALU.is_ge, fill=0.0, base=-1, channel_multiplier=1), reads=[Er], writes=[EMr])
                    P.op("pool", lambda e: e.affine_select(out=v4(EMT), in_=ET_[:].rearrange("p (h n) -> p h n", h=4), pattern=[[0, 4], [1, 128]],
                                                           compare_op=ALU.is_ge, fill=0.0, base=-1, channel_multiplier=-1), reads=[ETr], writes=[EMTr])
                    P.op("pool", lambda e: e.affine_select(out=v4(ECT), in_=ET_[:].rearrange("p (h n) -> p h n", h=4), pattern=[[0, 4], [1, 128]],
                                                           compare_op=ALU.is_ge, fill=0.0, base=0, channel_multiplier=-1), reads=[ETr], writes=[ECTr])
                    Pm, Pmr = tbf.next()
                    PT, PTr = tbf.next()
                    qkT, qkTr = gd["qkT"]
                    P.op("dve", lambda e: e.tensor_tensor(out=Pm[:], in0=pG_b[:], in1=EM[:], op=ALU.mult), reads=[pG_r, EMr], writes=[Pmr])
                    P.op("dve", lambda e: e.tensor_tensor(out=PT[:], in0=pGT_b[:], in1=EMT[:], op=ALU.mult), reads=[pGT_r, EMTr], writes=[PTr])
                    P.op("dve", lambda e: e.tensor_tensor(out=qkT[:], in0=pQK_b[:], in1=ECT[:], op=ALU.mult), reads=[pQK_r, ECTr], writes=[qkTr])
                    X, Xr = tbf.next()
                    for hh in range(4):
                        P.op("pool", lambda e, hh=hh, X=X, PT=PT: e.tensor_tensor(out=X[:, hh * 128:(hh + 1) * 128], in0=PT[:, hh * 128:(hh + 1) * 128], in1=identb[:], op=ALU.add),
                             reads=[PTr], writes=[Xr])
                    for n_ in range(1, 7):
                        pP_b, pP_r = psum("mix")
                        for hh in range(4):
                            hs_ = slice(hh * 128, (hh + 1) * 128)
                            P.op("pe", lambda e, hh=hh, hs_=hs_, PT=PT, Pm=Pm, pP_b=pP_b: e.matmul(ps4(pP_b)[:, hh, :], lhsT=PT[:, hs_], rhs=Pm[:, hs_], start=True, stop=True),
                                 reads=[PTr, Pmr], writes=[pP_r])
                        if n_ < 6:
                            pPT_b, pPT_r = psum("mix")
                            for hh in range(4):
                                hs_ = slice(hh * 128, (hh + 1) * 128)
                                P.op("pe", lambda e, hh=hh, hs_=hs_, PT=PT, Pm=Pm, pPT_b=pPT_b: e.matmul(ps4(pPT_b)[:, hh, :], lhsT=Pm[:, hs_], rhs=PT[:, hs_], start=True, stop=True),
                                     reads=[PTr, Pmr], writes=[pPT_r])
                        Pn, Pnr = tbf.next()
                        P.op("act", lambda e, Pn=Pn, pP_b=pP_b: e.activation(out=Pn[:], in_=pP_b[:], func=AF.Copy), reads=[pP_r], writes=[Pnr])
                        if n_ < 6:
                            PTn, PTnr = tbf.next()
                            P.op("dve", lambda e, PTn=PTn, pPT_b=pPT_b: e.tensor_copy(out=PTn[:], in_=pPT_b[:]), reads=[pPT_r], writes=[PTnr])
                        pX_b, pX_r = psum("mix")
                        for hh in range(4):
                            hs_ = slice(hh * 128, (hh + 1) * 128)
                            P.op("pe", lambda e, hh=hh, hs_=hs_, Pn=Pn, X=X, pX_b=pX_b: e.matmul(ps4(pX_b)[:, hh, :], lhsT=Pn[:, hs_], rhs=X[:, hs_], start=True, stop=False),
                                 reads=[Pnr, Xr], writes=[pX_r])
                            P.op("pe", lambda e, hh=hh, hs_=hs_, X=X, pX_b=pX_b: e.matmul(ps4(pX_b)[:, hh, :], lhsT=identb[:], rhs=X[:, hs_], start=False, stop=True),
                                 reads=[Xr], writes=[pX_r])
                        Xn, Xnr = tbf.next()
                        P.op("act" if n_ % 2 else "dve",
                             (lambda e, Xn=Xn, pX_b=pX_b: e.activation(out=Xn[:], in_=pX_b[:], func=AF.Copy)) if n_ % 2 else
                             (lambda e, Xn=Xn, pX_b=pX_b: e.tensor_copy(out=Xn[:], in_=pX_b[:])), reads=[pX_r], writes=[Xnr])
                        X, Xr = Xn, Xnr
                        Pm, Pmr = Pn, Pnr
                        if n_ < 6:
                            PT, PTr = PTn, PTnr
                    pKS_b, pKS_r = psum("mix")
                    pO1_b, pO1_r = psum("mix")
                    for hh in range(4):
                        hs_ = slice(hh * 128, (hh + 1) * 128)
                        P.op("pe", lambda e, hh=hh, hs_=hs_: e.matmul(ps4(pKS_b)[:, hh, :], lhsT=khT_[:, hs_], rhs=Sgb[:, l, hs_], start=True, stop=True),
                             reads=[khTr, Sgb_reg[l]], writes=[pKS_r])
                        P.op("pe", lambda e, hh=hh, hs_=hs_: e.matmul(ps4(pO1_b)[:, hh, :], lhsT=qsT_[:, hs_], rhs=Sgb[:, l, hs_], start=True, stop=True),
                             reads=[qsTr, Sgb_reg[l]], writes=[pO1_r])
                    r_, rr = gd["r"]
                    for hh in range(4):
                        hs_ = slice(hh * 128, (hh + 1) * 128)
                        P.op("dve", lambda e, hh=hh, hs_=hs_: e.scalar_tensor_tensor(out=r_[:, hs_], in0=ps4(pKS_b)[:, hh, :], scalar=eg[:, 12 + hh:13 + hh], in1=vs[:, hs_],
                                                                                    op0=ALU.mult, op1=ALU.add), reads=[pKS_r, egr, vsr], writes=[rr])
                    t1, t1r = tf.next()
                    head_scale("act", t1[:].rearrange("p (h n) -> p h n", h=4), ps4(pO1_b), eg[:, 0:4], [pO1_r, egr], [t1r])
                    pV_b, pV_r = psum("mix")
                    for hh in range(4):
                        hs_ = slice(hh * 128, (hh + 1) * 128)
                        P.op("pe", lambda e, hh=hh, hs_=hs_, X=X: e.matmul(ps4(pV_b)[:, hh, :], lhsT=X[:, hs_], rhs=r_[:, hs_], start=True, stop=True),
                             reads=[Xr, rr], writes=[pV_r])
                    vn, vnr = gd["vn"]
                    head_scale("act", v4(vn), ps4(pV_b), g1[:, 0:4], [pV_r, g1r], [vnr])
                    pO2_b, pO2_r = psum("mix")
                    pS_b, pS_r = psum("mix")
                    for hh in range(4):
                        hs_ = slice(hh * 128, (hh + 1) * 128)
                        P.op("pe", lambda e, hh=hh, hs_=hs_: e.matmul(ps4(pO2_b)[:, hh, :], lhsT=qkT[:, hs_], rhs=vn[:, hs_], start=True, stop=True),
                             reads=[qkTr, vnr], writes=[pO2_r])
                        P.op("pe", lambda e, hh=hh, hs_=hs_: e.matmul(ps4(pS_b)[:, hh, :], lhsT=kt[:, hs_], rhs=vn[:, hs_], start=True, stop=True),
                             reads=[ktr, vnr], writes=[pS_r])
                    og, ogr = tf.next()
                    P.op("dve", lambda e: e.tensor_tensor(out=og[:], in0=pO2_b[:], in1=t1[:], op=ALU.add), reads=[pO2_r, t1r], writes=[ogr])
                    for hh in range(4):
                        hs_ = slice(hh * 128, (hh + 1) * 128)
                        P.op("dve", lambda e, hh=hh, hs_=hs_: e.scalar_tensor_tensor(out=Sg[:, l, hs_], in0=Sg[:, l, hs_], scalar=eg[:, 8 + hh:9 + hh], in1=ps4(pS_b)[:, hh, :],
                                                                                    op0=ALU.mult, op1=ALU.add), reads=[pS_r, egr, Sg_reg[l]], writes=[Sg_reg[l]])
                    P.op("pool", lambda e: e.tensor_copy(out=Sgb[:, l, :], in_=Sg[:, l, :]), reads=[Sg_reg[l]], writes=[Sgb_reg[l]])

                    def head_norm(o_ap4, o_regs, wmul, wmul_reg, col0):
                        sq_t, sq_r = tf.next()
                        P.op("act", lambda e: e.activation(out=sq_t[:].rearrange("p (h n) -> p h n", h=4), in_=o_ap4, func=AF.Square), reads=o_regs, writes=[sq_r])
                        so, sor = tsm.next()
                        P.op("dve", lambda e: e.tensor_reduce(out=so[:, 0:4], in_=sq_t[:].rearrange("p (h n) -> p h n", h=4), axis=AX.X, op=ALU.add), reads=[sq_r], writes=[sor])
                        rstd_from_ss(so[:, 0:4], 4, 128.0, sor)
                        a_, ar = tf.next()
                        head_scale("act", a_[:].rearrange("p (h n) -> p h n", h=4), o_ap4, so[:, 0:4], o_regs + [sor], [ar])
                        P.op("pool", lambda e: e.tensor_tensor(out=mix[:, t, col0:col0 + 512], in0=a_[:], in1=wmul, op=ALU.mult), reads=[ar, wmul_reg], writes=[mix_reg[t]])
                    head_norm(og[:].rearrange("p (h n) -> p h n", h=4), [ogr], zw[:, t, :], zw_reg[t], 0)

                    pA_b, pA_r = psum("mix")
                    pKH_b, pKH_r = psum("mix")
                    for hh in range(4):
                        P.op("pe", lambda e, hh=hh: e.matmul(ps4(pA_b)[:, hh, :], lhsT=ktT[:, hh, tsl], rhs=qtT[:, hh, tsl], start=True, stop=True),
                             reads=[hg_reg[hh]], writes=[pA_r])
                        P.op("pe", lambda e, hh=hh: e.transpose(psb(pKH_b)[:, hh, :], khT[:, hh, tsl], identb[:]), reads=[hg_reg[hh]], writes=[pKH_r])
                    at_, atr = tf.next()
                    P.op("act", lambda e: e.activation(out=at_[:], in_=pA_b[:], func=AF.Copy), reads=[pA_r], writes=[atr])
                    aT, aTr = tbf.next()
                    P.op("pool", lambda e: e.affine_select(out=v4(aT), in_=at_[:].rearrange("p (h n) -> p h n", h=4), pattern=[[0, 4], [1, 128]],
                                                           compare_op=ALU.is_ge, fill=0.0, base=0, channel_multiplier=-1), reads=[atr], writes=[aTr])
                    P.op("pool", lambda e: e.memset(v4(aT)[0:64, :, 64:128], 0.0), reads=[aTr], writes=[aTr])
                    khm, khmr = tbf.next()
                    P.op("dve", lambda e: e.tensor_copy(out=v4(khm), in_=psb(pKH_b)), reads=[pKH_r], writes=[khmr])
                    Sprev, Sprevr = Shb[l][Shb_cur[l]]
                    pU_b, pU_r = psum("mix")
                    for hh in range(4):
                        hs_ = slice(hh * 128, (hh + 1) * 128)
                        P.op("pe", lambda e, hh=hh, hs_=hs_: e.matmul(ps4(pU_b)[:, hh, :], lhsT=khm[0:64, hs_], rhs=iv[0:64, t, hs_], start=True, stop=True),
                             reads=[khmr, iv_reg[t]], writes=[pU_r])
                    for hh in range(4):
                        hs_ = slice(hh * 128, (hh + 1) * 128)
                        P.op("dve", lambda e, hh=hh, hs_=hs_: e.scalar_tensor_tensor(out=Sh[:, l, hs_], in0=Sh[:, l, hs_], scalar=dS[:, hh, 2 * t:2 * t + 1], in1=ps4(pU_b)[:, hh, :],
                                                                                    op0=ALU.mult, op1=ALU.add), reads=[pU_r, hg_reg[hh], Sh_reg[l]], writes=[Sh_reg[l]])
                    Smid, Smidr = Shb[l][1 - Shb_cur[l]]
                    P.op("pool", lambda e: e.tensor_copy(out=Smid[:], in_=Sh[:, l, :]), reads=[Sh_reg[l]], writes=[Smidr])
                    pU2_b, pU2_r = psum("mix")
                    for hh in range(4):
                        hs_ = slice(hh * 128, (hh + 1) * 128)
                        P.op("pe", lambda e, hh=hh, hs_=hs_: e.matmul(ps4(pU2_b)[:, hh, :], lhsT=khm[64:128, hs_], rhs=iv[64:128, t, hs_], start=True, stop=True),
                             reads=[khmr, iv_reg[t]], writes=[pU2_r])
                    for hh in range(4):
                        hs_ = slice(hh * 128, (hh + 1) * 128)
                        P.op("dve", lambda e, hh=hh, hs_=hs_: e.scalar_tensor_tensor(out=Sh[:, l, hs_], in0=Sh[:, l, hs_], scalar=dS[:, hh, 2 * t + 1:2 * t + 2], in1=ps4(pU2_b)[:, hh, :],
                                                                                    op0=ALU.mult, op1=ALU.add), reads=[pU2_r, hg_reg[hh], Sh_reg[l], Smidr], writes=[Sh_reg[l]])
                    pO_b, pO_r = psum("mix")
                    for hh in range(4):
                        hs_ = slice(hh * 128, (hh + 1) * 128)
                        P.op("pe", lambda e, hh=hh, hs_=hs_: e.matmul(ps4(pO_b)[:, hh, :], lhsT=aT[:, hs_], rhs=iv[:, t, hs_], start=True, stop=False),
                             reads=[aTr, iv_reg[t]], writes=[pO_r])
                        P.op("pe", lambda e, hh=hh, hs_=hs_: e.matmul(ps4(pO_b)[:, hh, :], lhsT=qh0[:, hh, t, :], rhs=Sprev[:, hs_], start=False, stop=False),
                             reads=[hg_reg[hh], Sprevr], writes=[pO_r])
                        P.op("pe", lambda e, hh=hh, hs_=hs_: e.matmul(ps4(pO_b)[:, hh, :], lhsT=qh1[:, hh, t, :], rhs=Smid[:, hs_], start=False, stop=True),
                             reads=[hg_reg[hh], Smidr], writes=[pO_r])
                    Snew, Snewr = Sprev, Sprevr
                    P.op("pool", lambda e: e.tensor_copy(out=Snew[:], in_=Sh[:, l, :]), reads=[Sh_reg[l]], writes=[Snewr])
                    head_norm(ps4(pO_b), [pO_r], gw[:, t, :], gw_reg[t], 512)

                g_pm = gload(l * 5 + 1)
                to_actT(lambda t: mix[:, t, :], mix_reg, norm=False)

                def post_norm(src_fn, src_regs_fn, gain, t, eng2="pool"):
                    ss_t, ss_r = tsm.next()
                    for hf in range(2):
                        jk, jr = junk.next()
                        P.op("act", lambda e, hf=hf, jk=jk: e.activation(out=jk[:, 0:512], in_=src_fn(hf), func=AF.Square, accum_out=ss_t[:, hf:hf + 1]),
                             reads=src_regs_fn(hf), writes=[jr, ss_r])
                    P.op("dve", lambda e: e.tensor_tensor(out=ss_t[:, 0:1], in0=ss_t[:, 0:1], in1=ss_t[:, 1:2], op=ALU.add), reads=[ss_r], writes=[ss_r])
                    rstd_from_ss(ss_t[:, 0:1], 1, D, ss_r)
                    for hf in range(2):
                        tm_, tmr = tf.next()
                        P.op("dve", lambda e, hf=hf, tm_=tm_: e.scalar_tensor_tensor(out=tm_[:], in0=src_fn(hf), scalar=ss_t[:, 0:1], in1=gain.t[:, hf * 512:(hf + 1) * 512],
                                                                                    op0=ALU.mult, op1=ALU.mult), reads=src_regs_fn(hf) + [ss_r, gain.reg], writes=[tmr])
                        P.op(eng2, lambda e, hf=hf, tm_=tm_: e.tensor_tensor(out=h[:, t, hf * 512:(hf + 1) * 512], in0=h[:, t, hf * 512:(hf + 1) * 512], in1=tm_[:], op=ALU.add),
                             reads=[tmr, h_reg[t]], writes=[h_reg[t]])

                s_o0 = wget(L_o[0])
                s_o1 = wget(L_o[1])
                for t in range(NT):
                    res = []
                    for hf, s_ in enumerate((s_o0, s_o1)):
                        mm_tm(s_, s_.t[:].rearrange("p (a b) -> p a b", a=8), t, 512, lambda bk, br: res.append((bk, br)))
                    post_norm(lambda hf: res[hf][0][:], lambda hf: [res[hf][1]], g_pm, t)
                wdone(L_o[0])
                wdone(L_o[1])

                g_pl = gload(l * 5 + 2)
                to_actT(lambda t: h[:, t, :], h_reg, gain=g_pl)
                for g in range(8):
                    s_u = wget(L_ud[g][0])
                    w3u = s_u.t[:].rearrange("p (a b) -> p a b", a=8)
                    u_t, u_r = uT_items[uT_i[0] % 2]
                    uT_i[0] += 1
                    for hc in range(4):
                        def ev(bk, br, hc=hc):
                            sq_t, sq_r = tf.next()
                            P.op("act", lambda e: e.activation(out=sq_t[:], in_=bk[:], func=AF.Square), reads=[br], writes=[sq_r])
                            P.op("dve", lambda e: e.scalar_tensor_tensor(out=u_t[:, hc, :], in0=bk[:], scalar=0.0, in1=sq_t[:], op0=ALU.is_gt, op1=ALU.mult),
                                 reads=[br, sq_r], writes=[u_r] + hg_reg)
                        mm_fm(s_u, w3u, hc, ev)
                    wdone(L_ud[g][0])
                    s_d = wget(L_ud[g][1])
                    w3d = s_d.t[:].rearrange("p (a b) -> p a b", a=4)
                    for t in range(NT):
                        for hf in range(2):
                            def ev(bk, br, t=t, hf=hf):
                                dst = yacc[:, t, hf * 512:(hf + 1) * 512]
                                if g == 0:
                                    P.op("act", lambda e: e.activation(out=dst, in_=bk[:], func=AF.Copy), reads=[br], writes=[yacc_reg[t]] + AL_MIX)
                                else:
                                    P.op("dve", lambda e: e.tensor_tensor(out=dst, in0=bk[:], in1=dst, op=ALU.add), reads=[br, yacc_reg[t]], writes=[yacc_reg[t]])
                            mm_tm(s_d, w3d, t, 512, ev, kch=4, lhs=lambda kc, t=t: u_t[:, kc, t * 128:(t + 1) * 128], lhs_regs=[u_r], col0=hf * 512)
                    wdone(L_ud[g][1])
                g_pml = gload(l * 5 + 3)
                for t in range(NT):
                    post_norm(lambda hf, t=t: yacc[:, t, hf * 512:(hf + 1) * 512], lambda hf, t=t: [yacc_reg[t]], g_pml, t)

                g_ple = gload(l * 5 + 4)
                pbuf.n += 1
                P.dma("pool", lambda e, l=l: e.dma_start(out=pbuf.t[:], in_=p_d[l, tok0:tok0 + TB, :].rearrange("(t p) e -> p t e", p=128)),
                      pbuf.sem, pbuf.n, writes=[pbuf.reg])
                s_p = wget(L_p)
                s_g0 = wget(L_g0)
                s_g1 = wget(L_g1)
                to_actT(lambda t: h[:, t, :], h_reg, norm=False)
                w3p = s_p.t[:, 0:2048].rearrange("p (a b) -> p a b", a=2)
                for t in range(NT):
                    pb, pr = psum("mix")
                    for c in range(2):
                        P.op("pe", lambda e, c=c, pb=pb: e.transpose(psb(pb)[:, c, :], pbuf.t[:, t, c * 128:(c + 1) * 128], identb[:]), reads=[pbuf.reg], writes=[pr])
                    pT_, pTr = tbf.next()
                    P.op("dve", lambda e, pb=pb, pT_=pT_: e.tensor_copy(out=pT_[:, 0:256], in_=pb[:].bitcast(BF16)[:, 0:256]), reads=[pr], writes=[pTr])
                    eres = []
                    for hf in range(2):
                        mm_tm(s_p, w3p, t, 512, lambda bk, br: eres.append((bk, br)), kch=2,
                              lhs=lambda kc, pT_=pT_: pT_[:, kc * 128:(kc + 1) * 128], lhs_regs=[pTr], col0=hf * 512)
                    ss_t, ss_r = tsm.next()
                    for hf in range(2):
                        jk, jr = junk.next()
                        P.op("act", lambda e, hf=hf, jk=jk: e.activation(out=jk[:, 0:512], in_=eres[hf][0][:], func=AF.Square, accum_out=ss_t[:, hf:hf + 1]),
                             reads=[eres[hf][1]], writes=[jr, ss_r])
                    P.op("dve", lambda e: e.tensor_tensor(out=ss_t[:, 0:1], in0=ss_t[:, 0:1], in1=ss_t[:, 1:2], op=ALU.add), reads=[ss_r], writes=[ss_r])
                    rstd_from_ss(ss_t[:, 0:1], 1, D, ss_r)
                    en = []
                    for hf in range(2):
                        tm_, tmr = tf.next()
                        P.op("dve", lambda e, hf=hf, tm_=tm_: e.scalar_tensor_tensor(out=tm_[:], in0=eres[hf][0][:], scalar=ss_t[:, 0:1], in1=g_ple.t[:, hf * 512:(hf + 1) * 512],
                                                                                    op0=ALU.mult, op1=ALU.mult), reads=[eres[hf][1], ss_r, g_ple.reg], writes=[tmr])
                        en.append((tm_, tmr))
                    for hf, s_ in enumerate((s_g0, s_g1)):
                        def ev(bk, br, hf=hf):
                            gt, gtr = tf.next()
                            P.op("act", lambda e: e.activation(out=gt[:], in_=bk[:], func=AF.Sigmoid), reads=[br], writes=[gtr])
                            P.op("pool", lambda e: e.tensor_tensor(out=gt[:], in0=gt[:], in1=en[hf][0][:], op=ALU.mult), reads=[gtr, en[hf][1]], writes=[gtr])
                            P.op("pool", lambda e: e.tensor_tensor(out=h[:, t, hf * 512:(hf + 1) * 512], in0=h[:, t, hf * 512:(hf + 1) * 512], in1=gt[:], op=ALU.add),
                                 reads=[gtr, h_reg[t]], writes=[h_reg[t]])
                        mm_tm(s_, s_.t[:].rearrange("p (a b) -> p a b", a=8), t, 512, ev)
                wdone(L_p)
                wdone(L_g0)
                wdone(L_g1)
            for t in range(NT):
                st_n[t] += 1
                P.dma("sp", lambda e, t=t: e.dma_start(out=out_d[tok0 + t * 128: tok0 + (t + 1) * 128, :], in_=h[:, t, :]), st_sem[t], st_n[t],
                      reads=[h_reg[t]])
        P.wait_all("sp", [(st_sem[t], st_n[t] * 16, "dma") for t in range(NT)])
        P.emit()
    return nc


def host_consts():
    idx = np.arange(128)
    U = (idx[:, None] <= idx[None, :]).astype(np.float32)
    SU = (idx[:, None] > idx[None, :]).astype(np.float32)
    ones = np.ones((128, 128), np.float32)
    rst = np.ones((128, 512), np.float32)
    rst[:, ::64] = 0.0
    return {
        "c_identb": np.eye(128, dtype=np.float32).astype(ml_dtypes.bfloat16),
        "c_f32": np.ascontiguousarray(np.concatenate([U, SU, ones, rst], axis=1)),
    }


def make_in_maps(inp):
    f = lambda a: np.ascontiguousarray(np.asarray(a, dtype=np.float32))
    NL = NLAYER
    gains = np.stack([np.stack([f(inp[k])[l] for k in ("pre_mix_norm", "post_mix_norm", "pre_mlp_norm", "post_mlp_norm", "ple_norm")])
                      for l in range(NL)]).reshape(NL * 5, D)
    hw = np.stack([np.stack([np.tile(f(inp["gdn_norm"])[l], 4), np.tile(f(inp["hgrn_norm"])[l], 4)]) for l in range(NL)]).reshape(NL * 2, 512)
    sm = np.stack([np.stack([f(inp["gdn_a_log"])[l], f(inp["gdn_dt_bias"])[l]]) for l in range(NL)]).reshape(NL * 2, 4)
    cw = f(inp["gdn_conv"]).reshape(NL, 4, 12, 128).transpose(3, 0, 2, 1).reshape(128, NL * 48)
    lbl = f(inp["hgrn_lb_logits"]).reshape(NL, 4, 128).transpose(2, 0, 1).reshape(128, NL * 4)
    shared = {
        "w_in": f(inp["w_in"]), "w_out": f(inp["w_out"]), "w_up": f(inp["w_mlp_up"]), "w_dn": f(inp["w_mlp_down"]),
        "w_ple": f(inp["w_ple_proj"]), "w_gate": f(inp["w_ple_gate"]),
        "gains": np.ascontiguousarray(gains), "hw": np.ascontiguousarray(hw), "sm": np.ascontiguousarray(sm),
        "convw": np.ascontiguousarray(cw), "lbl": np.ascontiguousarray(lbl),
    }
    shared.update(host_consts())
    x = f(inp["x"])
    p = f(inp["p"])
    maps = []
    for b in range(x.shape[0]):
        m = dict(shared)
        m["x"] = np.ascontiguousarray(x[b])
        m["p"] = np.ascontiguousarray(p[:, b])
        maps.append(m)
    return maps


def kernel(**inputs):
    nc = build()
    maps = make_in_maps(inputs)
    res = run_bass_kernel_spmd(nc, maps, core_ids=list(range(len(maps))))
    return np.stack([np.asarray(r["out"], dtype=np.float32) for r in res.results], axis=0)
```

```python
import contextlib
import numpy as np
import ml_dtypes
import concourse.bass as bass
import concourse.mybir as mybir
from concourse.bass_utils import run_bass_kernel_spmd

F32 = mybir.dt.float32
BF16 = mybir.dt.bfloat16
AF = mybir.ActivationFunctionType
ALU = mybir.AluOpType
AX = mybir.AxisListType

D = 1024
T = 4096
NLAYER = 2
TB = 512
NT = TB // 128
EPS = 1e-6
DFF = 4096
PLE = 256
INC = 4104


class Region:
    __slots__ = ("name", "w", "r", "excl")

    def __init__(self, name, excl=False):
        self.name = name
        self.w = None
        self.r = []
        self.excl = excl


class _Rec:
    def __getattr__(self, name):
        def f(*a, **k):
            return (name, a, k)
        return f


_REC = _Rec()
DEBUG_MAP = None


class Prog:
    ENG = ("pe", "act", "dve", "pool", "sp")

    def __init__(self, nc, stack):
        self.nc = nc
        self.stack = stack
        self.q = {e: [] for e in self.ENG}
        self.sem = {e: stack.enter_context(nc.semaphore("s_" + e)) for e in self.ENG}
        self.cnt = {e: 0 for e in self.ENG}
        self.seen = {e: {} for e in self.ENG}

    def new_sem(self, name):
        return self.stack.enter_context(self.nc.semaphore(name))

    def _deps(self, eng, reads, writes):
        deps = {}

        def add(tok):
            if tok is None:
                return
            sem, val, teng = tok
            if teng == "pe" and eng == "pe":
                return
            k = id(sem)
            if k not in deps or deps[k][1] < val:
                deps[k] = (sem, val)
        for b in reads:
            add(b.w)
            if b.excl:
                for t in b.r:
                    if t[2] != eng:
                        add(t)
        for b in writes:
            add(b.w)
            for t in b.r:
                add(t)
        out = []
        seen = self.seen[eng]
        for k, (sem, val) in deps.items():
            if k in seen and seen[k] >= val:
                continue
            seen[k] = val
            out.append((sem, val))
        return out

    def _mark(self, tok, reads, writes):
        for b in reads:
            b.r.append(tok)
            if len(b.r) > 64:
                best = {}
                for t in b.r:
                    k = id(t[0])
                    if k not in best or best[k][1] < t[1]:
                        best[k] = t
                b.r = list(best.values())
        for b in writes:
            b.w = tok
            b.r = []

    def op(self, eng, fn, reads=(), writes=()):
        waits = self._deps(eng, reads, writes)
        self.cnt[eng] += 1
        tok = (self.sem[eng], self.cnt[eng], eng)
        self.q[eng].append((waits, fn(_REC), self.sem[eng], 1))
        self._mark(tok, reads, writes)
        return tok

    def dma(self, eng, fn, dsem, dcount, reads=(), writes=()):
        waits = self._deps(eng, reads, writes)
        tok = (dsem, dcount * 16, "dma")
        self.q[eng].append((waits, fn(_REC), dsem, 16))
        self._mark(tok, reads, writes)
        return tok

    def wait_all(self, eng, toks):
        self.q[eng].append(([(s, v) for (s, v, _) in toks], None, None, 0))

    def emit(self):
        nc = self.nc
        engmap = {"pe": "tensor", "act": "scalar", "dve": "vector", "pool": "gpsimd", "sp": "sync"}
        with nc.Block() as block:
            for e in self.ENG:
                q = self.q[e]

                def body(engine, q=q):
                    for waits, fn, sem, inc in q:
                        for (s, v) in waits:
                            engine.wait_ge(s, v)
                        if fn is not None:
                            name, a, k = fn
                            ins = getattr(engine, name)(*a, **k)
                            ins.then_inc(sem, inc)
                            if DEBUG_MAP is not None:
                                try:
                                    DEBUG_MAP.append((str(getattr(ins, "name", None) or getattr(getattr(ins, "ins", None), "name", None)), e, name,
                                                      str(k.get("out", a[0] if a else ""))[:120]))
                                except Exception as ex:
                                    DEBUG_MAP.append(("?", e, name, str(ex)))
                getattr(block, engmap[e])(body)


class Ring:
    def __init__(self, alloc, name, shape, dtype, n):
        self.items = [(alloc(f"{name}{i}", shape, dtype), Region(f"{name}{i}")) for i in range(n)]
        self.i = 0

    def next(self):
        it = self.items[self.i % len(self.items)]
        self.i += 1
        return it


class DmaBuf:
    def __init__(self, P, tensor, name):
        self.t = tensor
        self.reg = Region(name)
        self.sem = P.new_sem("d_" + name)
        self.n = 0


def build(nblk=T // TB, layers=(0, 1), dbg=False):
    nc = bass.Bass("TRN2", target_bir_lowering=False)
    NL = NLAYER
    dram = lambda n, s, d, k="ExternalInput": nc.dram_tensor(n, s, d, kind=k).ap()
    x_d = dram("x", [T, D], F32)
    p_d = dram("p", [NL, T, PLE], F32)
    w_in_d = dram("w_in", [NL, D, INC], F32)
    w_out_d = dram("w_out", [NL, D, D], F32)
    w_up_d = dram("w_up", [NL, D, DFF], F32)
    w_dn_d = dram("w_dn", [NL, DFF, D], F32)
    w_ple_d = dram("w_ple", [NL, PLE, D], F32)
    w_gate_d = dram("w_gate", [NL, D, D], F32)
    gains_d = dram("gains", [NL * 5, D], F32)
    hw_d = dram("hw", [NL * 2, 512], F32)
    sm_d = dram("sm", [NL * 2, 4], F32)
    convw_d = dram("convw", [128, NL * 48], F32)
    lbl_d = dram("lbl", [128, NL * 4], F32)
    c_identb = dram("c_identb", [128, 128], BF16)
    c_f32 = dram("c_f32", [128, 6 * 128 + 512], F32)
    out_d = dram("out", [T, D], F32, "ExternalOutput")
    dbg_d = dram("dbg", [128, 4096], F32, "ExternalOutput") if dbg else None

    with contextlib.ExitStack() as st:
        P = Prog(nc, st)
        sb = lambda n, s, d: st.enter_context(nc.sbuf_tensor(n, s, d))
        banks = [st.enter_context(nc.psum_tensor(f"bank{i}", [128, 512], F32)) for i in range(8)]
        bank_reg = [Region(f"bank{i}", excl=True) for i in range(8)]
        pool_i = {"dense": 0, "mix": 0}

        def psum(kind):
            if kind == "dense":
                i = pool_i["dense"] % 3
                pool_i["dense"] += 1
            else:
                i = 3 + pool_i["mix"] % 5
                pool_i["mix"] += 1
            return banks[i], bank_reg[i]

        def psb(bank):
            return bank[:].bitcast(BF16)[:, 0:512].rearrange("p (h n) -> p h n", h=4)

        def ps4(bank):
            return bank[:].rearrange("p (h n) -> p h n", h=4)

        identb = sb("identb", [128, 128], BF16)
        cf = sb("cf", [128, 6 * 128 + 512], F32)
        Umat = cf[:, 0:128]
        SUmat = cf[:, 128:256]
        ones = cf[:, 256:384]
        rst = cf[:, 384:896]
        identf = cf[:, 896:1024]
        NEGU = cf[:, 1024:1152]
        NEGL = cf[:, 1152:1280]
        convw = sb("convw_sb", [128, NL * 48], F32)
        lbl = sb("lbl_sb", [128, NL * 4], F32)
        smb = sb("smb", [128, NL * 2 * 4], F32)
        hwb = sb("hwb", [128, NL * 2 * 512], F32)
        negA = sb("negA", [128, NL * 4], F32)
        lb = sb("lb", [128, NL * 4], F32)
        oml = sb("oml", [128, NL * 4], F32)
        cst_sem = P.new_sem("cst")
        ncst = 0
        cst_toks = []

        def cload(dst, src):
            nonlocal ncst
            ncst += 1
            cst_toks.append(P.dma("sp", lambda e: e.dma_start(out=dst, in_=src), cst_sem, ncst))
        cload(identb[:], c_identb)
        cload(cf[:], c_f32)
        cload(convw[:], convw_d)
        cload(lbl[:], lbl_d)
        cload(smb[:], sm_d.rearrange("a b -> (a b)").partition_broadcast(128))
        cload(hwb[:], hw_d.rearrange("a b -> (a b)").partition_broadcast(128))
        for e in ("pe", "act", "dve", "pool"):
            P.wait_all(e, [cst_toks[-1]])
        R_c = Region("consts_derived")
        for l in range(NL):
            P.op("act", lambda e, l=l: e.activation(out=negA[:, l * 4:l * 4 + 4], in_=smb[:, l * 8:l * 8 + 4], func=AF.Exp),
                 writes=[R_c])
        P.op("dve", lambda e: e.tensor_scalar(out=negA[:], in0=negA[:], scalar1=-1.0, scalar2=None, op0=ALU.mult),
             reads=[R_c], writes=[R_c])
        elb = sb("elb", [128, NL * 4], F32)
        slb = sb("slb", [128, 4], F32)
        P.op("act", lambda e: e.activation(out=elb[:], in_=lbl[:], func=AF.Exp), writes=[R_c])
        P.op("dve", lambda e: e.tensor_copy(out=slb[:], in_=elb[:, 0:4]), reads=[R_c], writes=[R_c])
        for l in range(1, NL):
            P.op("dve", lambda e, l=l: e.tensor_tensor(out=slb[:], in0=slb[:], in1=elb[:, l * 4:l * 4 + 4], op=ALU.add),
                 reads=[R_c], writes=[R_c])
        P.op("dve", lambda e: e.reciprocal(out=slb[:], in_=slb[:]), reads=[R_c], writes=[R_c])
        P.op("dve", lambda e: e.memset(lb[:, 0:4], 0.0), reads=[R_c], writes=[R_c])
        for l in range(1, NL):
            P.op("dve", lambda e, l=l: e.tensor_tensor(out=elb[:, l * 4:l * 4 + 4], in0=elb[:, l * 4:l * 4 + 4], in1=slb[:], op=ALU.mult),
                 reads=[R_c], writes=[R_c])
            P.op("dve", lambda e, l=l: e.tensor_tensor(out=lb[:, l * 4:l * 4 + 4], in0=lb[:, (l - 1) * 4:l * 4], in1=elb[:, l * 4:l * 4 + 4], op=ALU.add),
                 reads=[R_c], writes=[R_c])
        P.op("dve", lambda e: e.tensor_scalar(out=oml[:], in0=lb[:], scalar1=-1.0, scalar2=1.0, op0=ALU.mult, op1=ALU.add),
             reads=[R_c], writes=[R_c])
        for e in ("act", "pool"):
            P.op(e, (lambda en: en.memset(slb[:, 0:1], 0.0)) if e == "pool" else
                 (lambda en: en.activation(out=slb[:, 0:1], in_=slb[:, 0:1], func=AF.Copy)), reads=[R_c], writes=[R_c])

        h = sb("h", [128, NT, D], F32)
        h_reg = [Region(f"h{t}") for t in range(NT)]
        h_ld = [DmaBuf(P, None, f"hld{t}") for t in range(NT)]
        actT = sb("actT", [128, 8, TB], BF16)
        actT_reg = [Region(f"actT{t}") for t in range(NT)]
        big = sb("big", [128, 5136], F32)
        yacc = big[:, 0:4096].rearrange("p (t d) -> p t d", t=NT)
        yacc_reg = [Region(f"yacc{t}") for t in range(NT)]
        mix = sb("mix", [128, NT, D], BF16)
        mix_reg = [Region(f"mix{t}") for t in range(NT)]
        NSLOT = 3
        wslots = [DmaBuf(P, sb(f"wslot{i}", [128, 4096], BF16), f"wslot{i}") for i in range(NSLOT)]
        wslot_i = [0]
        wab = DmaBuf(P, sb("wab", [128, 8, 8], BF16), "wab")
        gring = [DmaBuf(P, sb(f"gain{i}", [128, D], F32), f"gain{i}") for i in range(2)]
        gring_i = [0]
        pbuf = DmaBuf(P, sb("pbuf", [128, NT, PLE], BF16), "pbuf")
        out_sem = P.new_sem("outst")
        out_n = [0]
        st_sem = [P.new_sem(f"st{t}") for t in range(NT)]
        st_n = [0] * NT

        raw = big[:, 0:4 * (TB + 3)].rearrange("p (a b) -> p a b", a=4)
        raw_reg = Region("raw")
        halo = sb("halo", [128, NL * 12, 3], F32)
        halo_reg = Region("halo")
        cs = big[:, 2064:2064 + 3072].bitcast(BF16).rearrange("p (a b) -> p a b", a=12)
        cs_reg = [Region(f"cs{g}") for g in range(3)]
        AL_MIX = [raw_reg] + cs_reg
        AL_MLP = yacc_reg
        zw = sb("zw", [128, NT, 512], BF16)
        zw_reg = [Region(f"zw{t}") for t in range(NT)]
        gw = sb("gw", [128, NT, 512], BF16)
        gw_reg = [Region(f"gw{t}") for t in range(NT)]
        iv = sb("iv", [128, NT, 512], BF16)
        iv_reg = [Region(f"iv{t}") for t in range(NT)]
        ab = sb("ab", [128, NT, 8], F32)
        ab_reg = [Region(f"ab{t}") for t in range(NT)]
        qhs = sb("qhs", [128, 4, TB], BF16)
        qhs_reg = [Region(f"qhs{i}") for i in range(4)]
        big2 = sb("big2", [128, 4096], BF16)
        qtT = big2[:, 0:2048].rearrange("p (a b) -> p a b", a=4)
        ktT = big2[:, 2048:4096].rearrange("p (a b) -> p a b", a=4)
        khT = sb("khT", [128, 4, TB], BF16)
        qh0 = sb("qh0", [128, 4, NT, 128], BF16)
        qh1 = sb("qh1", [128, 4, NT, 128], BF16)
        dS = sb("dS", [128, 4, 8], F32)
        hg_reg = [Region(f"hg{i}") for i in range(4)]
        Sg = sb("Sg", [128, NL, 512], F32)
        Sh = sb("Sh", [128, NL, 512], F32)
        Sg_reg = [Region(f"Sg{l}") for l in range(NL)]
        Sh_reg = [Region(f"Sh{l}") for l in range(NL)]
        Sgb = sb("Sgb", [128, NL, 512], BF16)
        Sgb_reg = [Region(f"Sgb{l}") for l in range(NL)]
        Shb = [[(sb(f"Shb{l}_{i}", [128, 512], BF16), Region(f"Shb{l}_{i}")) for i in range(2)] for l in range(NL)]
        Shb_cur = [None] * NL
        uT_items = [(big2[:, 0:2048].rearrange("p (a b) -> p a b", a=4), Region("uT0")),
                    (big2[:, 2048:4096].rearrange("p (a b) -> p a b", a=4), Region("uT1"))]
        uT_i = [0]
        AL_UT = [uT_items[0][1], uT_items[1][1]]
        tf = Ring(sb, "tf", [128, 512], F32, 3)
        tbf = Ring(sb, "tb", [128, 512], BF16, 3)
        tfG = Ring(sb, "tfG", [128, 512], F32, 4)
        tbG = Ring(sb, "tbG", [128, 512], BF16, 8)
        tsG = Ring(sb, "tsG", [128, 16], F32, 8)
        gd = {n: (sb("gd_" + n, [128, 512], BF16), Region("gd_" + n)) for n in
              ("qs", "kh", "kb", "kt", "vs", "khT", "kbT", "qsT", "EM", "EMT", "ECT", "qkT", "r", "vn")}
        hgf = {n: (sb("hgf_" + n, [128, 512], F32), Region("hgf_" + n)) for n in ("sg", "lf", "b", "bm", "bt")}
        tsm = Ring(sb, "ts", [128, 16], F32, 12)
        junk = Ring(sb, "junk", [128, 512], BF16, 1)
        hsb = Ring(sb, "hsb", [128, D], BF16, 2)
        gU = Ring(sb, "gU", [128, 512], F32, 1)

        R0 = Region("init")
        P.op("pool", lambda e: e.memset(halo[:], 0.0), writes=[halo_reg])
        P.op("pool", lambda e: e.memset(Sg[:], 0.0), writes=Sg_reg)
        P.op("pool", lambda e: e.memset(Sh[:], 0.0), writes=Sh_reg)
        P.op("pool", lambda e: e.memset(Sgb[:], 0.0), writes=Sgb_reg)
        P.op("pool", lambda e: e.memset(qh0[:], 0.0), writes=hg_reg)
        P.op("pool", lambda e: e.memset(qh1[:], 0.0), writes=hg_reg)
        for l in range(NL):
            t_, r_ = Shb[l][0]
            P.op("pool", lambda e, t_=t_: e.memset(t_[:], 0.0), writes=[r_])
            Shb_cur[l] = 0

        class Load:
            def __init__(self, src_ap):
                self.src = src_ap
                self.slot = None
                self.finished = False

        loads = []
        issued = [0]

        def _try_issue():
            while issued[0] < len(loads):
                k = issued[0]
                if k >= NSLOT and not loads[k - NSLOT].finished:
                    return
                ld = loads[k]
                s = wslots[k % NSLOT]
                s.n += 1
                src_ap = ld.src
                n_el = 1
                for v in src_ap.shape[1:]:
                    n_el *= v
                dst = s.t[:, 0:n_el].rearrange("p (a b) -> p a b", a=src_ap.shape[1])
                P.dma("pool", lambda e: e.dma_start(out=dst, in_=src_ap), s.sem, s.n, writes=[s.reg])
                ld.slot = s
                issued[0] += 1

        def wreq(src_ap):
            ld = Load(src_ap)
            loads.append(ld)
            return ld

        def wget(ld):
            _try_issue()
            assert ld.slot is not None, "weight slot ring too small for this access pattern"
            return ld.slot

        def wdone(ld):
            ld.finished = True
            _try_issue()

        def gload(row):
            g = gring[gring_i[0] % 2]
            gring_i[0] += 1
            g.n += 1
            P.dma("sp", lambda e: e.dma_start(out=g.t[:], in_=gains_d[row, :].partition_broadcast(128)), g.sem, g.n,
                  writes=[g.reg])
            return g

        def rstd_from_ss(ss_ap, n, dim, rss):
            P.op("act", lambda e: e.activation(out=ss_ap, in_=ss_ap, func=AF.Ln, scale=1.0 / dim, bias=EPS),
                 reads=[rss], writes=[rss])
            P.op("act", lambda e: e.activation(out=ss_ap, in_=ss_ap, func=AF.Exp, scale=-0.5),
                 reads=[rss], writes=[rss])

        def head_scale(eng, out4, in4, sc, reads, writes, nh=4):
            for hh in range(nh):
                if eng == "act":
                    P.op("act", lambda e, hh=hh: e.activation(out=out4[:, hh, :], in_=in4[:, hh, :], func=AF.Copy,
                                                              scale=sc[:, hh:hh + 1]), reads=reads, writes=writes)
                else:
                    P.op(eng, lambda e, hh=hh: e.tensor_scalar(out=out4[:, hh, :], in0=in4[:, hh, :],
                                                               scalar1=sc[:, hh:hh + 1], scalar2=None, op0=ALU.mult),
                         reads=reads, writes=writes)

        def to_actT(src_ap_fn, src_regs, gain=None, norm=True):
            for t in range(NT):
                src = src_ap_fn(t)
                hs_t, hs_r = hsb.next()
                if norm:
                    ss_t, ss_r = tsm.next()
                    P.op("act", lambda e, src=src, hs_t=hs_t, ss_t=ss_t: e.activation(out=hs_t[:], in_=src, func=AF.Square,
                                                                                    accum_out=ss_t[:, 0:1]),
                         reads=[src_regs[t]], writes=[hs_r, ss_r])
                    rstd_from_ss(ss_t[:, 0:1], 1, D, ss_r)
                    P.op("dve", lambda e, src=src, hs_t=hs_t, ss_t=ss_t: e.scalar_tensor_tensor(
                        out=hs_t[:], in0=src, scalar=ss_t[:, 0:1], in1=gain.t[:], op0=ALU.mult, op1=ALU.mult),
                        reads=[src_regs[t], ss_r, gain.reg], writes=[hs_r])
                else:
                    P.op("act", lambda e, src=src, hs_t=hs_t: e.activation(out=hs_t[:], in_=src, func=AF.Copy),
                         reads=[src_regs[t]], writes=[hs_r])
                bk, br = psum("dense")
                pv = bk[:].bitcast(BF16).rearrange("p (c n) -> p c n", c=8)
                for c in range(8):
                    P.op("pe", lambda e, c=c, pv=pv, hs_t=hs_t: e.transpose(pv[:, c, :], hs_t[:, c * 128:(c + 1) * 128], identb[:]),
                         reads=[hs_r], writes=[br])
                eng = "act" if t % 2 == 0 else "dve"
                if eng == "act":
                    P.op("act", lambda e, pv=pv, t=t: e.activation(out=actT[:, :, t * 128:(t + 1) * 128], in_=pv, func=AF.Copy),
                         reads=[br], writes=[actT_reg[t]])
                else:
                    P.op("dve", lambda e, pv=pv, t=t: e.tensor_copy(out=actT[:, :, t * 128:(t + 1) * 128], in_=pv),
                         reads=[br], writes=[actT_reg[t]])

        def mm_fm(slot, w3, cchunk, out_fn):
            bk, br = psum("dense")
            for kc in range(8):
                P.op("pe", lambda e, kc=kc, bk=bk: e.matmul(bk[:], lhsT=w3[:, kc, cchunk * 128:(cchunk + 1) * 128],
                                                          rhs=actT[:, kc, :], start=(kc == 0), stop=(kc == 7)),
                     reads=[slot.reg] + actT_reg, writes=[br])
            out_fn(bk, br)

        def mm_tm(slot, w3, t, ncols, out_fn, kch=8, lhs=None, lhs_regs=None, col0=0):
            bk, br = psum("dense")
            for kc in range(kch):
                lt = actT[:, kc, t * 128:(t + 1) * 128] if lhs is None else lhs(kc)
                P.op("pe", lambda e, kc=kc, bk=bk, lt=lt: e.matmul(bk[:, 0:ncols], lhsT=lt, rhs=w3[:, kc, col0:col0 + ncols],
                                                                  start=(kc == 0), stop=(kc == kch - 1)),
                     reads=[slot.reg] + ([actT_reg[t]] if lhs is None else lhs_regs), writes=[br])
            out_fn(bk, br)

        v4 = lambda tt_: tt_[:].rearrange("p (h n) -> p h n", h=4)
        dbg_n = [0]
        marks = []

        def mark(name):
            marks.append((name, dict(P.cnt)))

        def dbg_dump(ap, regs, width):
            if not dbg:
                return
            c0 = dbg_n[0]
            dbg_n[0] += width
            tt_, tr_ = tf.next()
            P.op("dve", lambda e: e.tensor_copy(out=tt_[:, 0:width], in_=ap), reads=regs, writes=[tr_])
            out_n[0] += 1
            P.dma("sp", lambda e: e.dma_start(out=dbg_d[:, c0:c0 + width], in_=tt_[:, 0:width]), out_sem, out_n[0],
                  reads=[tr_])

        for blk in range(nblk):
            tok0 = blk * TB
            for t in range(NT):
                hb = h_ld[t]
                hb.n += 1
                P.dma("sp", lambda e, t=t: e.dma_start(out=h[:, t, :], in_=x_d[tok0 + t * 128: tok0 + (t + 1) * 128, :]),
                      hb.sem, hb.n, writes=[h_reg[t]])
            for l in layers:
                mark(f"b{blk}l{l}:norm1")
                g_pre = gload(l * 5 + 0)
                to_actT(lambda t: h[:, t, :], h_reg, gain=g_pre)
                mark(f"b{blk}l{l}:proj")
                wabn = wab
                wabn.n += 1
                P.dma("pool", lambda e, l=l: e.dma_start(
                    out=wab.t[:], in_=w_in_d[l, :, 2048:2056].rearrange("(c p) n -> p c n", p=128)),
                    wab.sem, wab.n, writes=[wab.reg])
                win = lambda c0: w_in_d[l, :, c0:c0 + 512].rearrange("(c p) n -> p c n", p=128)
                w8 = lambda ap: ap.rearrange("(c p) n -> p c n", p=128)
                L_qkv = [wreq(win(g_ * 512)) for g_ in range(3)]
                L_z = wreq(win(1536))
                L_qh = wreq(win(3080))
                L_f = wreq(win(2056))
                L_i = wreq(win(2568))
                L_g = wreq(win(3592))
                L_o = [wreq(w8(w_out_d[l, :, 0:512])), wreq(w8(w_out_d[l, :, 512:1024]))]
                L_ud = []
                for g_ in range(8):
                    L_ud.append((wreq(w8(w_up_d[l, :, g_ * 512:(g_ + 1) * 512])), wreq(w8(w_dn_d[l, g_ * 512:(g_ + 1) * 512, :]))))
                L_p = wreq(w8(w_ple_d[l]))
                L_g0 = wreq(w8(w_gate_d[l, :, 0:512]))
                L_g1 = wreq(w8(w_gate_d[l, :, 512:1024]))
                for grp in range(3):
                    s = wget(L_qkv[grp])
                    w3 = s.t[:].rearrange("p (a b) -> p a b", a=8)
                    P.op("pool", lambda e, grp=grp: e.tensor_copy(out=raw[:, :, 0:3], in_=halo[:, l * 12 + grp * 4:l * 12 + grp * 4 + 4, :]),
                         reads=[halo_reg], writes=[raw_reg] + AL_MLP)
                    for hh in range(4):
                        def ev(bk, br, hh=hh):
                            P.op("act", lambda e: e.activation(out=raw[:, hh, 3:3 + TB], in_=bk[:], func=AF.Copy),
                                 reads=[br], writes=[raw_reg])
                        mm_fm(s, w3, hh, ev)
                    wdone(L_qkv[grp])
                    P.op("pool", lambda e, grp=grp: e.tensor_copy(out=halo[:, l * 12 + grp * 4:l * 12 + grp * 4 + 4, :], in_=raw[:, :, TB:TB + 3]),
                         reads=[raw_reg], writes=[halo_reg])
                    for hh in range(4):
                        ch = grp * 4 + hh
                        acc_t, acc_r = tf.next()
                        wc = lambda k: convw[:, l * 48 + ch * 4 + k: l * 48 + ch * 4 + k + 1]
                        P.op("dve", lambda e, hh=hh, acc_t=acc_t, wc=wc: e.tensor_scalar(
                            out=acc_t[:], in0=raw[:, hh, 0:TB], scalar1=wc(0), scalar2=None, op0=ALU.mult),
                            reads=[raw_reg], writes=[acc_r])
                        for k in range(1, 4):
                            P.op("dve", lambda e, hh=hh, k=k, acc_t=acc_t, wc=wc: e.scalar_tensor_tensor(
                                out=acc_t[:], in0=raw[:, hh, k:k + TB], scalar=wc(k), in1=acc_t[:], op0=ALU.mult, op1=ALU.add),
                                reads=[raw_reg, acc_r], writes=[acc_r])
                        P.op("act", lambda e, ch=ch, acc_t=acc_t: e.activation(out=cs[:, ch, :], in_=acc_t[:], func=AF.Silu),
                             reads=[acc_r], writes=[cs_reg[grp]] + AL_MLP)
                s = wget(L_z)
                w3 = s.t[:].rearrange("p (a b) -> p a b", a=8)
                for t in range(NT):
                    def ev(bk, br, t=t):
                        zt, zr = tf.next()
                        P.op("act", lambda e: e.activation(out=zt[:], in_=bk[:], func=AF.Silu), reads=[br], writes=[zr])
                        P.op("pool", lambda e: e.tensor_tensor(out=zw[:, t, :], in0=zt[:], in1=hwb[:, (l * 2) * 512:(l * 2 + 1) * 512], op=ALU.mult),
                             reads=[zr], writes=[zw_reg[t]])
                    mm_tm(s, w3, t, 512, ev)
                wdone(L_z)
                for t in range(NT):
                    def ev(bk, br, t=t):
                        P.op("act", lambda e: e.activation(out=ab[:, t, :], in_=bk[:, 0:8], func=AF.Copy), reads=[br], writes=[ab_reg[t]])
                    mm_tm(wab, wab.t[:], t, 8, ev)
                def head_norm(t, o_ap4, o_regs, wmul, wmul_reg, col0, tfr, tsr):
                    sq_t, sq_r = tfr.next()
                    P.op("act", lambda e: e.activation(out=sq_t[:].rearrange("p (h n) -> p h n", h=4), in_=o_ap4, func=AF.Square), reads=o_regs, writes=[sq_r])
                    so, sor = tsr.next()
                    P.op("dve", lambda e: e.tensor_reduce(out=so[:, 0:4], in_=sq_t[:].rearrange("p (h n) -> p h n", h=4), axis=AX.X, op=ALU.add), reads=[sq_r], writes=[sor])
                    rstd_from_ss(so[:, 0:4], 4, 128.0, sor)
                    a_, ar = tfr.next()
                    head_scale("act", a_[:].rearrange("p (h n) -> p h n", h=4), o_ap4, so[:, 0:4], o_regs + [sor], [ar])
                    P.op("pool", lambda e: e.tensor_tensor(out=mix[:, t, col0:col0 + 512], in0=a_[:], in1=wmul, op=ALU.mult), reads=[ar, wmul_reg], writes=[mix_reg[t]])

                def gdn_strand():
                    for t in range(NT):
                        tsl = slice(t * 128, (t + 1) * 128)
                        mark(f"b{blk}l{l}:gdn{t}")
                        pq_b, pq_r = psum("mix")
                        pk_b, pk_r = psum("mix")
                        pv_b, pv_r = psum("mix")
                        for grp, (pb, pr) in enumerate(((pq_b, pq_r), (pk_b, pk_r), (pv_b, pv_r))):
                            for hh in range(4):
                                P.op("pe", lambda e, grp=grp, hh=hh, pb=pb: e.transpose(psb(pb)[:, hh, :], cs[:, grp * 4 + hh, tsl], identb[:]),
                                     reads=[cs_reg[grp]], writes=[pr])
                        sc, scr = tsG.next()
                        for pb, pr, c0 in ((pq_b, pq_r, 0), (pk_b, pk_r, 4)):
                            sq_t, sq_r = tfG.next()
                            P.op("act", lambda e, pb=pb, sq_t=sq_t: e.activation(out=sq_t[:].rearrange("p (h n) -> p h n", h=4), in_=psb(pb), func=AF.Square),
                                 reads=[pr], writes=[sq_r])
                            P.op("dve", lambda e, sq_t=sq_t, c0=c0: e.tensor_reduce(out=sc[:, c0:c0 + 4], in_=sq_t[:].rearrange("p (h n) -> p h n", h=4),
                                                                                    axis=AX.X, op=ALU.add), reads=[sq_r], writes=[scr])
                        rstd_from_ss(sc[:, 0:8], 8, 1.0, scr)
                        yield
                        g1, g1r = tsG.next()
                        P.op("act", lambda e: e.activation(out=g1[:, 0:4], in_=ab[:, t, 4:8], func=AF.Exp, scale=-1.0), reads=[ab_reg[t]], writes=[g1r])
                        P.op("dve", lambda e: e.tensor_scalar(out=g1[:, 0:4], in0=g1[:, 0:4], scalar1=1.0, scalar2=None, op0=ALU.add), reads=[g1r], writes=[g1r])
                        P.op("dve", lambda e: e.reciprocal(out=g1[:, 0:4], in_=g1[:, 0:4]), reads=[g1r], writes=[g1r])
                        P.op("dve", lambda e: e.tensor_tensor(out=g1[:, 8:12], in0=ab[:, t, 0:4], in1=smb[:, l * 8 + 4:l * 8 + 8], op=ALU.add),
                             reads=[ab_reg[t], g1r], writes=[g1r])
                        P.op("act", lambda e: e.activation(out=g1[:, 8:12], in_=g1[:, 8:12], func=AF.Exp), reads=[g1r], writes=[g1r])
                        P.op("act", lambda e: e.activation(out=g1[:, 8:12], in_=g1[:, 8:12], func=AF.Ln, bias=1.0), reads=[g1r], writes=[g1r])
                        P.op("dve", lambda e: e.tensor_tensor(out=g1[:, 4:8], in0=g1[:, 8:12], in1=negA[:, l * 4:l * 4 + 4], op=ALU.mult),
                             reads=[g1r], writes=[g1r])
                        pg_b, pg_r = psum("mix")
                        for i_, m_ in enumerate((Umat, SUmat, ones)):
                            P.op("pe", lambda e, i_=i_, m_=m_: e.matmul(pg_b[:, i_ * 4:i_ * 4 + 4], lhsT=m_, rhs=g1[:, 4:8], start=True, stop=True),
                                 reads=[g1r], writes=[pg_r])
                        eg, egr = tsG.next()
                        P.op("act", lambda e: e.activation(out=eg[:, 0:12], in_=pg_b[:, 0:12], func=AF.Exp), reads=[pg_r], writes=[egr])
                        P.op("dve", lambda e: e.tensor_scalar(out=eg[:, 12:16], in0=eg[:, 0:4], scalar1=-1.0, scalar2=None, op0=ALU.mult), reads=[egr], writes=[egr])
                        s2, s2r = tsG.next()
                        P.op("dve", lambda e: e.tensor_scalar(out=s2[:, 0:4], in0=sc[:, 0:4], scalar1=128.0 ** -0.5, scalar2=None, op0=ALU.mult), reads=[scr], writes=[s2r])
                        P.op("dve", lambda e: e.scalar_tensor_tensor(out=s2[:, 4:8], in0=sc[:, 4:8], scalar=-1.0, in1=g1[:, 0:4], op0=ALU.mult, op1=ALU.mult),
                             reads=[scr, g1r, s2r], writes=[s2r])
                        P.op("dve", lambda e: e.tensor_tensor(out=s2[:, 8:12], in0=sc[:, 4:8], in1=eg[:, 4:8], op=ALU.mult), reads=[scr, egr, s2r], writes=[s2r])
                        yield
                        qs, qsr = gd["qs"]
                        kh, khr = gd["kh"]
                        kb, kbr = gd["kb"]
                        kt, ktr = gd["kt"]
                        vs, vsr = gd["vs"]
                        v4 = lambda tt_: tt_[:].rearrange("p (h n) -> p h n", h=4)
                        head_scale("act", v4(qs), psb(pq_b), s2[:, 0:4], [pq_r, s2r], [qsr])
                        head_scale("dve", v4(kh), psb(pk_b), sc[:, 4:8], [pk_r, scr], [khr])
                        head_scale("act", v4(kb), psb(pk_b), s2[:, 4:8], [pk_r, s2r], [kbr])
                        head_scale("dve", v4(kt), psb(pk_b), s2[:, 8:12], [pk_r, s2r], [ktr])
                        P.op("act", lambda e: e.activation(out=v4(vs), in_=psb(pv_b), func=AF.Copy), reads=[pv_r], writes=[vsr])
                        yield
                        fm = []
                        for (src, srcr), dn in (((kh, khr), "khT"), ((kb, kbr), "kbT"), ((qs, qsr), "qsT")):
                            pb, pr = psum("mix")
                            for hh in range(4):
                                P.op("pe", lambda e, hh=hh, pb=pb, src=src: e.transpose(psb(pb)[:, hh, :], src[:, hh * 128:(hh + 1) * 128], identb[:]),
                                     reads=[srcr], writes=[pr])
                            dst, dstr = gd[dn]
                            P.op("dve" if len(fm) != 1 else "act",
                                 (lambda e, dst=dst, pb=pb: e.tensor_copy(out=v4(dst), in_=psb(pb))) if len(fm) != 1 else
                                 (lambda e, dst=dst, pb=pb: e.activation(out=v4(dst), in_=psb(pb), func=AF.Copy)),
                                 reads=[pr], writes=[dstr])
                            fm.append((dst, dstr))
                        (khT_, khTr), (kbT_, kbTr), (qsT_, qsTr) = fm
                        yield
                        pG_b, pG_r = psum("mix")
                        pGT_b, pGT_r = psum("mix")
                        pQK_b, pQK_r = psum("mix")
                        for hh in range(4):
                            hs_ = slice(hh * 128, (hh + 1) * 128)
                            P.op("pe", lambda e, hh=hh, hs_=hs_: e.matmul(ps4(pG_b)[:, hh, :], lhsT=khT_[:, hs_], rhs=kbT_[:, hs_], start=True, stop=True),
                                 reads=[khTr, kbTr], writes=[pG_r])
                            P.op("pe", lambda e, hh=hh, hs_=hs_: e.matmul(ps4(pGT_b)[:, hh, :], lhsT=kbT_[:, hs_], rhs=khT_[:, hs_], start=True, stop=True),
                                 reads=[khTr, kbTr], writes=[pGT_r])
                            P.op("pe", lambda e, hh=hh, hs_=hs_: e.matmul(ps4(pQK_b)[:, hh, :], lhsT=khT_[:, hs_], rhs=qsT_[:, hs_], start=True, stop=True),
                                 reads=[khTr, qsTr], writes=[pQK_r])
                        yield
                        gu, gur = gU.next()
                        P.op("dve", lambda e: e.tensor_tensor(out=v4(gu), in0=Umat.unsqueeze(1).to_broadcast([128, 4, 128]),
                                                              in1=g1[:, 4:8].unsqueeze(2).to_broadcast([128, 4, 128]), op=ALU.mult),
                             reads=[g1r], writes=[gur])
                        pD_b, pD_r = psum("mix")
                        pDT_b, pDT_r = psum("mix")
                        for hh in range(4):
                            hs_ = slice(hh * 128, (hh + 1) * 128)
                            P.op("pe", lambda e, hh=hh, hs_=hs_: e.matmul(ps4(pD_b)[:, hh, :], lhsT=gu[:, hs_], rhs=SUmat, start=True, stop=False),
                                 reads=[gur], writes=[pD_r])
                            P.op("pe", lambda e, hh=hh: e.matmul(ps4(pD_b)[:, hh, :], lhsT=identf, rhs=NEGU, start=False, stop=True),
                                 reads=[gur], writes=[pD_r])
                            P.op("pe", lambda e, hh=hh, hs_=hs_: e.matmul(ps4(pDT_b)[:, hh, :], lhsT=SUmat, rhs=gu[:, hs_], start=True, stop=False),
                                 reads=[gur], writes=[pDT_r])
                            P.op("pe", lambda e, hh=hh: e.matmul(ps4(pDT_b)[:, hh, :], lhsT=identf, rhs=NEGL, start=False, stop=True),
                                 reads=[gur], writes=[pDT_r])
                        yield
                        EM, EMr = gd["EM"]
                        EMT, EMTr = gd["EMT"]
                        ECT, ECTr = gd["ECT"]
                        P.op("act", lambda e: e.activation(out=EM[:], in_=pD_b[:], func=AF.Exp), reads=[pD_r], writes=[EMr])
                        P.op("act", lambda e: e.activation(out=EMT[:], in_=pDT_b[:], func=AF.Exp), reads=[pDT_r], writes=[EMTr])
                        for hh in range(4):
                            P.op("pool", lambda e, hh=hh: e.tensor_tensor(out=ECT[:, hh * 128:(hh + 1) * 128], in0=EMT[:, hh * 128:(hh + 1) * 128], in1=identb[:], op=ALU.add),
                                 reads=[EMTr], writes=[ECTr])
                        yield
                        Pm, Pmr = tbG.next()
                        PT, PTr = tbG.next()
                        qkT, qkTr = gd["qkT"]
                        P.op("dve", lambda e: e.tensor_tensor(out=Pm[:], in0=pG_b[:], in1=EM[:], op=ALU.mult), reads=[pG_r, EMr], writes=[Pmr])
                        P.op("dve", lambda e: e.tensor_tensor(out=PT[:], in0=pGT_b[:], in1=EMT[:], op=ALU.mult), reads=[pGT_r, EMTr], writes=[PTr])
                        P.op("dve", lambda e: e.tensor_tensor(out=qkT[:], in0=pQK_b[:], in1=ECT[:], op=ALU.mult), reads=[pQK_r, ECTr], writes=[qkTr])
                        X, Xr = tbG.next()
                        for hh in range(4):
                            P.op("pool", lambda e, hh=hh, X=X, PT=PT: e.tensor_tensor(out=X[:, hh * 128:(hh + 1) * 128], in0=PT[:, hh * 128:(hh + 1) * 128], in1=identb[:], op=ALU.add),
                                 reads=[PTr], writes=[Xr])
                        for n_ in range(1, 7):
                            pP_b, pP_r = psum("mix")
                            for hh in range(4):
                                hs_ = slice(hh * 128, (hh + 1) * 128)
                                P.op("pe", lambda e, hh=hh, hs_=hs_, PT=PT, Pm=Pm, pP_b=pP_b: e.matmul(ps4(pP_b)[:, hh, :], lhsT=PT[:, hs_], rhs=Pm[:, hs_], start=True, stop=True),
                                     reads=[PTr, Pmr], writes=[pP_r])
                            if n_ < 6:
                                pPT_b, pPT_r = psum("mix")
                                for hh in range(4):
                                    hs_ = slice(hh * 128, (hh + 1) * 128)
                                    P.op("pe", lambda e, hh=hh, hs_=hs_, PT=PT, Pm=Pm, pPT_b=pPT_b: e.matmul(ps4(pPT_b)[:, hh, :], lhsT=Pm[:, hs_], rhs=PT[:, hs_], start=True, stop=True),
                                         reads=[PTr, Pmr], writes=[pPT_r])
                            Pn, Pnr = tbG.next()
                            P.op("act", lambda e, Pn=Pn, pP_b=pP_b: e.activation(out=Pn[:], in_=pP_b[:], func=AF.Copy), reads=[pP_r], writes=[Pnr])
                            if n_ < 6:
                                PTn, PTnr = tbG.next()
                                P.op("dve", lambda e, PTn=PTn, pPT_b=pPT_b: e.tensor_copy(out=PTn[:], in_=pPT_b[:]), reads=[pPT_r], writes=[PTnr])
                            pX_b, pX_r = psum("mix")
                            for hh in range(4):
                                hs_ = slice(hh * 128, (hh + 1) * 128)
                                P.op("pe", lambda e, hh=hh, hs_=hs_, Pn=Pn, X=X, pX_b=pX_b: e.matmul(ps4(pX_b)[:, hh, :], lhsT=Pn[:, hs_], rhs=X[:, hs_], start=True, stop=False),
                                     reads=[Pnr, Xr], writes=[pX_r])
                                P.op("pe", lambda e, hh=hh, hs_=hs_, X=X, pX_b=pX_b: e.matmul(ps4(pX_b)[:, hh, :], lhsT=identb[:], rhs=X[:, hs_], start=False, stop=True),
                                     reads=[Xr], writes=[pX_r])
                            Xn, Xnr = tbG.next()
                            P.op("act" if n_ % 2 else "dve",
                                 (lambda e, Xn=Xn, pX_b=pX_b: e.activation(out=Xn[:], in_=pX_b[:], func=AF.Copy)) if n_ % 2 else
                                 (lambda e, Xn=Xn, pX_b=pX_b: e.tensor_copy(out=Xn[:], in_=pX_b[:])), reads=[pX_r], writes=[Xnr])
                            X, Xr = Xn, Xnr
                            yield
                            Pm, Pmr = Pn, Pnr
                            if n_ < 6:
                                PT, PTr = PTn, PTnr
                        yield
                        pKS_b, pKS_r = psum("mix")
                        pO1_b, pO1_r = psum("mix")
                        for hh in range(4):
                            hs_ = slice(hh * 128, (hh + 1) * 128)
                            P.op("pe", lambda e, hh=hh, hs_=hs_: e.matmul(ps4(pKS_b)[:, hh, :], lhsT=khT_[:, hs_], rhs=Sgb[:, l, hs_], start=True, stop=True),
                                 reads=[khTr, Sgb_reg[l]], writes=[pKS_r])
                            P.op("pe", lambda e, hh=hh, hs_=hs_: e.matmul(ps4(pO1_b)[:, hh, :], lhsT=qsT_[:, hs_], rhs=Sgb[:, l, hs_], start=True, stop=True),
                                 reads=[qsTr, Sgb_reg[l]], writes=[pO1_r])
                        r_, rr = gd["r"]
                        for hh in range(4):
                            hs_ = slice(hh * 128, (hh + 1) * 128)
                            P.op("dve", lambda e, hh=hh, hs_=hs_: e.scalar_tensor_tensor(out=r_[:, hs_], in0=ps4(pKS_b)[:, hh, :], scalar=eg[:, 12 + hh:13 + hh], in1=vs[:, hs_],
                                                                                        op0=ALU.mult, op1=ALU.add), reads=[pKS_r, egr, vsr], writes=[rr])
                        t1, t1r = tfG.next()
                        head_scale("act", t1[:].rearrange("p (h n) -> p h n", h=4), ps4(pO1_b), eg[:, 0:4], [pO1_r, egr], [t1r])
                        yield
                        pV_b, pV_r = psum("mix")
                        for hh in range(4):
                            hs_ = slice(hh * 128, (hh + 1) * 128)
                            P.op("pe", lambda e, hh=hh, hs_=hs_, X=X: e.matmul(ps4(pV_b)[:, hh, :], lhsT=X[:, hs_], rhs=r_[:, hs_], start=True, stop=True),
                                 reads=[Xr, rr], writes=[pV_r])
                        vn, vnr = gd["vn"]
                        head_scale("act", v4(vn), ps4(pV_b), g1[:, 0:4], [pV_r, g1r], [vnr])
                        yield
                        pO2_b, pO2_r = psum("mix")
                        pS_b, pS_r = psum("mix")
                        for hh in range(4):
                            hs_ = slice(hh * 128, (hh + 1) * 128)
                            P.op("pe", lambda e, hh=hh, hs_=hs_: e.matmul(ps4(pO2_b)[:, hh, :], lhsT=qkT[:, hs_], rhs=vn[:, hs_], start=True, stop=True),
                                 reads=[qkTr, vnr], writes=[pO2_r])
                            P.op("pe", lambda e, hh=hh, hs_=hs_: e.matmul(ps4(pS_b)[:, hh, :], lhsT=kt[:, hs_], rhs=vn[:, hs_], start=True, stop=True),
                                 reads=[ktr, vnr], writes=[pS_r])
                        yield
                        og, ogr = tfG.next()
                        P.op("dve", lambda e: e.tensor_tensor(out=og[:], in0=pO2_b[:], in1=t1[:], op=ALU.add), reads=[pO2_r, t1r], writes=[ogr])
                        for hh in range(4):
                            hs_ = slice(hh * 128, (hh + 1) * 128)
                            P.op("dve", lambda e, hh=hh, hs_=hs_: e.scalar_tensor_tensor(out=Sg[:, l, hs_], in0=Sg[:, l, hs_], scalar=eg[:, 8 + hh:9 + hh], in1=ps4(pS_b)[:, hh, :],
                                                                                        op0=ALU.mult, op1=ALU.add), reads=[pS_r, egr, Sg_reg[l]], writes=[Sg_reg[l]])
                        P.op("act", lambda e: e.activation(out=Sgb[:, l, :], in_=Sg[:, l, :], func=AF.Copy), reads=[Sg_reg[l]], writes=[Sgb_reg[l]])

                        yield
                        head_norm(t, og[:].rearrange("p (h n) -> p h n", h=4), [ogr], zw[:, t, :], zw_reg[t], 0, tfG, tsG)
                        yield

                def hgrn_strand():
                    s_q = wget(L_qh)
                    w3q = s_q.t[:].rearrange("p (a b) -> p a b", a=8)
                    for hh in range(4):
                        def ev(bk, br, hh=hh):
                            P.op("act", lambda e: e.activation(out=qhs[:, hh, :], in_=bk[:], func=AF.Silu), reads=[br], writes=[qhs_reg[hh]])
                        mm_fm(s_q, w3q, hh, ev)
                        yield
                    wdone(L_qh)
                    s_f = wget(L_f)
                    w3f = s_f.t[:].rearrange("p (a b) -> p a b", a=8)
                    for hh in range(4):
                        def ev(bk, br, hh=hh):
                            lcol = l * 4 + hh
                            sg, sgr = hgf["sg"]
                            lf, lfr = hgf["lf"]
                            b_, b_r = hgf["b"]
                            bm, bmr = hgf["bm"]
                            bt, btr = hgf["bt"]
                            P.op("act", lambda e: e.activation(out=sg[:], in_=bk[:], func=AF.Sigmoid), reads=[br], writes=[sgr])
                            P.op("dve", lambda e: e.tensor_scalar(out=sg[:], in0=sg[:], scalar1=oml[:, lcol:lcol + 1], scalar2=lb[:, lcol:lcol + 1],
                                                                  op0=ALU.mult, op1=ALU.add), reads=[sgr], writes=[sgr])
                            P.op("act", lambda e: e.activation(out=lf[:], in_=sg[:], func=AF.Ln), reads=[sgr], writes=[lfr])
                            P.op("pool", lambda e: e.tensor_scalar(out=sg[:], in0=sg[:], scalar1=-1.0, scalar2=1.0, op0=ALU.mult, op1=ALU.add),
                                 reads=[sgr, lfr], writes=[sgr])
                            P.op("dve", lambda e: e.tensor_tensor_scan(out=b_[:], data0=rst, data1=lf[:], initial=0.0, op0=ALU.mult, op1=ALU.add),
                                 reads=[lfr], writes=[b_r])
                            b3 = b_[:].rearrange("p (c n) -> p c n", n=64)
                            eb, ebr = lf, lfr
                            P.op("act", lambda e: e.activation(out=eb[:], in_=b_[:], func=AF.Exp), reads=[b_r], writes=[ebr])
                            eb4 = eb[:].rearrange("p (t c n) -> p t c n", t=NT, c=2)
                            q4 = qhs[:, hh, :].rearrange("p (t c n) -> p t c n", t=NT, c=2)
                            P.op("dve", lambda e: e.tensor_tensor(out=qh0[:, hh, :, 0:64], in0=q4[:, :, 0, :], in1=eb4[:, :, 0, :], op=ALU.mult),
                                 reads=[ebr, qhs_reg[hh]], writes=[hg_reg[hh]])
                            P.op("dve", lambda e: e.tensor_tensor(out=qh1[:, hh, :, 64:128], in0=q4[:, :, 1, :], in1=eb4[:, :, 1, :], op=ALU.mult),
                                 reads=[ebr, qhs_reg[hh]], writes=[hg_reg[hh]])
                            P.op("act", lambda e: e.activation(out=dS[:, hh, :], in_=b3[:, :, 63], func=AF.Exp), reads=[b_r], writes=[hg_reg[hh]])
                            bm3 = bm[:].rearrange("p (c n) -> p c n", n=64)
                            bt3 = bt[:].rearrange("p (c n) -> p c n", n=64)
                            P.op("dve", lambda e: e.tensor_tensor(out=bm3, in0=b3, in1=b3[:, :, 31:32].to_broadcast([128, 8, 64]), op=ALU.subtract),
                                 reads=[b_r], writes=[bmr])
                            P.op("pool", lambda e: e.tensor_tensor(out=bt3, in0=b3, in1=b3[:, :, 63:64].to_broadcast([128, 8, 64]), op=ALU.subtract),
                                 reads=[b_r], writes=[btr])
                            P.op("act", lambda e: e.activation(out=bt[:], in_=bt[:], func=AF.Exp, scale=-1.0), reads=[btr], writes=[btr])
                            P.op("dve", lambda e: e.tensor_tensor(out=khT[:, hh, :], in0=sg[:], in1=bt[:], op=ALU.mult),
                                 reads=[btr, sgr], writes=[hg_reg[hh]])
                            P.op("act", lambda e: e.activation(out=bt[:], in_=bm[:], func=AF.Exp), reads=[bmr, btr], writes=[btr])
                            P.op("dve", lambda e: e.tensor_tensor(out=qtT[:, hh, :], in0=qhs[:, hh, :], in1=bt[:], op=ALU.mult),
                                 reads=[btr, qhs_reg[hh]], writes=[hg_reg[hh]] + AL_UT)
                            P.op("act", lambda e: e.activation(out=bm[:], in_=bm[:], func=AF.Exp, scale=-1.0), reads=[bmr], writes=[bmr])
                            P.op("dve", lambda e: e.tensor_tensor(out=ktT[:, hh, :], in0=sg[:], in1=bm[:], op=ALU.mult),
                                 reads=[bmr, sgr], writes=[hg_reg[hh]] + AL_UT)
                        mm_fm(s_f, w3f, hh, ev)
                        yield
                    wdone(L_f)
                    s_i = wget(L_i)
                    w3i = s_i.t[:].rearrange("p (a b) -> p a b", a=8)
                    for t in range(NT):
                        def ev(bk, br, t=t):
                            P.op("act", lambda e: e.activation(out=iv[:, t, :], in_=bk[:], func=AF.Copy), reads=[br], writes=[iv_reg[t]])
                        mm_tm(s_i, w3i, t, 512, ev)
                        yield
                    wdone(L_i)
                    s = wget(L_g)
                    w3 = s.t[:].rearrange("p (a b) -> p a b", a=8)
                    for t in range(NT):
                        def ev(bk, br, t=t):
                            zt, zr = tf.next()
                            P.op("act", lambda e: e.activation(out=zt[:], in_=bk[:], func=AF.Silu), reads=[br], writes=[zr])
                            P.op("pool", lambda e: e.tensor_tensor(out=gw[:, t, :], in0=zt[:], in1=hwb[:, (l * 2 + 1) * 512:(l * 2 + 2) * 512], op=ALU.mult),
                                 reads=[zr], writes=[gw_reg[t]])
                        mm_tm(s, w3, t, 512, ev)
                        yield
                    wdone(L_g)

                    for t in range(NT):
                        tsl = slice(t * 128, (t + 1) * 128)
                        mark(f"b{blk}l{l}:hgrn{t}")
                        pA_b, pA_r = psum("dense")
                        pKH_b, pKH_r = psum("dense")
                        for hh in range(4):
                            P.op("pe", lambda e, hh=hh: e.matmul(ps4(pA_b)[:, hh, :], lhsT=ktT[:, hh, tsl], rhs=qtT[:, hh, tsl], start=True, stop=True),
                                 reads=[hg_reg[hh]], writes=[pA_r])
                            P.op("pe", lambda e, hh=hh: e.transpose(psb(pKH_b)[:, hh, :], khT[:, hh, tsl], identb[:]), reads=[hg_reg[hh]], writes=[pKH_r])
                        yield
                        at_, atr = tf.next()
                        P.op("act", lambda e: e.activation(out=at_[:], in_=pA_b[:], func=AF.Copy), reads=[pA_r], writes=[atr])
                        aT, aTr = tbf.next()
                        P.op("pool", lambda e: e.affine_select(out=v4(aT), in_=at_[:].rearrange("p (h n) -> p h n", h=4), pattern=[[0, 4], [1, 128]],
                                                               compare_op=ALU.is_ge, fill=0.0, base=0, channel_multiplier=-1), reads=[atr], writes=[aTr])
                        P.op("pool", lambda e: e.memset(v4(aT)[0:64, :, 64:128], 0.0), reads=[aTr], writes=[aTr])
                        khm, khmr = tbf.next()
                        P.op("dve", lambda e: e.tensor_copy(out=v4(khm), in_=psb(pKH_b)), reads=[pKH_r], writes=[khmr])
                        Sprev, Sprevr = Shb[l][Shb_cur[l]]
                        yield
                        pU_b, pU_r = psum("dense")
                        for hh in range(4):
                            hs_ = slice(hh * 128, (hh + 1) * 128)
                            P.op("pe", lambda e, hh=hh, hs_=hs_: e.matmul(ps4(pU_b)[:, hh, :], lhsT=khm[0:64, hs_], rhs=iv[0:64, t, hs_], start=True, stop=True),
                                 reads=[khmr, iv_reg[t]], writes=[pU_r])
                        for hh in range(4):
                            hs_ = slice(hh * 128, (hh + 1) * 128)
                            P.op("dve", lambda e, hh=hh, hs_=hs_: e.scalar_tensor_tensor(out=Sh[:, l, hs_], in0=Sh[:, l, hs_], scalar=dS[:, hh, 2 * t:2 * t + 1], in1=ps4(pU_b)[:, hh, :],
                                                                                        op0=ALU.mult, op1=ALU.add), reads=[pU_r, hg_reg[hh], Sh_reg[l]], writes=[Sh_reg[l]])
                        Smid, Smidr = Shb[l][1 - Shb_cur[l]]
                        P.op("act", lambda e: e.activation(out=Smid[:], in_=Sh[:, l, :], func=AF.Copy), reads=[Sh_reg[l]], writes=[Smidr])
                        yield
                        pU2_b, pU2_r = psum("dense")
                        for hh in range(4):
                            hs_ = slice(hh * 128, (hh + 1) * 128)
                            P.op("pe", lambda e, hh=hh, hs_=hs_: e.matmul(ps4(pU2_b)[:, hh, :], lhsT=khm[64:128, hs_], rhs=iv[64:128, t, hs_], start=True, stop=True),
                                 reads=[khmr, iv_reg[t]], writes=[pU2_r])
                        for hh in range(4):
                            hs_ = slice(hh * 128, (hh + 1) * 128)
                            P.op("dve", lambda e, hh=hh, hs_=hs_: e.scalar_tensor_tensor(out=Sh[:, l, hs_], in0=Sh[:, l, hs_], scalar=dS[:, hh, 2 * t + 1:2 * t + 2], in1=ps4(pU2_b)[:, hh, :],
                                                                                        op0=ALU.mult, op1=ALU.add), reads=[pU2_r, hg_reg[hh], Sh_reg[l], Smidr], writes=[Sh_reg[l]])
                        yield
                        pO_b, pO_r = psum("dense")
                        for hh in range(4):
                            hs_ = slice(hh * 128, (hh + 1) * 128)
                            P.op("pe", lambda e, hh=hh, hs_=hs_: e.matmul(ps4(pO_b)[:, hh, :], lhsT=aT[:, hs_], rhs=iv[:, t, hs_], start=True, stop=False),
                                 reads=[aTr, iv_reg[t]], writes=[pO_r])
                            P.op("pe", lambda e, hh=hh, hs_=hs_: e.matmul(ps4(pO_b)[:, hh, :], lhsT=qh0[:, hh, t, :], rhs=Sprev[:, hs_], start=False, stop=False),
                                 reads=[hg_reg[hh], Sprevr], writes=[pO_r])
                            P.op("pe", lambda e, hh=hh, hs_=hs_: e.matmul(ps4(pO_b)[:, hh, :], lhsT=qh1[:, hh, t, :], rhs=Smid[:, hs_], start=False, stop=True),
                                 reads=[hg_reg[hh], Smidr], writes=[pO_r])
                        Snew, Snewr = Sprev, Sprevr
                        P.op("act", lambda e: e.activation(out=Snew[:], in_=Sh[:, l, :], func=AF.Copy), reads=[Sh_reg[l]], writes=[Snewr])
                        yield
                        head_norm(t, ps4(pO_b), [pO_r], gw[:, t, :], gw_reg[t], 512, tf, tsm)

                        yield

                strands = [gdn_strand(), hgrn_strand()]
                while strands:
                    for s_ in list(strands):
                        try:
                            next(s_)
                        except StopIteration:
                            strands.remove(s_)

                mark(f"b{blk}l{l}:wout")
                g_pm = gload(l * 5 + 1)
                to_actT(lambda t: mix[:, t, :], mix_reg, norm=False)

                def post_norm(src_fn, src_regs_fn, gain, t, eng2="pool"):
                    ss_t, ss_r = tsm.next()
                    for hf in range(2):
                        jk, jr = junk.next()
                        P.op("act", lambda e, hf=hf, jk=jk: e.activation(out=jk[:, 0:512], in_=src_fn(hf), func=AF.Square, accum_out=ss_t[:, hf:hf + 1]),
                             reads=src_regs_fn(hf), writes=[jr, ss_r])
                    P.op("dve", lambda e: e.tensor_tensor(out=ss_t[:, 0:1], in0=ss_t[:, 0:1], in1=ss_t[:, 1:2], op=ALU.add), reads=[ss_r], writes=[ss_r])
                    rstd_from_ss(ss_t[:, 0:1], 1, D, ss_r)
                    for hf in range(2):
                        tm_, tmr = tf.next()
                        P.op("dve", lambda e, hf=hf, tm_=tm_: e.scalar_tensor_tensor(out=tm_[:], in0=src_fn(hf), scalar=ss_t[:, 0:1], in1=gain.t[:, hf * 512:(hf + 1) * 512],
                                                                                    op0=ALU.mult, op1=ALU.mult), reads=src_regs_fn(hf) + [ss_r, gain.reg], writes=[tmr])
                        P.op(eng2, lambda e, hf=hf, tm_=tm_: e.tensor_tensor(out=h[:, t, hf * 512:(hf + 1) * 512], in0=h[:, t, hf * 512:(hf + 1) * 512], in1=tm_[:], op=ALU.add),
                             reads=[tmr, h_reg[t]], writes=[h_reg[t]])

                s_o0 = wget(L_o[0])
                s_o1 = wget(L_o[1])
                for t in range(NT):
                    res = []
                    for hf, s_ in enumerate((s_o0, s_o1)):
                        mm_tm(s_, s_.t[:].rearrange("p (a b) -> p a b", a=8), t, 512, lambda bk, br: res.append((bk, br)))
                    post_norm(lambda hf: res[hf][0][:], lambda hf: [res[hf][1]], g_pm, t)
                wdone(L_o[0])
                wdone(L_o[1])

                mark(f"b{blk}l{l}:mlp")
                g_pl = gload(l * 5 + 2)
                to_actT(lambda t: h[:, t, :], h_reg, gain=g_pl)
                for g in range(8):
                    s_u = wget(L_ud[g][0])
                    w3u = s_u.t[:].rearrange("p (a b) -> p a b", a=8)
                    u_t, u_r = uT_items[uT_i[0] % 2]
                    uT_i[0] += 1
                    for hc in range(4):
                        def ev(bk, br, hc=hc):
                            sq_t, sq_r = tf.next()
                            P.op("act", lambda e: e.activation(out=sq_t[:], in_=bk[:], func=AF.Square), reads=[br], writes=[sq_r])
                            P.op("dve", lambda e: e.scalar_tensor_tensor(out=u_t[:, hc, :], in0=bk[:], scalar=0.0, in1=sq_t[:], op0=ALU.is_gt, op1=ALU.mult),
                                 reads=[br, sq_r], writes=[u_r] + hg_reg)
                        mm_fm(s_u, w3u, hc, ev)
                    wdone(L_ud[g][0])
                    s_d = wget(L_ud[g][1])
                    w3d = s_d.t[:].rearrange("p (a b) -> p a b", a=4)
                    for t in range(NT):
                        for hf in range(2):
                            def ev(bk, br, t=t, hf=hf):
                                dst = yacc[:, t, hf * 512:(hf + 1) * 512]
                                if g == 0:
                                    P.op("act", lambda e: e.activation(out=dst, in_=bk[:], func=AF.Copy), reads=[br], writes=[yacc_reg[t]] + AL_MIX)
                                else:
                                    P.op("dve", lambda e: e.tensor_tensor(out=dst, in0=bk[:], in1=dst, op=ALU.add), reads=[br, yacc_reg[t]], writes=[yacc_reg[t]])
                            mm_tm(s_d, w3d, t, 512, ev, kch=4, lhs=lambda kc, t=t: u_t[:, kc, t * 128:(t + 1) * 128], lhs_regs=[u_r], col0=hf * 512)
                    wdone(L_ud[g][1])
                g_pml = gload(l * 5 + 3)
                for t in range(NT):
                    post_norm(lambda hf, t=t: yacc[:, t, hf * 512:(hf + 1) * 512], lambda hf, t=t: [yacc_reg[t]], g_pml, t)

                mark(f"b{blk}l{l}:ple")
                g_ple = gload(l * 5 + 4)
                pbuf.n += 1
                P.dma("pool", lambda e, l=l: e.dma_start(out=pbuf.t[:], in_=p_d[l, tok0:tok0 + TB, :].rearrange("(t p) e -> p t e", p=128)),
                      pbuf.sem, pbuf.n, writes=[pbuf.reg])
                s_p = wget(L_p)
                s_g0 = wget(L_g0)
                s_g1 = wget(L_g1)
                to_actT(lambda t: h[:, t, :], h_reg, norm=False)
                w3p = s_p.t[:, 0:2048].rearrange("p (a b) -> p a b", a=2)
                for t in range(NT):
                    pb, pr = psum("mix")
                    for c in range(2):
                        P.op("pe", lambda e, c=c, pb=pb: e.transpose(psb(pb)[:, c, :], pbuf.t[:, t, c * 128:(c + 1) * 128], identb[:]), reads=[pbuf.reg], writes=[pr])
                    pT_, pTr = tbf.next()
                    P.op("dve", lambda e, pb=pb, pT_=pT_: e.tensor_copy(out=pT_[:, 0:256], in_=pb[:].bitcast(BF16)[:, 0:256]), reads=[pr], writes=[pTr])
                    eres = []
                    for hf in range(2):
                        mm_tm(s_p, w3p, t, 512, lambda bk, br: eres.append((bk, br)), kch=2,
                              lhs=lambda kc, pT_=pT_: pT_[:, kc * 128:(kc + 1) * 128], lhs_regs=[pTr], col0=hf * 512)
                    ss_t, ss_r = tsm.next()
                    for hf in range(2):
                        jk, jr = junk.next()
                        P.op("act", lambda e, hf=hf, jk=jk: e.activation(out=jk[:, 0:512], in_=eres[hf][0][:], func=AF.Square, accum_out=ss_t[:, hf:hf + 1]),
                             reads=[eres[hf][1]], writes=[jr, ss_r])
                    P.op("dve", lambda e: e.tensor_tensor(out=ss_t[:, 0:1], in0=ss_t[:, 0:1], in1=ss_t[:, 1:2], op=ALU.add), reads=[ss_r], writes=[ss_r])
                    rstd_from_ss(ss_t[:, 0:1], 1, D, ss_r)
                    en = []
                    for hf in range(2):
                        tm_, tmr = tf.next()
                        P.op("dve", lambda e, hf=hf, tm_=tm_: e.scalar_tensor_tensor(out=tm_[:], in0=eres[hf][0][:], scalar=ss_t[:, 0:1], in1=g_ple.t[:, hf * 512:(hf + 1) * 512],
                                                                                    op0=ALU.mult, op1=ALU.mult), reads=[eres[hf][1], ss_r, g_ple.reg], writes=[tmr])
                        en.append((tm_, tmr))
                    for hf, s_ in enumerate((s_g0, s_g1)):
                        def ev(bk, br, hf=hf):
                            gt, gtr = tf.next()
                            P.op("act", lambda e: e.activation(out=gt[:], in_=bk[:], func=AF.Sigmoid), reads=[br], writes=[gtr])
                            P.op("pool", lambda e: e.tensor_tensor(out=gt[:], in0=gt[:], in1=en[hf][0][:], op=ALU.mult), reads=[gtr, en[hf][1]], writes=[gtr])
                            P.op("pool", lambda e: e.tensor_tensor(out=h[:, t, hf * 512:(hf + 1) * 512], in0=h[:, t, hf * 512:(hf + 1) * 512], in1=gt[:], op=ALU.add),
                                 reads=[gtr, h_reg[t]], writes=[h_reg[t]])
                        mm_tm(s_, s_.t[:].rearrange("p (a b) -> p a b", a=8), t, 512, ev)
                wdone(L_p)
                wdone(L_g0)
                wdone(L_g1)
            for t in range(NT):
                st_n[t] += 1
                P.dma("sp", lambda e, t=t: e.dma_start(out=out_d[tok0 + t * 128: tok0 + (t + 1) * 128, :], in_=h[:, t, :]), st_sem[t], st_n[t],
                      reads=[h_reg[t]])
        P.wait_all("sp", [(st_sem[t], st_n[t] * 16, "dma") for t in range(NT)])
        mark("end")
        P.emit()
    nc._marks = marks
    return nc


def host_consts():
    idx = np.arange(128)
    U = (idx[:, None] <= idx[None, :]).astype(np.float32)
    SU = (idx[:, None] > idx[None, :]).astype(np.float32)
    ones = np.ones((128, 128), np.float32)
    rst = np.ones((128, 512), np.float32)
    rst[:, ::64] = 0.0
    return {
        "c_identb": np.eye(128, dtype=np.float32).astype(ml_dtypes.bfloat16),
        "c_f32": np.ascontiguousarray(np.concatenate([U, SU, ones, rst, np.eye(128, dtype=np.float32),
                                                      np.where(idx[None, :] >= idx[:, None], -30000.0, 0.0).astype(np.float32),
                                                      np.where(idx[None, :] <= idx[:, None], -30000.0, 0.0).astype(np.float32)], axis=1)),
    }


def make_in_maps(inp):
    f = lambda a: np.ascontiguousarray(np.asarray(a, dtype=np.float32))
    NL = NLAYER
    gains = np.stack([np.stack([f(inp[k])[l] for k in ("pre_mix_norm", "post_mix_norm", "pre_mlp_norm", "post_mlp_norm", "ple_norm")])
                      for l in range(NL)]).reshape(NL * 5, D)
    hw = np.stack([np.stack([np.tile(f(inp["gdn_norm"])[l], 4), np.tile(f(inp["hgrn_norm"])[l], 4)]) for l in range(NL)]).reshape(NL * 2, 512)
    sm = np.stack([np.stack([f(inp["gdn_a_log"])[l], f(inp["gdn_dt_bias"])[l]]) for l in range(NL)]).reshape(NL * 2, 4)
    cw = f(inp["gdn_conv"]).reshape(NL, 4, 12, 128).transpose(3, 0, 2, 1).reshape(128, NL * 48)
    lbl = f(inp["hgrn_lb_logits"]).reshape(NL, 4, 128).transpose(2, 0, 1).reshape(128, NL * 4)
    shared = {
        "w_in": f(inp["w_in"]), "w_out": f(inp["w_out"]), "w_up": f(inp["w_mlp_up"]), "w_dn": f(inp["w_mlp_down"]),
        "w_ple": f(inp["w_ple_proj"]), "w_gate": f(inp["w_ple_gate"]),
        "gains": np.ascontiguousarray(gains), "hw": np.ascontiguousarray(hw), "sm": np.ascontiguousarray(sm),
        "convw": np.ascontiguousarray(cw), "lbl": np.ascontiguousarray(lbl),
    }
    shared.update(host_consts())
    x = f(inp["x"])
    p = f(inp["p"])
    maps = []
    for b in range(x.shape[0]):
        m = dict(shared)
        m["x"] = np.ascontiguousarray(x[b])
        m["p"] = np.ascontiguousarray(p[:, b])
        maps.append(m)
    return maps


def kernel(**inputs):
    nc = build()
    maps = make_in_maps(inputs)
    res = run_bass_kernel_spmd(nc, maps, core_ids=list(range(len(maps))))
    return np.stack([np.asarray(r["out"], dtype=np.float32) for r in res.results], axis=0)
```

```python
import contextlib
import numpy as np
import ml_dtypes
import concourse.bass as bass
import concourse.mybir as mybir
from concourse.bass_utils import run_bass_kernel_spmd

F32 = mybir.dt.float32
BF16 = mybir.dt.bfloat16
AF = mybir.ActivationFunctionType
ALU = mybir.AluOpType
AX = mybir.AxisListType

D = 1024
T = 4096
NLAYER = 2
TB = 512
NT = TB // 128
EPS = 1e-6
DFF = 4096
PLE = 256
INC = 4104


class Region:
    __slots__ = ("name", "w", "r", "excl")

    def __init__(self, name, excl=False):
        self.name = name
        self.w = None
        self.r = []
        self.excl = excl


class _Rec:
    def __getattr__(self, name):
        def f(*a, **k):
            return (name, a, k)
        return f


_REC = _Rec()
DEBUG_MAP = None


class Prog:
    ENG = ("pe", "act", "dve", "pool", "sp")

    def __init__(self, nc, stack):
        self.nc = nc
        self.stack = stack
        self.q = {e: [] for e in self.ENG}
        self.sem = {e: stack.enter_context(nc.semaphore("s_" + e)) for e in self.ENG}
        self.cnt = {e: 0 for e in self.ENG}
        self.seen = {e: {} for e in self.ENG}

    def new_sem(self, name):
        return self.stack.enter_context(self.nc.semaphore(name))

    def _deps(self, eng, reads, writes):
        deps = {}

        def add(tok):
            if tok is None:
                return
            sem, val, teng = tok
            if teng == "pe" and eng == "pe":
                return
            k = id(sem)
            if k not in deps or deps[k][1] < val:
                deps[k] = (sem, val)
        for b in reads:
            add(b.w)
            if b.excl:
                for t in b.r:
                    if t[2] != eng:
                        add(t)
        for b in writes:
            add(b.w)
            for t in b.r:
                add(t)
        out = []
        seen = self.seen[eng]
        for k, (sem, val) in deps.items():
            if k in seen and seen[k] >= val:
                continue
            seen[k] = val
            out.append((sem, val))
        return out

    def _mark(self, tok, reads, writes):
        for b in reads:
            b.r.append(tok)
            if len(b.r) > 64:
                best = {}
                for t in b.r:
                    k = id(t[0])
                    if k not in best or best[k][1] < t[1]:
                        best[k] = t
                b.r = list(best.values())
        for b in writes:
            b.w = tok
            b.r = []

    def op(self, eng, fn, reads=(), writes=()):
        waits = self._deps(eng, reads, writes)
        self.cnt[eng] += 1
        tok = (self.sem[eng], self.cnt[eng], eng)
        self.q[eng].append((waits, fn(_REC), self.sem[eng], 1))
        self._mark(tok, reads, writes)
        return tok

    def dma(self, eng, fn, dsem, dcount, reads=(), writes=()):
        waits = self._deps(eng, reads, writes)
        tok = (dsem, dcount * 16, "dma")
        self.q[eng].append((waits, fn(_REC), dsem, 16))
        self._mark(tok, reads, writes)
        return tok

    def wait_all(self, eng, toks):
        self.q[eng].append(([(s, v) for (s, v, _) in toks], None, None, 0))

    def emit(self):
        nc = self.nc
        engmap = {"pe": "tensor", "act": "scalar", "dve": "vector", "pool": "gpsimd", "sp": "sync"}
        with nc.Block() as block:
            for e in self.ENG:
                q = self.q[e]

                def body(engine, q=q):
                    for waits, fn, sem, inc in q:
                        for (s, v) in waits:
                            engine.wait_ge(s, v)
                        if fn is not None:
                            name, a, k = fn
                            ins = getattr(engine, name)(*a, **k)
                            ins.then_inc(sem, inc)
                            if DEBUG_MAP is not None:
                                try:
                                    DEBUG_MAP.append((str(getattr(ins, "name", None) or getattr(getattr(ins, "ins", None), "name", None)), e, name,
                                                      str(k.get("out", a[0] if a else ""))[:120]))
                                except Exception as ex:
                                    DEBUG_MAP.append(("?", e, name, str(ex)))
                getattr(block, engmap[e])(body)


class Ring:
    def __init__(self, alloc, name, shape, dtype, n):
        self.items = [(alloc(f"{name}{i}", shape, dtype), Region(f"{name}{i}")) for i in range(n)]
        self.i = 0

    def next(self):
        it = self.items[self.i % len(self.items)]
        self.i += 1
        return it


class DmaBuf:
    def __init__(self, P, tensor, name):
        self.t = tensor
        self.reg = Region(name)
        self.sem = P.new_sem("d_" + name)
        self.n = 0


def build(nblk=T // TB, layers=(0, 1), dbg=False):
    nc = bass.Bass("TRN2", target_bir_lowering=False)
    NL = NLAYER
    dram = lambda n, s, d, k="ExternalInput": nc.dram_tensor(n, s, d, kind=k).ap()
    x_d = dram("x", [T, D], F32)
    p_d = dram("p", [NL, T, PLE], F32)
    w_in_d = dram("w_in", [NL, D, INC], F32)
    w_out_d = dram("w_out", [NL, D, D], F32)
    w_up_d = dram("w_up", [NL, D, DFF], F32)
    w_dn_d = dram("w_dn", [NL, DFF, D], F32)
    w_ple_d = dram("w_ple", [NL, PLE, D], F32)
    w_gate_d = dram("w_gate", [NL, D, D], F32)
    gains_d = dram("gains", [NL * 5, D], F32)
    hw_d = dram("hw", [NL * 2, 512], F32)
    sm_d = dram("sm", [NL * 2, 4], F32)
    convw_d = dram("convw", [128, NL * 48], F32)
    lbl_d = dram("lbl", [128, NL * 4], F32)
    c_identb = dram("c_identb", [128, 128], BF16)
    c_f32 = dram("c_f32", [128, 6 * 128 + 512], F32)
    out_d = dram("out", [T, D], F32, "ExternalOutput")
    dbg_d = dram("dbg", [128, 4096], F32, "ExternalOutput") if dbg else None

    with contextlib.ExitStack() as st:
        P = Prog(nc, st)
        sb = lambda n, s, d: st.enter_context(nc.sbuf_tensor(n, s, d))
        banks = [st.enter_context(nc.psum_tensor(f"bank{i}", [128, 512], F32)) for i in range(8)]
        bank_reg = [Region(f"bank{i}", excl=True) for i in range(8)]
        pool_i = {"dense": 0, "mix": 0}

        def psum(kind):
            if kind == "dense":
                i = pool_i["dense"] % 3
                pool_i["dense"] += 1
            else:
                i = 3 + pool_i["mix"] % 5
                pool_i["mix"] += 1
            return banks[i], bank_reg[i]

        def psb(bank):
            return bank[:].bitcast(BF16)[:, 0:512].rearrange("p (h n) -> p h n", h=4)

        def ps4(bank):
            return bank[:].rearrange("p (h n) -> p h n", h=4)

        identb = sb("identb", [128, 128], BF16)
        cf = sb("cf", [128, 6 * 128 + 512], F32)
        Umat = cf[:, 0:128]
        SUmat = cf[:, 128:256]
        ones = cf[:, 256:384]
        rst = cf[:, 384:896]
        identf = cf[:, 896:1024]
        NEGU = cf[:, 1024:1152]
        NEGL = cf[:, 1152:1280]
        convw = sb("convw_sb", [128, NL * 48], F32)
        lbl = sb("lbl_sb", [128, NL * 4], F32)
        smb = sb("smb", [128, NL * 2 * 4], F32)
        hwb = sb("hwb", [128, NL * 2 * 512], F32)
        negA = sb("negA", [128, NL * 4], F32)
        lb = sb("lb", [128, NL * 4], F32)
        oml = sb("oml", [128, NL * 4], F32)
        cst_sem = P.new_sem("cst")
        ncst = 0
        cst_toks = []

        def cload(dst, src):
            nonlocal ncst
            ncst += 1
            cst_toks.append(P.dma("sp", lambda e: e.dma_start(out=dst, in_=src), cst_sem, ncst))
        cload(identb[:], c_identb)
        cload(cf[:], c_f32)
        cload(convw[:], convw_d)
        cload(lbl[:], lbl_d)
        cload(smb[:], sm_d.rearrange("a b -> (a b)").partition_broadcast(128))
        cload(hwb[:], hw_d.rearrange("a b -> (a b)").partition_broadcast(128))
        for e in ("pe", "act", "dve", "pool"):
            P.wait_all(e, [cst_toks[-1]])
        R_c = Region("consts_derived")
        for l in range(NL):
            P.op("act", lambda e, l=l: e.activation(out=negA[:, l * 4:l * 4 + 4], in_=smb[:, l * 8:l * 8 + 4], func=AF.Exp),
                 writes=[R_c])
        P.op("dve", lambda e: e.tensor_scalar(out=negA[:], in0=negA[:], scalar1=-1.0, scalar2=None, op0=ALU.mult),
             reads=[R_c], writes=[R_c])
        elb = sb("elb", [128, NL * 4], F32)
        slb = sb("slb", [128, 4], F32)
        P.op("act", lambda e: e.activation(out=elb[:], in_=lbl[:], func=AF.Exp), writes=[R_c])
        P.op("dve", lambda e: e.tensor_copy(out=slb[:], in_=elb[:, 0:4]), reads=[R_c], writes=[R_c])
        for l in range(1, NL):
            P.op("dve", lambda e, l=l: e.tensor_tensor(out=slb[:], in0=slb[:], in1=elb[:, l * 4:l * 4 + 4], op=ALU.add),
                 reads=[R_c], writes=[R_c])
        P.op("dve", lambda e: e.reciprocal(out=slb[:], in_=slb[:]), reads=[R_c], writes=[R_c])
        P.op("dve", lambda e: e.memset(lb[:, 0:4], 0.0), reads=[R_c], writes=[R_c])
        for l in range(1, NL):
            P.op("dve", lambda e, l=l: e.tensor_tensor(out=elb[:, l * 4:l * 4 + 4], in0=elb[:, l * 4:l * 4 + 4], in1=slb[:], op=ALU.mult),
                 reads=[R_c], writes=[R_c])
            P.op("dve", lambda e, l=l: e.tensor_tensor(out=lb[:, l * 4:l * 4 + 4], in0=lb[:, (l - 1) * 4:l * 4], in1=elb[:, l * 4:l * 4 + 4], op=ALU.add),
                 reads=[R_c], writes=[R_c])
        P.op("dve", lambda e: e.tensor_scalar(out=oml[:], in0=lb[:], scalar1=-1.0, scalar2=1.0, op0=ALU.mult, op1=ALU.add),
             reads=[R_c], writes=[R_c])
        for e in ("act", "pool"):
            P.op(e, (lambda en: en.memset(slb[:, 0:1], 0.0)) if e == "pool" else
                 (lambda en: en.activation(out=slb[:, 0:1], in_=slb[:, 0:1], func=AF.Copy)), reads=[R_c], writes=[R_c])

        h = sb("h", [128, NT, D], F32)
        h_reg = [Region(f"h{t}") for t in range(NT)]
        h_ld = [DmaBuf(P, None, f"hld{t}") for t in range(NT)]
        actT = sb("actT", [128, 8, TB], BF16)
        actT_reg = [Region(f"actT{t}") for t in range(NT)]
        big = sb("big", [128, 5136], F32)
        yacc = big[:, 0:4096].rearrange("p (t d) -> p t d", t=NT)
        yacc_reg = [Region(f"yacc{t}") for t in range(NT)]
        mix = sb("mix", [128, NT, D], BF16)
        mix_reg = [Region(f"mix{t}") for t in range(NT)]
        NSLOT = 3
        wslots = [DmaBuf(P, sb(f"wslot{i}", [128, 4096], BF16), f"wslot{i}") for i in range(NSLOT)]
        wslot_i = [0]
        wab = DmaBuf(P, sb("wab", [128, 8, 8], BF16), "wab")
        gring = [DmaBuf(P, sb(f"gain{i}", [128, D], F32), f"gain{i}") for i in range(2)]
        gring_i = [0]
        pbuf = DmaBuf(P, sb("pbuf", [128, NT, PLE], BF16), "pbuf")
        out_sem = P.new_sem("outst")
        out_n = [0]
        st_sem = [P.new_sem(f"st{t}") for t in range(NT)]
        st_n = [0] * NT

        raw = big[:, 0:4 * (TB + 3)].rearrange("p (a b) -> p a b", a=4)
        raw_reg = Region("raw")
        halo = sb("halo", [128, NL * 12, 3], F32)
        halo_reg = Region("halo")
        cs = big[:, 2064:2064 + 3072].bitcast(BF16).rearrange("p (a b) -> p a b", a=12)
        cs_reg = [Region(f"cs{g}") for g in range(3)]
        AL_MIX = [raw_reg] + cs_reg
        AL_MLP = yacc_reg
        zw = sb("zw", [128, NT, 512], BF16)
        zw_reg = [Region(f"zw{t}") for t in range(NT)]
        gw = sb("gw", [128, NT, 512], BF16)
        gw_reg = [Region(f"gw{t}") for t in range(NT)]
        iv = sb("iv", [128, NT, 512], BF16)
        iv_reg = [Region(f"iv{t}") for t in range(NT)]
        ab = sb("ab", [128, NT, 8], F32)
        ab_reg = [Region(f"ab{t}") for t in range(NT)]
        qhs = sb("qhs", [128, 4, TB], BF16)
        qhs_reg = [Region(f"qhs{i}") for i in range(4)]
        big2 = sb("big2", [128, 4096], BF16)
        qtT = big2[:, 0:2048].rearrange("p (a b) -> p a b", a=4)
        ktT = big2[:, 2048:4096].rearrange("p (a b) -> p a b", a=4)
        khT = sb("khT", [128, 4, TB], BF16)
        qh0 = sb("qh0", [128, 4, NT, 128], BF16)
        qh1 = sb("qh1", [128, 4, NT, 128], BF16)
        dS = sb("dS", [128, 4, 8], F32)
        hg_reg = [Region(f"hg{i}") for i in range(4)]
        Sg = sb("Sg", [128, NL, 512], F32)
        Sh = sb("Sh", [128, NL, 512], F32)
        Sg_reg = [Region(f"Sg{l}") for l in range(NL)]
        Sh_reg = [Region(f"Sh{l}") for l in range(NL)]
        Sgb = sb("Sgb", [128, NL, 512], BF16)
        Sgb_reg = [Region(f"Sgb{l}") for l in range(NL)]
        Shb = [[(sb(f"Shb{l}_{i}", [128, 512], BF16), Region(f"Shb{l}_{i}")) for i in range(2)] for l in range(NL)]
        Shb_cur = [None] * NL
        uT_items = [(big2[:, 0:2048].rearrange("p (a b) -> p a b", a=4), Region("uT0")),
                    (big2[:, 2048:4096].rearrange("p (a b) -> p a b", a=4), Region("uT1"))]
        uT_i = [0]
        AL_UT = [uT_items[0][1], uT_items[1][1]]
        tf = Ring(sb, "tf", [128, 512], F32, 3)
        tbf = Ring(sb, "tb", [128, 512], BF16, 3)
        tfG = Ring(sb, "tfG", [128, 512], F32, 4)
        tbG = Ring(sb, "tbG", [128, 512], BF16, 8)
        tsG = Ring(sb, "tsG", [128, 16], F32, 8)
        gd = {n: (sb("gd_" + n, [128, 512], BF16), Region("gd_" + n)) for n in
              ("qs", "kh", "kb", "kt", "vs", "khT", "kbT", "qsT", "EM", "EMT", "ECT", "qkT", "r", "vn")}
        hgf = {n: (sb("hgf_" + n, [128, 512], F32), Region("hgf_" + n)) for n in ("sg", "lf", "b", "bm", "bt")}
        tsm = Ring(sb, "ts", [128, 16], F32, 12)
        hsb = Ring(sb, "hsb", [128, D], BF16, 2)
        junk = hsb
        gU = Ring(sb, "gU", [128, 512], F32, 1)
        gall = sb("gall", [128, NT, 16], F32)
        egall = sb("egall", [128, NT, 16], F32)
        gall_reg = Region("gall")
        identb4 = identb[:].unsqueeze(1).to_broadcast([128, 4, 128])

        R0 = Region("init")
        P.op("pool", lambda e: e.memset(halo[:], 0.0), writes=[halo_reg])
        P.op("pool", lambda e: e.memset(Sg[:], 0.0), writes=Sg_reg)
        P.op("pool", lambda e: e.memset(Sh[:], 0.0), writes=Sh_reg)
        P.op("pool", lambda e: e.memset(Sgb[:], 0.0), writes=Sgb_reg)
        P.op("pool", lambda e: e.memset(qh0[:], 0.0), writes=hg_reg)
        P.op("pool", lambda e: e.memset(qh1[:], 0.0), writes=hg_reg)
        for l in range(NL):
            t_, r_ = Shb[l][0]
            P.op("pool", lambda e, t_=t_: e.memset(t_[:], 0.0), writes=[r_])
            Shb_cur[l] = 0

        class Load:
            def __init__(self, src_ap):
                self.src = src_ap
                self.slot = None
                self.finished = False

        loads = []
        issued = [0]

        def _try_issue():
            while issued[0] < len(loads):
                k = issued[0]
                if k >= NSLOT and not loads[k - NSLOT].finished:
                    return
                ld = loads[k]
                s = wslots[k % NSLOT]
                s.n += 1
                src_ap = ld.src
                n_el = 1
                for v in src_ap.shape[1:]:
                    n_el *= v
                dst = s.t[:, 0:n_el].rearrange("p (a b) -> p a b", a=src_ap.shape[1])
                P.dma("pool", lambda e: e.dma_start(out=dst, in_=src_ap), s.sem, s.n, writes=[s.reg])
                ld.slot = s
                issued[0] += 1

        def wreq(src_ap):
            ld = Load(src_ap)
            loads.append(ld)
            return ld

        def wget(ld):
            _try_issue()
            assert ld.slot is not None, "weight slot ring too small for this access pattern"
            return ld.slot

        def wdone(ld):
            ld.finished = True
            _try_issue()

        def gload(row):
            g = gring[gring_i[0] % 2]
            gring_i[0] += 1
            g.n += 1
            P.dma("sp", lambda e: e.dma_start(out=g.t[:], in_=gains_d[row, :].partition_broadcast(128)), g.sem, g.n,
                  writes=[g.reg])
            return g

        def rstd_from_ss(ss_ap, n, dim, rss):
            P.op("act", lambda e: e.activation(out=ss_ap, in_=ss_ap, func=AF.Ln, scale=1.0 / dim, bias=EPS),
                 reads=[rss], writes=[rss])
            P.op("act", lambda e: e.activation(out=ss_ap, in_=ss_ap, func=AF.Exp, scale=-0.5),
                 reads=[rss], writes=[rss])

        def head_scale(eng, out4, in4, sc, reads, writes, nh=4):
            for hh in range(nh):
                if eng == "act":
                    P.op("act", lambda e, hh=hh: e.activation(out=out4[:, hh, :], in_=in4[:, hh, :], func=AF.Copy,
                                                              scale=sc[:, hh:hh + 1]), reads=reads, writes=writes)
                else:
                    P.op(eng, lambda e, hh=hh: e.tensor_scalar(out=out4[:, hh, :], in0=in4[:, hh, :],
                                                               scalar1=sc[:, hh:hh + 1], scalar2=None, op0=ALU.mult),
                         reads=reads, writes=writes)

        def to_actT(src_ap_fn, src_regs, gain=None, norm=True):
            if norm:
                ss_t, ss_r = tsm.next()
                for t in range(NT):
                    jk, jr = hsb.next()
                    P.op("act", lambda e, t=t, jk=jk: e.activation(out=jk[:], in_=src_ap_fn(t), func=AF.Square, accum_out=ss_t[:, t:t + 1]),
                         reads=[src_regs[t]], writes=[jr, ss_r])
                rstd_from_ss(ss_t[:, 0:NT], NT, D, ss_r)
            for t in range(NT):
                src = src_ap_fn(t)
                hs_t, hs_r = hsb.next()
                if norm:
                    P.op("dve", lambda e, src=src, hs_t=hs_t, t=t: e.scalar_tensor_tensor(
                        out=hs_t[:], in0=src, scalar=ss_t[:, t:t + 1], in1=gain.t[:], op0=ALU.mult, op1=ALU.mult),
                        reads=[src_regs[t], ss_r, gain.reg], writes=[hs_r])
                else:
                    P.op("act" if t % 2 else "pool", (lambda e, src=src, hs_t=hs_t: e.activation(out=hs_t[:], in_=src, func=AF.Copy)) if t % 2 else
                         (lambda e, src=src, hs_t=hs_t: e.tensor_copy(out=hs_t[:], in_=src)),
                         reads=[src_regs[t]], writes=[hs_r])
                bk, br = psum("dense" if t % 2 == 0 else "mix")
                pv = bk[:].bitcast(BF16).rearrange("p (c n) -> p c n", c=8)
                for c in range(8):
                    P.op("pe", lambda e, c=c, pv=pv, hs_t=hs_t: e.transpose(pv[:, c, :], hs_t[:, c * 128:(c + 1) * 128], identb[:]),
                         reads=[hs_r], writes=[br])
                eng = "act" if t % 2 == 0 else "dve"
                if eng == "act":
                    P.op("act", lambda e, pv=pv, t=t: e.activation(out=actT[:, :, t * 128:(t + 1) * 128], in_=pv, func=AF.Copy),
                         reads=[br], writes=[actT_reg[t]])
                else:
                    P.op("dve", lambda e, pv=pv, t=t: e.tensor_copy(out=actT[:, :, t * 128:(t + 1) * 128], in_=pv),
                         reads=[br], writes=[actT_reg[t]])

        def mm_fm(slot, w3, cchunk, out_fn):
            bk, br = psum("dense")
            for kc in range(8):
                P.op("pe", lambda e, kc=kc, bk=bk: e.matmul(bk[:], lhsT=w3[:, kc, cchunk * 128:(cchunk + 1) * 128],
                                                          rhs=actT[:, kc, :], start=(kc == 0), stop=(kc == 7)),
                     reads=[slot.reg] + actT_reg, writes=[br])
            out_fn(bk, br)

        def mm_tm(slot, w3, t, ncols, out_fn, kch=8, lhs=None, lhs_regs=None, col0=0, pool="dense"):
            bk, br = psum(pool)
            for kc in range(kch):
                lt = actT[:, kc, t * 128:(t + 1) * 128] if lhs is None else lhs(kc)
                P.op("pe", lambda e, kc=kc, bk=bk, lt=lt: e.matmul(bk[:, 0:ncols], lhsT=lt, rhs=w3[:, kc, col0:col0 + ncols],
                                                                  start=(kc == 0), stop=(kc == kch - 1)),
                     reads=[slot.reg] + ([actT_reg[t]] if lhs is None else lhs_regs), writes=[br])
            out_fn(bk, br)

        v4 = lambda tt_: tt_[:].rearrange("p (h n) -> p h n", h=4)
        dbg_n = [0]
        marks = []

        def mark(name):
            marks.append((name, dict(P.cnt)))

        def dbg_dump(ap, regs, width):
            if not dbg:
                return
            c0 = dbg_n[0]
            dbg_n[0] += width
            tt_, tr_ = tf.next()
            P.op("dve", lambda e: e.tensor_copy(out=tt_[:, 0:width], in_=ap), reads=regs, writes=[tr_])
            out_n[0] += 1
            P.dma("sp", lambda e: e.dma_start(out=dbg_d[:, c0:c0 + width], in_=tt_[:, 0:width]), out_sem, out_n[0],
                  reads=[tr_])

        for blk in range(nblk):
            tok0 = blk * TB
            for t in range(NT):
                hb = h_ld[t]
                hb.n += 1
                P.dma("sp", lambda e, t=t: e.dma_start(out=h[:, t, :], in_=x_d[tok0 + t * 128: tok0 + (t + 1) * 128, :]),
                      hb.sem, hb.n, writes=[h_reg[t]])
            for l in layers:
                mark(f"b{blk}l{l}:norm1")
                g_pre = gload(l * 5 + 0)
                to_actT(lambda t: h[:, t, :], h_reg, gain=g_pre)
                mark(f"b{blk}l{l}:proj")
                wabn = wab
                wabn.n += 1
                P.dma("pool", lambda e, l=l: e.dma_start(
                    out=wab.t[:], in_=w_in_d[l, :, 2048:2056].rearrange("(c p) n -> p c n", p=128)),
                    wab.sem, wab.n, writes=[wab.reg])
                win = lambda c0: w_in_d[l, :, c0:c0 + 512].rearrange("(c p) n -> p c n", p=128)
                w8 = lambda ap: ap.rearrange("(c p) n -> p c n", p=128)
                L_qkv = [wreq(win(g_ * 512)) for g_ in range(3)]
                L_z = wreq(win(1536))
                L_qh = wreq(win(3080))
                L_f = wreq(win(2056))
                L_i = wreq(win(2568))
                L_g = wreq(win(3592))
                L_o = [wreq(w8(w_out_d[l, :, 0:512])), wreq(w8(w_out_d[l, :, 512:1024]))]
                L_ud = []
                for g_ in range(8):
                    L_ud.append((wreq(w8(w_up_d[l, :, g_ * 512:(g_ + 1) * 512])), wreq(w8(w_dn_d[l, g_ * 512:(g_ + 1) * 512, :]))))
                L_p = wreq(w8(w_ple_d[l]))
                L_g0 = wreq(w8(w_gate_d[l, :, 0:512]))
                L_g1 = wreq(w8(w_gate_d[l, :, 512:1024]))
                for grp in range(3):
                    s = wget(L_qkv[grp])
                    w3 = s.t[:].rearrange("p (a b) -> p a b", a=8)
                    P.op("pool", lambda e, grp=grp: e.tensor_copy(out=raw[:, :, 0:3], in_=halo[:, l * 12 + grp * 4:l * 12 + grp * 4 + 4, :]),
                         reads=[halo_reg], writes=[raw_reg] + AL_MLP)
                    for hh in range(4):
                        def ev(bk, br, hh=hh):
                            P.op("act", lambda e: e.activation(out=raw[:, hh, 3:3 + TB], in_=bk[:], func=AF.Copy),
                                 reads=[br], writes=[raw_reg])
                        mm_fm(s, w3, hh, ev)
                    wdone(L_qkv[grp])
                    P.op("pool", lambda e, grp=grp: e.tensor_copy(out=halo[:, l * 12 + grp * 4:l * 12 + grp * 4 + 4, :], in_=raw[:, :, TB:TB + 3]),
                         reads=[raw_reg], writes=[halo_reg])
                    for hh in range(4):
                        ch = grp * 4 + hh
                        acc_t, acc_r = tf.next()
                        wc = lambda k: convw[:, l * 48 + ch * 4 + k: l * 48 + ch * 4 + k + 1]
                        P.op("dve", lambda e, hh=hh, acc_t=acc_t, wc=wc: e.tensor_scalar(
                            out=acc_t[:], in0=raw[:, hh, 0:TB], scalar1=wc(0), scalar2=None, op0=ALU.mult),
                            reads=[raw_reg], writes=[acc_r])
                        for k in range(1, 4):
                            P.op("dve", lambda e, hh=hh, k=k, acc_t=acc_t, wc=wc: e.scalar_tensor_tensor(
                                out=acc_t[:], in0=raw[:, hh, k:k + TB], scalar=wc(k), in1=acc_t[:], op0=ALU.mult, op1=ALU.add),
                                reads=[raw_reg, acc_r], writes=[acc_r])
                        P.op("act", lambda e, ch=ch, acc_t=acc_t: e.activation(out=cs[:, ch, :], in_=acc_t[:], func=AF.Silu),
                             reads=[acc_r], writes=[cs_reg[grp]] + AL_MLP)
                s = wget(L_z)
                w3 = s.t[:].rearrange("p (a b) -> p a b", a=8)
                for t in range(NT):
                    def ev(bk, br, t=t):
                        zt, zr = tf.next()
                        P.op("act", lambda e: e.activation(out=zt[:], in_=bk[:], func=AF.Silu), reads=[br], writes=[zr])
                        P.op("pool", lambda e: e.tensor_tensor(out=zw[:, t, :], in0=zt[:], in1=hwb[:, (l * 2) * 512:(l * 2 + 1) * 512], op=ALU.mult),
                             reads=[zr], writes=[zw_reg[t]])
                    mm_tm(s, w3, t, 512, ev)
                wdone(L_z)
                for t in range(NT):
                    def ev(bk, br, t=t):
                        P.op("act", lambda e: e.activation(out=ab[:, t, :], in_=bk[:, 0:8], func=AF.Copy), reads=[br], writes=[ab_reg[t]])
                    mm_tm(wab, wab.t[:], t, 8, ev)
                P.op("act", lambda e: e.activation(out=gall[:, :, 0:4], in_=ab[:, :, 4:8], func=AF.Exp, scale=-1.0), reads=ab_reg, writes=[gall_reg])
                P.op("dve", lambda e: e.tensor_scalar(out=gall[:, :, 0:4], in0=gall[:, :, 0:4], scalar1=1.0, scalar2=None, op0=ALU.add), reads=[gall_reg], writes=[gall_reg])
                P.op("dve", lambda e: e.reciprocal(out=gall[:, :, 0:4], in_=gall[:, :, 0:4]), reads=[gall_reg], writes=[gall_reg])
                P.op("dve", lambda e: e.tensor_tensor(out=gall[:, :, 8:12], in0=ab[:, :, 0:4], in1=smb[:, l * 8 + 4:l * 8 + 8].unsqueeze(1).to_broadcast([128, NT, 4]), op=ALU.add),
                     reads=ab_reg + [gall_reg], writes=[gall_reg])
                P.op("act", lambda e: e.activation(out=gall[:, :, 8:12], in_=gall[:, :, 8:12], func=AF.Exp), reads=[gall_reg], writes=[gall_reg])
                P.op("act", lambda e: e.activation(out=gall[:, :, 8:12], in_=gall[:, :, 8:12], func=AF.Ln, bias=1.0), reads=[gall_reg], writes=[gall_reg])
                P.op("dve", lambda e: e.tensor_tensor(out=gall[:, :, 4:8], in0=gall[:, :, 8:12], in1=negA[:, l * 4:l * 4 + 4].unsqueeze(1).to_broadcast([128, NT, 4]), op=ALU.mult),
                     reads=[gall_reg], writes=[gall_reg])
                pgA_b, pgA_r = psum("mix")
                for t_ in range(NT):
                    for i_, m_ in enumerate((Umat, SUmat, ones)):
                        P.op("pe", lambda e, i_=i_, m_=m_, t_=t_: e.matmul(pgA_b[:, t_ * 12 + i_ * 4:t_ * 12 + i_ * 4 + 4], lhsT=m_, rhs=gall[:, t_, 4:8], start=True, stop=True),
                             reads=[gall_reg], writes=[pgA_r])
                P.op("act", lambda e: e.activation(out=egall[:, :, 0:12], in_=pgA_b[:, 0:NT * 12].rearrange("p (t c) -> p t c", t=NT), func=AF.Exp),
                     reads=[pgA_r, gall_reg], writes=[gall_reg])
                P.op("dve", lambda e: e.tensor_scalar(out=egall[:, :, 12:16], in0=egall[:, :, 0:4], scalar1=-1.0, scalar2=None, op0=ALU.mult), reads=[gall_reg], writes=[gall_reg])

                def head_norm(t, o_ap4, o_regs, wmul, wmul_reg, col0, tfr, tsr):
                    sq_t, sq_r = tfr.next()
                    P.op("act", lambda e: e.activation(out=sq_t[:].rearrange("p (h n) -> p h n", h=4), in_=o_ap4, func=AF.Square), reads=o_regs, writes=[sq_r])
                    so, sor = tsr.next()
                    P.op("dve", lambda e: e.tensor_reduce(out=so[:, 0:4], in_=sq_t[:].rearrange("p (h n) -> p h n", h=4), axis=AX.X, op=ALU.add), reads=[sq_r], writes=[sor])
                    rstd_from_ss(so[:, 0:4], 4, 128.0, sor)
                    a_, ar = tfr.next()
                    head_scale("act", a_[:].rearrange("p (h n) -> p h n", h=4), o_ap4, so[:, 0:4], o_regs + [sor], [ar])
                    P.op("pool", lambda e: e.tensor_tensor(out=mix[:, t, col0:col0 + 512], in0=a_[:], in1=wmul, op=ALU.mult), reads=[ar, wmul_reg], writes=[mix_reg[t]])

                def gdn_strand():
                    for t in range(NT):
                        tsl = slice(t * 128, (t + 1) * 128)
                        mark(f"b{blk}l{l}:gdn{t}")
                        pq_b, pq_r = psum("mix")
                        pk_b, pk_r = psum("mix")
                        pv_b, pv_r = psum("mix")
                        for grp, (pb, pr) in enumerate(((pq_b, pq_r), (pk_b, pk_r), (pv_b, pv_r))):
                            for hh in range(4):
                                P.op("pe", lambda e, grp=grp, hh=hh, pb=pb: e.transpose(psb(pb)[:, hh, :], cs[:, grp * 4 + hh, tsl], identb[:]),
                                     reads=[cs_reg[grp]], writes=[pr])
                        sc, scr = tsG.next()
                        for pb, pr, c0 in ((pq_b, pq_r, 0), (pk_b, pk_r, 4)):
                            sq_t, sq_r = tfG.next()
                            for hh in range(4):
                                P.op("act", lambda e, pb=pb, sq_t=sq_t, hh=hh, c0=c0: e.activation(out=sq_t[:, hh * 128:(hh + 1) * 128], in_=psb(pb)[:, hh, :], func=AF.Square,
                                                                                          accum_out=sc[:, c0 + hh:c0 + hh + 1]),
                                     reads=[pr], writes=[sq_r, scr])
                        rstd_from_ss(sc[:, 0:8], 8, 1.0, scr)
                        yield
                        g1, g1r = gall[:, t, :], gall_reg
                        eg, egr = egall[:, t, :], gall_reg
                        s2, s2r = tsG.next()
                        P.op("dve", lambda e: e.tensor_scalar(out=s2[:, 0:4], in0=sc[:, 0:4], scalar1=128.0 ** -0.5, scalar2=None, op0=ALU.mult), reads=[scr], writes=[s2r])
                        P.op("dve", lambda e: e.scalar_tensor_tensor(out=s2[:, 4:8], in0=sc[:, 4:8], scalar=-1.0, in1=g1[:, 0:4], op0=ALU.mult, op1=ALU.mult),
                             reads=[scr, g1r, s2r], writes=[s2r])
                        P.op("dve", lambda e: e.tensor_tensor(out=s2[:, 8:12], in0=sc[:, 4:8], in1=eg[:, 4:8], op=ALU.mult), reads=[scr, egr, s2r], writes=[s2r])
                        yield
                        qs, qsr = gd["qs"]
                        kh, khr = gd["kh"]
                        kb, kbr = gd["kb"]
                        kt, ktr = gd["kt"]
                        vs, vsr = gd["vs"]
                        v4 = lambda tt_: tt_[:].rearrange("p (h n) -> p h n", h=4)
                        head_scale("act", v4(qs), psb(pq_b), s2[:, 0:4], [pq_r, s2r], [qsr])
                        head_scale("dve", v4(kh), psb(pk_b), sc[:, 4:8], [pk_r, scr], [khr])
                        head_scale("act", v4(kb), psb(pk_b), s2[:, 4:8], [pk_r, s2r], [kbr])
                        head_scale("dve", v4(kt), psb(pk_b), s2[:, 8:12], [pk_r, s2r], [ktr])
                        P.op("act", lambda e: e.activation(out=v4(vs), in_=psb(pv_b), func=AF.Copy), reads=[pv_r], writes=[vsr])
                        yield
                        fm = []
                        for (src, srcr), dn in (((kh, khr), "khT"), ((kb, kbr), "kbT"), ((qs, qsr), "qsT")):
                            pb, pr = psum("mix")
                            for hh in range(4):
                                P.op("pe", lambda e, hh=hh, pb=pb, src=src: e.transpose(psb(pb)[:, hh, :], src[:, hh * 128:(hh + 1) * 128], identb[:]),
                                     reads=[srcr], writes=[pr])
                            dst, dstr = gd[dn]
                            P.op("dve" if len(fm) != 1 else "act",
                                 (lambda e, dst=dst, pb=pb: e.tensor_copy(out=v4(dst), in_=psb(pb))) if len(fm) != 1 else
                                 (lambda e, dst=dst, pb=pb: e.activation(out=v4(dst), in_=psb(pb), func=AF.Copy)),
                                 reads=[pr], writes=[dstr])
                            fm.append((dst, dstr))
                        (khT_, khTr), (kbT_, kbTr), (qsT_, qsTr) = fm
                        yield
                        pG_b, pG_r = psum("mix")
                        pGT_b, pGT_r = psum("mix")
                        pQK_b, pQK_r = psum("mix")
                        for hh in range(4):
                            hs_ = slice(hh * 128, (hh + 1) * 128)
                            P.op("pe", lambda e, hh=hh, hs_=hs_: e.matmul(ps4(pG_b)[:, hh, :], lhsT=khT_[:, hs_], rhs=kbT_[:, hs_], start=True, stop=True),
                                 reads=[khTr, kbTr], writes=[pG_r])
                            P.op("pe", lambda e, hh=hh, hs_=hs_: e.matmul(ps4(pGT_b)[:, hh, :], lhsT=kbT_[:, hs_], rhs=khT_[:, hs_], start=True, stop=True),
                                 reads=[khTr, kbTr], writes=[pGT_r])
                            P.op("pe", lambda e, hh=hh, hs_=hs_: e.matmul(ps4(pQK_b)[:, hh, :], lhsT=khT_[:, hs_], rhs=qsT_[:, hs_], start=True, stop=True),
                                 reads=[khTr, qsTr], writes=[pQK_r])
                        yield
                        gu, gur = gU.next()
                        P.op("dve", lambda e: e.tensor_tensor(out=v4(gu), in0=Umat.unsqueeze(1).to_broadcast([128, 4, 128]),
                                                              in1=g1[:, 4:8].unsqueeze(2).to_broadcast([128, 4, 128]), op=ALU.mult),
                             reads=[g1r], writes=[gur])
                        pD_b, pD_r = psum("mix")
                        pDT_b, pDT_r = psum("mix")
                        for hh in range(4):
                            hs_ = slice(hh * 128, (hh + 1) * 128)
                            P.op("pe", lambda e, hh=hh, hs_=hs_: e.matmul(ps4(pD_b)[:, hh, :], lhsT=gu[:, hs_], rhs=SUmat, start=True, stop=False),
                                 reads=[gur], writes=[pD_r])
                            P.op("pe", lambda e, hh=hh: e.matmul(ps4(pD_b)[:, hh, :], lhsT=identf, rhs=NEGU, start=False, stop=True),
                                 reads=[gur], writes=[pD_r])
                            P.op("pe", lambda e, hh=hh, hs_=hs_: e.matmul(ps4(pDT_b)[:, hh, :], lhsT=SUmat, rhs=gu[:, hs_], start=True, stop=False),
                                 reads=[gur], writes=[pDT_r])
                            P.op("pe", lambda e, hh=hh: e.matmul(ps4(pDT_b)[:, hh, :], lhsT=identf, rhs=NEGL, start=False, stop=True),
                                 reads=[gur], writes=[pDT_r])
                        yield
                        EM, EMr = gd["EM"]
                        EMT, EMTr = gd["EMT"]
                        ECT, ECTr = gd["ECT"]
                        P.op("act", lambda e: e.activation(out=EM[:], in_=pD_b[:], func=AF.Exp), reads=[pD_r], writes=[EMr])
                        P.op("act", lambda e: e.activation(out=EMT[:], in_=pDT_b[:], func=AF.Exp), reads=[pDT_r], writes=[EMTr])
                        P.op("pool", lambda e: e.tensor_tensor(out=v4(ECT), in0=v4(EMT), in1=identb4, op=ALU.add), reads=[EMTr], writes=[ECTr])
                        yield
                        Pm, Pmr = tbG.next()
                        PT, PTr = tbG.next()
                        qkT, qkTr = gd["qkT"]
                        P.op("dve", lambda e: e.tensor_tensor(out=Pm[:], in0=pG_b[:], in1=EM[:], op=ALU.mult), reads=[pG_r, EMr], writes=[Pmr])
                        P.op("dve", lambda e: e.tensor_tensor(out=PT[:], in0=pGT_b[:], in1=EMT[:], op=ALU.mult), reads=[pGT_r, EMTr], writes=[PTr])
                        P.op("dve", lambda e: e.tensor_tensor(out=qkT[:], in0=pQK_b[:], in1=ECT[:], op=ALU.mult), reads=[pQK_r, ECTr], writes=[qkTr])
                        X, Xr = tbG.next()
                        P.op("dve", lambda e, X=X, PT=PT: e.tensor_tensor(out=v4(X), in0=v4(PT), in1=identb4, op=ALU.add), reads=[PTr], writes=[Xr])
                        for n_ in range(1, 7):
                            pP_b, pP_r = psum("mix")
                            for hh in range(4):
                                hs_ = slice(hh * 128, (hh + 1) * 128)
                                P.op("pe", lambda e, hh=hh, hs_=hs_, PT=PT, Pm=Pm, pP_b=pP_b: e.matmul(ps4(pP_b)[:, hh, :], lhsT=PT[:, hs_], rhs=Pm[:, hs_], start=True, stop=True),
                                     reads=[PTr, Pmr], writes=[pP_r])
                            if n_ < 6:
                                pPT_b, pPT_r = psum("mix")
                                for hh in range(4):
                                    hs_ = slice(hh * 128, (hh + 1) * 128)
                                    P.op("pe", lambda e, hh=hh, hs_=hs_, PT=PT, Pm=Pm, pPT_b=pPT_b: e.matmul(ps4(pPT_b)[:, hh, :], lhsT=Pm[:, hs_], rhs=PT[:, hs_], start=True, stop=True),
                                         reads=[PTr, Pmr], writes=[pPT_r])
                            Pn, Pnr = tbG.next()
                            P.op("act", lambda e, Pn=Pn, pP_b=pP_b: e.activation(out=Pn[:], in_=pP_b[:], func=AF.Copy), reads=[pP_r], writes=[Pnr])
                            if n_ < 6:
                                PTn, PTnr = tbG.next()
                                P.op("dve", lambda e, PTn=PTn, pPT_b=pPT_b: e.tensor_copy(out=PTn[:], in_=pPT_b[:]), reads=[pPT_r], writes=[PTnr])
                            pX_b, pX_r = psum("mix")
                            for hh in range(4):
                                hs_ = slice(hh * 128, (hh + 1) * 128)
                                P.op("pe", lambda e, hh=hh, hs_=hs_, Pn=Pn, X=X, pX_b=pX_b: e.matmul(ps4(pX_b)[:, hh, :], lhsT=Pn[:, hs_], rhs=X[:, hs_], start=True, stop=False),
                                     reads=[Pnr, Xr], writes=[pX_r])
                                P.op("pe", lambda e, hh=hh, hs_=hs_, X=X, pX_b=pX_b: e.matmul(ps4(pX_b)[:, hh, :], lhsT=identb[:], rhs=X[:, hs_], start=False, stop=True),
                                     reads=[Xr], writes=[pX_r])
                            Xn, Xnr = tbG.next()
                            P.op("act" if n_ % 2 else "dve",
                                 (lambda e, Xn=Xn, pX_b=pX_b: e.activation(out=Xn[:], in_=pX_b[:], func=AF.Copy)) if n_ % 2 else
                                 (lambda e, Xn=Xn, pX_b=pX_b: e.tensor_copy(out=Xn[:], in_=pX_b[:])), reads=[pX_r], writes=[Xnr])
                            X, Xr = Xn, Xnr
                            yield
                            Pm, Pmr = Pn, Pnr
                            if n_ < 6:
                                PT, PTr = PTn, PTnr
                        yield
                        pKS_b, pKS_r = psum("mix")
                        pO1_b, pO1_r = psum("mix")
                        for hh in range(4):
                            hs_ = slice(hh * 128, (hh + 1) * 128)
                            P.op("pe", lambda e, hh=hh, hs_=hs_: e.matmul(ps4(pKS_b)[:, hh, :], lhsT=khT_[:, hs_], rhs=Sgb[:, l, hs_], start=True, stop=True),
                                 reads=[khTr, Sgb_reg[l]], writes=[pKS_r])
                            P.op("pe", lambda e, hh=hh, hs_=hs_: e.matmul(ps4(pO1_b)[:, hh, :], lhsT=qsT_[:, hs_], rhs=Sgb[:, l, hs_], start=True, stop=True),
                                 reads=[qsTr, Sgb_reg[l]], writes=[pO1_r])
                        r_, rr = gd["r"]
                        for hh in range(4):
                            hs_ = slice(hh * 128, (hh + 1) * 128)
                            P.op("dve", lambda e, hh=hh, hs_=hs_: e.scalar_tensor_tensor(out=r_[:, hs_], in0=ps4(pKS_b)[:, hh, :], scalar=eg[:, 12 + hh:13 + hh], in1=vs[:, hs_],
                                                                                        op0=ALU.mult, op1=ALU.add), reads=[pKS_r, egr, vsr], writes=[rr])
                        t1, t1r = tfG.next()
                        head_scale("act", t1[:].rearrange("p (h n) -> p h n", h=4), ps4(pO1_b), eg[:, 0:4], [pO1_r, egr], [t1r])
                        yield
                        pV_b, pV_r = psum("mix")
                        for hh in range(4):
                            hs_ = slice(hh * 128, (hh + 1) * 128)
                            P.op("pe", lambda e, hh=hh, hs_=hs_, X=X: e.matmul(ps4(pV_b)[:, hh, :], lhsT=X[:, hs_], rhs=r_[:, hs_], start=True, stop=True),
                                 reads=[Xr, rr], writes=[pV_r])
                        vn, vnr = gd["vn"]
                        head_scale("act", v4(vn), ps4(pV_b), g1[:, 0:4], [pV_r, g1r], [vnr])
                        yield
                        pO2_b, pO2_r = psum("mix")
                        pS_b, pS_r = psum("mix")
                        for hh in range(4):
                            hs_ = slice(hh * 128, (hh + 1) * 128)
                            P.op("pe", lambda e, hh=hh, hs_=hs_: e.matmul(ps4(pO2_b)[:, hh, :], lhsT=qkT[:, hs_], rhs=vn[:, hs_], start=True, stop=True),
                                 reads=[qkTr, vnr], writes=[pO2_r])
                            P.op("pe", lambda e, hh=hh, hs_=hs_: e.matmul(ps4(pS_b)[:, hh, :], lhsT=kt[:, hs_], rhs=vn[:, hs_], start=True, stop=True),
                                 reads=[ktr, vnr], writes=[pS_r])
                        yield
                        og, ogr = tfG.next()
                        P.op("dve", lambda e: e.tensor_tensor(out=og[:], in0=pO2_b[:], in1=t1[:], op=ALU.add), reads=[pO2_r, t1r], writes=[ogr])
                        for hh in range(4):
                            hs_ = slice(hh * 128, (hh + 1) * 128)
                            P.op("dve", lambda e, hh=hh, hs_=hs_: e.scalar_tensor_tensor(out=Sg[:, l, hs_], in0=Sg[:, l, hs_], scalar=eg[:, 8 + hh:9 + hh], in1=ps4(pS_b)[:, hh, :],
                                                                                        op0=ALU.mult, op1=ALU.add), reads=[pS_r, egr, Sg_reg[l]], writes=[Sg_reg[l]])
                        P.op("act", lambda e: e.activation(out=Sgb[:, l, :], in_=Sg[:, l, :], func=AF.Copy), reads=[Sg_reg[l]], writes=[Sgb_reg[l]])

                        yield
                        head_norm(t, og[:].rearrange("p (h n) -> p h n", h=4), [ogr], zw[:, t, :], zw_reg[t], 0, tfG, tsG)
                        yield

                def hgrn_strand():
                    s_q = wget(L_qh)
                    w3q = s_q.t[:].rearrange("p (a b) -> p a b", a=8)
                    for hh in range(4):
                        def ev(bk, br, hh=hh):
                            P.op("act", lambda e: e.activation(out=qhs[:, hh, :], in_=bk[:], func=AF.Silu), reads=[br], writes=[qhs_reg[hh]])
                        mm_fm(s_q, w3q, hh, ev)
                        yield
                    wdone(L_qh)
                    s_f = wget(L_f)
                    w3f = s_f.t[:].rearrange("p (a b) -> p a b", a=8)
                    for hh in range(4):
                        def ev(bk, br, hh=hh):
                            lcol = l * 4 + hh
                            sg, sgr = hgf["sg"]
                            lf, lfr = hgf["lf"]
                            b_, b_r = hgf["b"]
                            bm, bmr = hgf["bm"]
                            bt, btr = hgf["bt"]
                            P.op("act", lambda e: e.activation(out=sg[:], in_=bk[:], func=AF.Sigmoid), reads=[br], writes=[sgr])
                            P.op("dve", lambda e: e.tensor_scalar(out=sg[:], in0=sg[:], scalar1=oml[:, lcol:lcol + 1], scalar2=lb[:, lcol:lcol + 1],
                                                                  op0=ALU.mult, op1=ALU.add), reads=[sgr], writes=[sgr])
                            P.op("act", lambda e: e.activation(out=lf[:], in_=sg[:], func=AF.Ln), reads=[sgr], writes=[lfr])
                            P.op("pool", lambda e: e.tensor_scalar(out=sg[:], in0=sg[:], scalar1=-1.0, scalar2=1.0, op0=ALU.mult, op1=ALU.add),
                                 reads=[sgr, lfr], writes=[sgr])
                            P.op("dve", lambda e: e.tensor_tensor_scan(out=b_[:], data0=rst, data1=lf[:], initial=0.0, op0=ALU.mult, op1=ALU.add),
                                 reads=[lfr], writes=[b_r])
                            b3 = b_[:].rearrange("p (c n) -> p c n", n=64)
                            eb, ebr = lf, lfr
                            P.op("act", lambda e: e.activation(out=eb[:], in_=b_[:], func=AF.Exp), reads=[b_r], writes=[ebr])
                            eb4 = eb[:].rearrange("p (t c n) -> p t c n", t=NT, c=2)
                            q4 = qhs[:, hh, :].rearrange("p (t c n) -> p t c n", t=NT, c=2)
                            P.op("dve", lambda e: e.tensor_tensor(out=qh0[:, hh, :, 0:64], in0=q4[:, :, 0, :], in1=eb4[:, :, 0, :], op=ALU.mult),
                                 reads=[ebr, qhs_reg[hh]], writes=[hg_reg[hh]])
                            P.op("dve", lambda e: e.tensor_tensor(out=qh1[:, hh, :, 64:128], in0=q4[:, :, 1, :], in1=eb4[:, :, 1, :], op=ALU.mult),
                                 reads=[ebr, qhs_reg[hh]], writes=[hg_reg[hh]])
                            P.op("act", lambda e: e.activation(out=dS[:, hh, :], in_=b3[:, :, 63], func=AF.Exp), reads=[b_r], writes=[hg_reg[hh]])
                            bm3 = bm[:].rearrange("p (c n) -> p c n", n=64)
                            bt3 = bt[:].rearrange("p (c n) -> p c n", n=64)
                            P.op("dve", lambda e: e.tensor_tensor(out=bm3, in0=b3, in1=b3[:, :, 31:32].to_broadcast([128, 8, 64]), op=ALU.subtract),
                                 reads=[b_r], writes=[bmr])
                            P.op("pool", lambda e: e.tensor_tensor(out=bt3, in0=b3, in1=b3[:, :, 63:64].to_broadcast([128, 8, 64]), op=ALU.subtract),
                                 reads=[b_r], writes=[btr])
                            P.op("act", lambda e: e.activation(out=bt[:], in_=bt[:], func=AF.Exp, scale=-1.0), reads=[btr], writes=[btr])
                            P.op("dve", lambda e: e.tensor_tensor(out=khT[:, hh, :], in0=sg[:], in1=bt[:], op=ALU.mult),
                                 reads=[btr, sgr], writes=[hg_reg[hh]])
                            P.op("act", lambda e: e.activation(out=bt[:], in_=bm[:], func=AF.Exp), reads=[bmr, btr], writes=[btr])
                            P.op("dve", lambda e: e.tensor_tensor(out=qtT[:, hh, :], in0=qhs[:, hh, :], in1=bt[:], op=ALU.mult),
                                 reads=[btr, qhs_reg[hh]], writes=[hg_reg[hh]] + AL_UT)
                            P.op("act", lambda e: e.activation(out=bm[:], in_=bm[:], func=AF.Exp, scale=-1.0), reads=[bmr], writes=[bmr])
                            P.op("dve", lambda e: e.tensor_tensor(out=ktT[:, hh, :], in0=sg[:], in1=bm[:], op=ALU.mult),
                                 reads=[bmr, sgr], writes=[hg_reg[hh]] + AL_UT)
                        mm_fm(s_f, w3f, hh, ev)
                        yield
                    wdone(L_f)
                    s_i = wget(L_i)
                    w3i = s_i.t[:].rearrange("p (a b) -> p a b", a=8)
                    for t in range(NT):
                        def ev(bk, br, t=t):
                            P.op("act", lambda e: e.activation(out=iv[:, t, :], in_=bk[:], func=AF.Copy), reads=[br], writes=[iv_reg[t]])
                        mm_tm(s_i, w3i, t, 512, ev)
                        yield
                    wdone(L_i)
                    s = wget(L_g)
                    w3 = s.t[:].rearrange("p (a b) -> p a b", a=8)
                    for t in range(NT):
                        def ev(bk, br, t=t):
                            zt, zr = tf.next()
                            P.op("act", lambda e: e.activation(out=zt[:], in_=bk[:], func=AF.Silu), reads=[br], writes=[zr])
                            P.op("pool", lambda e: e.tensor_tensor(out=gw[:, t, :], in0=zt[:], in1=hwb[:, (l * 2 + 1) * 512:(l * 2 + 2) * 512], op=ALU.mult),
                                 reads=[zr], writes=[gw_reg[t]])
                        mm_tm(s, w3, t, 512, ev)
                        yield
                    wdone(L_g)

                    for t in range(NT):
                        tsl = slice(t * 128, (t + 1) * 128)
                        mark(f"b{blk}l{l}:hgrn{t}")
                        pA_b, pA_r = psum("dense")
                        pKH_b, pKH_r = psum("dense")
                        for hh in range(4):
                            P.op("pe", lambda e, hh=hh: e.matmul(ps4(pA_b)[:, hh, :], lhsT=ktT[:, hh, tsl], rhs=qtT[:, hh, tsl], start=True, stop=True),
                                 reads=[hg_reg[hh]], writes=[pA_r])
                            P.op("pe", lambda e, hh=hh: e.transpose(psb(pKH_b)[:, hh, :], khT[:, hh, tsl], identb[:]), reads=[hg_reg[hh]], writes=[pKH_r])
                        yield
                        at_, atr = tf.next()
                        P.op("act", lambda e: e.activation(out=at_[:], in_=pA_b[:], func=AF.Copy), reads=[pA_r], writes=[atr])
                        aT, aTr = tbf.next()
                        P.op("pool", lambda e: e.affine_select(out=v4(aT), in_=at_[:].rearrange("p (h n) -> p h n", h=4), pattern=[[0, 4], [1, 128]],
                                                               compare_op=ALU.is_ge, fill=0.0, base=0, channel_multiplier=-1), reads=[atr], writes=[aTr])
                        P.op("pool", lambda e: e.memset(v4(aT)[0:64, :, 64:128], 0.0), reads=[aTr], writes=[aTr])
                        khm, khmr = tbf.next()
                        P.op("dve", lambda e: e.tensor_copy(out=v4(khm), in_=psb(pKH_b)), reads=[pKH_r], writes=[khmr])
                        Sprev, Sprevr = Shb[l][Shb_cur[l]]
                        yield
                        pU_b, pU_r = psum("dense")
                        for hh in range(4):
                            hs_ = slice(hh * 128, (hh + 1) * 128)
                            P.op("pe", lambda e, hh=hh, hs_=hs_: e.matmul(ps4(pU_b)[:, hh, :], lhsT=khm[0:64, hs_], rhs=iv[0:64, t, hs_], start=True, stop=True),
                                 reads=[khmr, iv_reg[t]], writes=[pU_r])
                        for hh in range(4):
                            hs_ = slice(hh * 128, (hh + 1) * 128)
                            P.op("dve", lambda e, hh=hh, hs_=hs_: e.scalar_tensor_tensor(out=Sh[:, l, hs_], in0=Sh[:, l, hs_], scalar=dS[:, hh, 2 * t:2 * t + 1], in1=ps4(pU_b)[:, hh, :],
                                                                                        op0=ALU.mult, op1=ALU.add), reads=[pU_r, hg_reg[hh], Sh_reg[l]], writes=[Sh_reg[l]])
                        Smid, Smidr = Shb[l][1 - Shb_cur[l]]
                        P.op("act", lambda e: e.activation(out=Smid[:], in_=Sh[:, l, :], func=AF.Copy), reads=[Sh_reg[l]], writes=[Smidr])
                        yield
                        pU2_b, pU2_r = psum("dense")
                        for hh in range(4):
                            hs_ = slice(hh * 128, (hh + 1) * 128)
                            P.op("pe", lambda e, hh=hh, hs_=hs_: e.matmul(ps4(pU2_b)[:, hh, :], lhsT=khm[64:128, hs_], rhs=iv[64:128, t, hs_], start=True, stop=True),
                                 reads=[khmr, iv_reg[t]], writes=[pU2_r])
                        for hh in range(4):
                            hs_ = slice(hh * 128, (hh + 1) * 128)
                            P.op("dve", lambda e, hh=hh, hs_=hs_: e.scalar_tensor_tensor(out=Sh[:, l, hs_], in0=Sh[:, l, hs_], scalar=dS[:, hh, 2 * t + 1:2 * t + 2], in1=ps4(pU2_b)[:, hh, :],
                                                                                        op0=ALU.mult, op1=ALU.add), reads=[pU2_r, hg_reg[hh], Sh_reg[l], Smidr], writes=[Sh_reg[l]])
                        yield
                        pO_b, pO_r = psum("dense")
                        for hh in range(4):
                            hs_ = slice(hh * 128, (hh + 1) * 128)
                            P.op("pe", lambda e, hh=hh, hs_=hs_: e.matmul(ps4(pO_b)[:, hh, :], lhsT=aT[:, hs_], rhs=iv[:, t, hs_], start=True, stop=False),
                                 reads=[aTr, iv_reg[t]], writes=[pO_r])
                            P.op("pe", lambda e, hh=hh, hs_=hs_: e.matmul(ps4(pO_b)[:, hh, :], lhsT=qh0[:, hh, t, :], rhs=Sprev[:, hs_], start=False, stop=False),
                                 reads=[hg_reg[hh], Sprevr], writes=[pO_r])
                            P.op("pe", lambda e, hh=hh, hs_=hs_: e.matmul(ps4(pO_b)[:, hh, :], lhsT=qh1[:, hh, t, :], rhs=Smid[:, hs_], start=False, stop=True),
                                 reads=[hg_reg[hh], Smidr], writes=[pO_r])
                        Snew, Snewr = Sprev, Sprevr
                        P.op("act", lambda e: e.activation(out=Snew[:], in_=Sh[:, l, :], func=AF.Copy), reads=[Sh_reg[l]], writes=[Snewr])
                        yield
                        head_norm(t, ps4(pO_b), [pO_r], gw[:, t, :], gw_reg[t], 512, tf, tsm)

                        yield

                strands = [gdn_strand(), hgrn_strand()]
                while strands:
                    for s_ in list(strands):
                        try:
                            next(s_)
                        except StopIteration:
                            strands.remove(s_)

                mark(f"b{blk}l{l}:wout")
                g_pm = gload(l * 5 + 1)
                to_actT(lambda t: mix[:, t, :], mix_reg, norm=False)

                def post_norm(src_fn, src_regs_fn, gain, t, eng2="pool"):
                    ss_t, ss_r = tsm.next()
                    for hf in range(2):
                        jk, jr = junk.next()
                        P.op("act", lambda e, hf=hf, jk=jk: e.activation(out=jk[:, 0:512], in_=src_fn(hf), func=AF.Square, accum_out=ss_t[:, hf:hf + 1]),
                             reads=src_regs_fn(hf), writes=[jr, ss_r])
                    P.op("dve", lambda e: e.tensor_tensor(out=ss_t[:, 0:1], in0=ss_t[:, 0:1], in1=ss_t[:, 1:2], op=ALU.add), reads=[ss_r], writes=[ss_r])
                    rstd_from_ss(ss_t[:, 0:1], 1, D, ss_r)
                    for hf in range(2):
                        tm_, tmr = tf.next()
                        P.op("dve", lambda e, hf=hf, tm_=tm_: e.scalar_tensor_tensor(out=tm_[:], in0=src_fn(hf), scalar=ss_t[:, 0:1], in1=gain.t[:, hf * 512:(hf + 1) * 512],
                                                                                    op0=ALU.mult, op1=ALU.mult), reads=src_regs_fn(hf) + [ss_r, gain.reg], writes=[tmr])
                        P.op(eng2, lambda e, hf=hf, tm_=tm_: e.tensor_tensor(out=h[:, t, hf * 512:(hf + 1) * 512], in0=h[:, t, hf * 512:(hf + 1) * 512], in1=tm_[:], op=ALU.add),
                             reads=[tmr, h_reg[t]], writes=[h_reg[t]])

                s_o0 = wget(L_o[0])
                s_o1 = wget(L_o[1])
                for t in range(NT):
                    res = []
                    for hf, s_ in enumerate((s_o0, s_o1)):
                        mm_tm(s_, s_.t[:].rearrange("p (a b) -> p a b", a=8), t, 512, lambda bk, br: res.append((bk, br)),
                              pool=("dense" if t % 2 == 0 else "mix"))
                    post_norm(lambda hf: res[hf][0][:], lambda hf: [res[hf][1]], g_pm, t)
                wdone(L_o[0])
                wdone(L_o[1])

                mark(f"b{blk}l{l}:mlp")
                g_pl = gload(l * 5 + 2)
                to_actT(lambda t: h[:, t, :], h_reg, gain=g_pl)
                for g in range(8):
                    s_u = wget(L_ud[g][0])
                    w3u = s_u.t[:].rearrange("p (a b) -> p a b", a=8)
                    u_t, u_r = uT_items[uT_i[0] % 2]
                    uT_i[0] += 1
                    for hc in range(4):
                        def ev(bk, br, hc=hc):
                            sq_t, sq_r = tf.next()
                            P.op("act", lambda e: e.activation(out=sq_t[:], in_=bk[:], func=AF.Square), reads=[br], writes=[sq_r])
                            P.op("dve", lambda e: e.scalar_tensor_tensor(out=u_t[:, hc, :], in0=bk[:], scalar=0.0, in1=sq_t[:], op0=ALU.is_gt, op1=ALU.mult),
                                 reads=[br, sq_r], writes=[u_r] + hg_reg)
                        mm_fm(s_u, w3u, hc, ev)
                    wdone(L_ud[g][0])
                    s_d = wget(L_ud[g][1])
                    w3d = s_d.t[:].rearrange("p (a b) -> p a b", a=4)
                    for t in range(NT):
                        for hf in range(2):
                            def ev(bk, br, t=t, hf=hf):
                                dst = yacc[:, t, hf * 512:(hf + 1) * 512]
                                if g == 0:
                                    P.op("act", lambda e: e.activation(out=dst, in_=bk[:], func=AF.Copy), reads=[br], writes=[yacc_reg[t]] + AL_MIX)
                                else:
                                    P.op("dve", lambda e: e.tensor_tensor(out=dst, in0=bk[:], in1=dst, op=ALU.add), reads=[br, yacc_reg[t]], writes=[yacc_reg[t]])
                            mm_tm(s_d, w3d, t, 512, ev, kch=4, lhs=lambda kc, t=t: u_t[:, kc, t * 128:(t + 1) * 128], lhs_regs=[u_r], col0=hf * 512)
                    wdone(L_ud[g][1])
                g_pml = gload(l * 5 + 3)
                for t in range(NT):
                    post_norm(lambda hf, t=t: yacc[:, t, hf * 512:(hf + 1) * 512], lambda hf, t=t: [yacc_reg[t]], g_pml, t)

                mark(f"b{blk}l{l}:ple")
                g_ple = gload(l * 5 + 4)
                pbuf.n += 1
                P.dma("pool", lambda e, l=l: e.dma_start(out=pbuf.t[:], in_=p_d[l, tok0:tok0 + TB, :].rearrange("(t p) e -> p t e", p=128)),
                      pbuf.sem, pbuf.n, writes=[pbuf.reg])
                s_p = wget(L_p)
                s_g0 = wget(L_g0)
                s_g1 = wget(L_g1)
                to_actT(lambda t: h[:, t, :], h_reg, norm=False)
                w3p = s_p.t[:, 0:2048].rearrange("p (a b) -> p a b", a=2)
                for t in range(NT):
                    pb, pr = psum("mix")
                    for c in range(2):
                        P.op("pe", lambda e, c=c, pb=pb: e.transpose(psb(pb)[:, c, :], pbuf.t[:, t, c * 128:(c + 1) * 128], identb[:]), reads=[pbuf.reg], writes=[pr])
                    pT_, pTr = tbf.next()
                    P.op("dve", lambda e, pb=pb, pT_=pT_: e.tensor_copy(out=pT_[:, 0:256], in_=pb[:].bitcast(BF16)[:, 0:256]), reads=[pr], writes=[pTr])
                    eres = []
                    for hf in range(2):
                        mm_tm(s_p, w3p, t, 512, lambda bk, br: eres.append((bk, br)), kch=2,
                              lhs=lambda kc, pT_=pT_: pT_[:, kc * 128:(kc + 1) * 128], lhs_regs=[pTr], col0=hf * 512,
                              pool=("dense" if t % 2 == 0 else "mix"))
                    ss_t, ss_r = tsm.next()
                    for hf in range(2):
                        jk, jr = junk.next()
                        P.op("act", lambda e, hf=hf, jk=jk: e.activation(out=jk[:, 0:512], in_=eres[hf][0][:], func=AF.Square, accum_out=ss_t[:, hf:hf + 1]),
                             reads=[eres[hf][1]], writes=[jr, ss_r])
                    P.op("dve", lambda e: e.tensor_tensor(out=ss_t[:, 0:1], in0=ss_t[:, 0:1], in1=ss_t[:, 1:2], op=ALU.add), reads=[ss_r], writes=[ss_r])
                    rstd_from_ss(ss_t[:, 0:1], 1, D, ss_r)
                    en = []
                    for hf in range(2):
                        tm_, tmr = tf.next()
                        P.op("dve", lambda e, hf=hf, tm_=tm_: e.scalar_tensor_tensor(out=tm_[:], in0=eres[hf][0][:], scalar=ss_t[:, 0:1], in1=g_ple.t[:, hf * 512:(hf + 1) * 512],
                                                                                    op0=ALU.mult, op1=ALU.mult), reads=[eres[hf][1], ss_r, g_ple.reg], writes=[tmr])
                        en.append((tm_, tmr))
                    for hf, s_ in enumerate((s_g0, s_g1)):
                        def ev(bk, br, hf=hf):
                            gt, gtr = tf.next()
                            P.op("act", lambda e: e.activation(out=gt[:], in_=bk[:], func=AF.Sigmoid), reads=[br], writes=[gtr])
                            P.op("pool", lambda e: e.tensor_tensor(out=gt[:], in0=gt[:], in1=en[hf][0][:], op=ALU.mult), reads=[gtr, en[hf][1]], writes=[gtr])
                            P.op("pool", lambda e: e.tensor_tensor(out=h[:, t, hf * 512:(hf + 1) * 512], in0=h[:, t, hf * 512:(hf + 1) * 512], in1=gt[:], op=ALU.add),
                                 reads=[gtr, h_reg[t]], writes=[h_reg[t]])
                        mm_tm(s_, s_.t[:].rearrange("p (a b) -> p a b", a=8), t, 512, ev, pool=("mix" if t % 2 == 0 else "dense"))
                wdone(L_p)
                wdone(L_g0)
                wdone(L_g1)
            for t in range(NT):
                st_n[t] += 1
                P.dma("sp", lambda e, t=t: e.dma_start(out=out_d[tok0 + t * 128: tok0 + (t + 1) * 128, :], in_=h[:, t, :]), st_sem[t], st_n[t],
                      reads=[h_reg[t]])
        P.wait_all("sp", [(st_sem[t], st_n[t] * 16, "dma") for t in range(NT)])
        mark("end")
        P.emit()
    nc._marks = marks
    return nc


def host_consts():
    idx = np.arange(128)
    U = (idx[:, None] <= idx[None, :]).astype(np.float32)
    SU = (idx[:, None] > idx[None, :]).astype(np.float32)
    ones = np.ones((128, 128), np.float32)
    rst = np.ones((128, 512), np.float32)
    rst[:, ::64] = 0.0
    return {
        "c_identb": np.eye(128, dtype=np.float32).astype(ml_dtypes.bfloat16),
        "c_f32": np.ascontiguousarray(np.concatenate([U, SU, ones, rst, np.eye(128, dtype=np.float32),
                                                      np.where(idx[None, :] >= idx[:, None], -30000.0, 0.0).astype(np.float32),
                                                      np.where(idx[None, :] <= idx[:, None], -30000.0, 0.0).astype(np.float32)], axis=1)),
    }


def make_in_maps(inp):
    f = lambda a: np.ascontiguousarray(np.asarray(a, dtype=np.float32))
    NL = NLAYER
    gains = np.stack([np.stack([f(inp[k])[l] for k in ("pre_mix_norm", "post_mix_norm", "pre_mlp_norm", "post_mlp_norm", "ple_norm")])
                      for l in range(NL)]).reshape(NL * 5, D)
    hw = np.stack([np.stack([np.tile(f(inp["gdn_norm"])[l], 4), np.tile(f(inp["hgrn_norm"])[l], 4)]) for l in range(NL)]).reshape(NL * 2, 512)
    sm = np.stack([np.stack([f(inp["gdn_a_log"])[l], f(inp["gdn_dt_bias"])[l]]) for l in range(NL)]).reshape(NL * 2, 4)
    cw = f(inp["gdn_conv"]).reshape(NL, 4, 12, 128).transpose(3, 0, 2, 1).reshape(128, NL * 48)
    lbl = f(inp["hgrn_lb_logits"]).reshape(NL, 4, 128).transpose(2, 0, 1).reshape(128, NL * 4)
    shared = {
        "w_in": f(inp["w_in"]), "w_out": f(inp["w_out"]), "w_up": f(inp["w_mlp_up"]), "w_dn": f(inp["w_mlp_down"]),
        "w_ple": f(inp["w_ple_proj"]), "w_gate": f(inp["w_ple_gate"]),
        "gains": np.ascontiguousarray(gains), "hw": np.ascontiguousarray(hw), "sm": np.ascontiguousarray(sm),
        "convw": np.ascontiguousarray(cw), "lbl": np.ascontiguousarray(lbl),
    }
    shared.update(host_consts())
    x = f(inp["x"])
    p = f(inp["p"])
    maps = []
    for b in range(x.shape[0]):
        m = dict(shared)
        m["x"] = np.ascontiguousarray(x[b])
        m["p"] = np.ascontiguousarray(p[:, b])
        maps.append(m)
    return maps


def kernel(**inputs):
    nc = build()
    maps = make_in_maps(inputs)
    res = run_bass_kernel_spmd(nc, maps, core_ids=list(range(len(maps))))
    return np.stack([np.asarray(r["out"], dtype=np.float32) for r in res.results], axis=0)
```

```python
import contextlib
import numpy as np
import ml_dtypes
import concourse.bass as bass
import concourse.mybir as mybir
from concourse.bass_utils import run_bass_kernel_spmd

F32 = mybir.dt.float32
BF16 = mybir.dt.bfloat16
AF = mybir.ActivationFunctionType
ALU = mybir.AluOpType
AX = mybir.AxisListType

D = 1024
T = 4096
NLAYER = 2
TB = 512
NT = TB // 128
EPS = 1e-6
DFF = 4096
PLE = 256
INC = 4104


class Region:
    __slots__ = ("name", "w", "r", "excl")

    def __init__(self, name, excl=False):
        self.name = name
        self.w = None
        self.r = []
        self.excl = excl


class _Rec:
    def __getattr__(self, name):
        def f(*a, **k):
            return (name, a, k)
        return f


_REC = _Rec()
DEBUG_MAP = None


class Prog:
    ENG = ("pe", "act", "dve", "pool", "sp")

    def __init__(self, nc, stack):
        self.nc = nc
        self.stack = stack
        self.q = {e: [] for e in self.ENG}
        self.sem = {e: stack.enter_context(nc.semaphore("s_" + e)) for e in self.ENG}
        self.cnt = {e: 0 for e in self.ENG}
        self.seen = {e: {} for e in self.ENG}

    def new_sem(self, name):
        return self.stack.enter_context(self.nc.semaphore(name))

    def _deps(self, eng, reads, writes):
        deps = {}

        def add(tok):
            if tok is None:
                return
            sem, val, teng = tok
            if teng == "pe" and eng == "pe":
                return
            k = id(sem)
            if k not in deps or deps[k][1] < val:
                deps[k] = (sem, val)
        for b in reads:
            add(b.w)
            if b.excl:
                for t in b.r:
                    if t[2] != eng:
                        add(t)
        for b in writes:
            add(b.w)
            for t in b.r:
                add(t)
        out = []
        seen = self.seen[eng]
        for k, (sem, val) in deps.items():
            if k in seen and seen[k] >= val:
                continue
            seen[k] = val
            out.append((sem, val))
        return out

    def _mark(self, tok, reads, writes):
        for b in reads:
            b.r.append(tok)
            if len(b.r) > 64:
                best = {}
                for t in b.r:
                    k = id(t[0])
                    if k not in best or best[k][1] < t[1]:
                        best[k] = t
                b.r = list(best.values())
        for b in writes:
            b.w = tok
            b.r = []

    def op(self, eng, fn, reads=(), writes=()):
        waits = self._deps(eng, reads, writes)
        self.cnt[eng] += 1
        tok = (self.sem[eng], self.cnt[eng], eng)
        self.q[eng].append((waits, fn(_REC), self.sem[eng], 1))
        self._mark(tok, reads, writes)
        return tok

    def dma(self, eng, fn, dsem, dcount, reads=(), writes=()):
        waits = self._deps(eng, reads, writes)
        tok = (dsem, dcount * 16, "dma")
        self.q[eng].append((waits, fn(_REC), dsem, 16))
        self._mark(tok, reads, writes)
        return tok

    def wait_all(self, eng, toks):
        self.q[eng].append(([(s, v) for (s, v, _) in toks], None, None, 0))

    def emit(self):
        nc = self.nc
        engmap = {"pe": "tensor", "act": "scalar", "dve": "vector", "pool": "gpsimd", "sp": "sync"}
        with nc.Block() as block:
            for e in self.ENG:
                q = self.q[e]

                def body(engine, q=q):
                    for waits, fn, sem, inc in q:
                        for (s, v) in waits:
                            engine.wait_ge(s, v)
                        if fn is not None:
                            name, a, k = fn
                            ins = getattr(engine, name)(*a, **k)
                            ins.then_inc(sem, inc)
                            if DEBUG_MAP is not None:
                                try:
                                    DEBUG_MAP.append((str(getattr(ins, "name", None) or getattr(getattr(ins, "ins", None), "name", None)), e, name,
                                                      str(k.get("out", a[0] if a else ""))[:120]))
                                except Exception as ex:
                                    DEBUG_MAP.append(("?", e, name, str(ex)))
                getattr(block, engmap[e])(body)


class Ring:
    def __init__(self, alloc, name, shape, dtype, n):
        self.items = [(alloc(f"{name}{i}", shape, dtype), Region(f"{name}{i}")) for i in range(n)]
        self.i = 0

    def next(self):
        it = self.items[self.i % len(self.items)]
        self.i += 1
        return it


class DmaBuf:
    def __init__(self, P, tensor, name):
        self.t = tensor
        self.reg = Region(name)
        self.sem = P.new_sem("d_" + name)
        self.n = 0


def build(nblk=T // TB, layers=(0, 1), dbg=False):
    nc = bass.Bass("TRN2", target_bir_lowering=False)
    NL = NLAYER
    dram = lambda n, s, d, k="ExternalInput": nc.dram_tensor(n, s, d, kind=k).ap()
    x_d = dram("x", [T, D], F32)
    p_d = dram("p", [NL, T, PLE], F32)
    w_in_d = dram("w_in", [NL, D, INC], F32)
    w_out_d = dram("w_out", [NL, D, D], F32)
    w_up_d = dram("w_up", [NL, D, DFF], F32)
    w_dn_d = dram("w_dn", [NL, DFF, D], F32)
    w_ple_d = dram("w_ple", [NL, PLE, D], F32)
    w_gate_d = dram("w_gate", [NL, D, D], F32)
    gains_d = dram("gains", [NL * 5, D], F32)
    hw_d = dram("hw", [NL * 2, 512], F32)
    sm_d = dram("sm", [NL * 2, 4], F32)
    convw_d = dram("convw", [128, NL * 48], F32)
    lbl_d = dram("lbl", [128, NL * 4], F32)
    c_identb = dram("c_identb", [128, 128], BF16)
    c_f32 = dram("c_f32", [128, 6 * 128 + 512], F32)
    out_d = dram("out", [T, D], F32, "ExternalOutput")
    dbg_d = dram("dbg", [128, 4096], F32, "ExternalOutput") if dbg else None

    with contextlib.ExitStack() as st:
        P = Prog(nc, st)
        sb = lambda n, s, d: st.enter_context(nc.sbuf_tensor(n, s, d))
        banks = [st.enter_context(nc.psum_tensor(f"bank{i}", [128, 512], F32)) for i in range(8)]
        bank_reg = [Region(f"bank{i}", excl=True) for i in range(8)]
        pool_i = {"dense": 0, "mix": 0, "ga": 0, "gdb": 0}
        pool_banks = {"dense": (0, 1), "mix": (2, 3, 4, 5, 6, 7), "ga": (2, 3, 4), "gdb": (5, 6, 7)}

        def psum(kind):
            bl = pool_banks[kind]
            i = bl[pool_i[kind] % len(bl)]
            pool_i[kind] += 1
            return banks[i], bank_reg[i]

        def psb(bank):
            return bank[:].bitcast(BF16)[:, 0:512].rearrange("p (h n) -> p h n", h=4)

        def ps4(bank):
            return bank[:].rearrange("p (h n) -> p h n", h=4)

        identb = sb("identb", [128, 128], BF16)
        cf = sb("cf", [128, 6 * 128 + 512], F32)
        Umat = cf[:, 0:128]
        SUmat = cf[:, 128:256]
        ones = cf[:, 256:384]
        rst = cf[:, 384:896]
        identf = cf[:, 896:1024]
        NEGU = cf[:, 1024:1152]
        NEGL = cf[:, 1152:1280]
        convw = sb("convw_sb", [128, NL * 48], F32)
        lbl = sb("lbl_sb", [128, NL * 4], F32)
        smb = sb("smb", [128, NL * 2 * 4], F32)
        hwb = sb("hwb", [128, NL * 2 * 512], F32)
        negA = sb("negA", [128, NL * 4], F32)
        lb = sb("lb", [128, NL * 4], F32)
        oml = sb("oml", [128, NL * 4], F32)
        cst_sem = P.new_sem("cst")
        ncst = 0
        cst_toks = []

        def cload(dst, src):
            nonlocal ncst
            ncst += 1
            cst_toks.append(P.dma("sp", lambda e: e.dma_start(out=dst, in_=src), cst_sem, ncst))
        cload(identb[:], c_identb)
        cload(cf[:], c_f32)
        cload(convw[:], convw_d)
        cload(lbl[:], lbl_d)
        cload(smb[:], sm_d.rearrange("a b -> (a b)").partition_broadcast(128))
        cload(hwb[:], hw_d.rearrange("a b -> (a b)").partition_broadcast(128))
        for e in ("pe", "act", "dve", "pool"):
            P.wait_all(e, [cst_toks[-1]])
        R_c = Region("consts_derived")
        for l in range(NL):
            P.op("act", lambda e, l=l: e.activation(out=negA[:, l * 4:l * 4 + 4], in_=smb[:, l * 8:l * 8 + 4], func=AF.Exp),
                 writes=[R_c])
        P.op("dve", lambda e: e.tensor_scalar(out=negA[:], in0=negA[:], scalar1=-1.0, scalar2=None, op0=ALU.mult),
             reads=[R_c], writes=[R_c])
        elb = sb("elb", [128, NL * 4], F32)
        slb = sb("slb", [128, 4], F32)
        P.op("act", lambda e: e.activation(out=elb[:], in_=lbl[:], func=AF.Exp), writes=[R_c])
        P.op("dve", lambda e: e.tensor_copy(out=slb[:], in_=elb[:, 0:4]), reads=[R_c], writes=[R_c])
        for l in range(1, NL):
            P.op("dve", lambda e, l=l: e.tensor_tensor(out=slb[:], in0=slb[:], in1=elb[:, l * 4:l * 4 + 4], op=ALU.add),
                 reads=[R_c], writes=[R_c])
        P.op("dve", lambda e: e.reciprocal(out=slb[:], in_=slb[:]), reads=[R_c], writes=[R_c])
        P.op("dve", lambda e: e.memset(lb[:, 0:4], 0.0), reads=[R_c], writes=[R_c])
        for l in range(1, NL):
            P.op("dve", lambda e, l=l: e.tensor_tensor(out=elb[:, l * 4:l * 4 + 4], in0=elb[:, l * 4:l * 4 + 4], in1=slb[:], op=ALU.mult),
                 reads=[R_c], writes=[R_c])
            P.op("dve", lambda e, l=l: e.tensor_tensor(out=lb[:, l * 4:l * 4 + 4], in0=lb[:, (l - 1) * 4:l * 4], in1=elb[:, l * 4:l * 4 + 4], op=ALU.add),
                 reads=[R_c], writes=[R_c])
        P.op("dve", lambda e: e.tensor_scalar(out=oml[:], in0=lb[:], scalar1=-1.0, scalar2=1.0, op0=ALU.mult, op1=ALU.add),
             reads=[R_c], writes=[R_c])
        for e in ("act", "pool"):
            P.op(e, (lambda en: en.memset(slb[:, 0:1], 0.0)) if e == "pool" else
                 (lambda en: en.activation(out=slb[:, 0:1], in_=slb[:, 0:1], func=AF.Copy)), reads=[R_c], writes=[R_c])

        h = sb("h", [128, NT, D], F32)
        h_reg = [Region(f"h{t}") for t in range(NT)]
        h_ld = [DmaBuf(P, None, f"hld{t}") for t in range(NT)]
        actT = sb("actT", [128, 8, TB], BF16)
        actT_reg = [Region(f"actT{t}") for t in range(NT)]
        big = sb("big", [128, 5136], F32)
        yacc = big[:, 0:4096].rearrange("p (t d) -> p t d", t=NT)
        yacc_reg = [Region(f"yacc{t}") for t in range(NT)]
        mix_reg = [Region(f"mix{t}") for t in range(NT)]
        NSLOT = 3
        wslots = [DmaBuf(P, sb(f"wslot{i}", [128, 4096], BF16), f"wslot{i}") for i in range(NSLOT)]
        wslot_i = [0]
        wab = DmaBuf(P, sb("wab", [128, 8, 8], BF16), "wab")
        gring = [DmaBuf(P, sb(f"gain{i}", [128, D], F32), f"gain{i}") for i in range(1)]
        gring_i = [0]
        pbuf = DmaBuf(P, sb("pbuf", [128, NT, PLE], BF16), "pbuf")
        out_sem = P.new_sem("outst")
        out_n = [0]
        st_sem = [P.new_sem(f"st{t}") for t in range(NT)]
        st_n = [0] * NT

        raw = big[:, 0:4 * (TB + 3)].rearrange("p (a b) -> p a b", a=4)
        raw_reg = Region("raw")
        halo = sb("halo", [128, NL * 12, 3], F32)
        halo_reg = Region("halo")
        cs = big[:, 2064:2064 + 3072].bitcast(BF16).rearrange("p (a b) -> p a b", a=12)
        cs_reg = [Region(f"cs{g}") for g in range(3)]
        mix = big[:, 0:2048].bitcast(BF16).rearrange("p (t d) -> p t d", t=NT)
        AL_MIX = [raw_reg] + cs_reg + mix_reg
        AL_MLP = yacc_reg + mix_reg
        zw = sb("zw", [128, NT, 512], BF16)
        zw_reg = [Region(f"zw{t}") for t in range(NT)]
        gw = sb("gw", [128, NT, 512], BF16)
        gw_reg = [Region(f"gw{t}") for t in range(NT)]
        iv = sb("iv", [128, NT, 512], BF16)
        iv_reg = [Region(f"iv{t}") for t in range(NT)]
        ab = sb("ab", [128, NT, 8], F32)
        ab_reg = [Region(f"ab{t}") for t in range(NT)]
        qhs = sb("qhs", [128, 4, TB], BF16)
        qhs_reg = [Region(f"qhs{i}") for i in range(4)]
        big2 = sb("big2", [128, 4096], BF16)
        qtT = big2[:, 0:2048].rearrange("p (a b) -> p a b", a=4)
        ktT = big2[:, 2048:4096].rearrange("p (a b) -> p a b", a=4)
        khT = sb("khT", [128, 4, TB], BF16)
        qh0 = sb("qh0", [128, 4, NT, 128], BF16)
        qh1 = sb("qh1", [128, 4, NT, 128], BF16)
        dS = sb("dS", [128, 4, 8], F32)
        hg_reg = [Region(f"hg{i}") for i in range(4)]
        Sg = sb("Sg", [128, NL, 512], F32)
        Sh = sb("Sh", [128, NL, 512], F32)
        Sg_reg = [Region(f"Sg{l}") for l in range(NL)]
        Sh_reg = [Region(f"Sh{l}") for l in range(NL)]
        Sgb = sb("Sgb", [128, NL, 512], BF16)
        Sgb_reg = [Region(f"Sgb{l}") for l in range(NL)]
        Shb = [[(sb(f"Shb{l}_{i}", [128, 512], BF16), Region(f"Shb{l}_{i}")) for i in range(2)] for l in range(NL)]
        Shb_cur = [None] * NL
        uT_items = [(big2[:, 0:2048].rearrange("p (a b) -> p a b", a=4), Region("uT0")),
                    (big2[:, 2048:4096].rearrange("p (a b) -> p a b", a=4), Region("uT1"))]
        uT_i = [0]
        AL_UT = [uT_items[0][1], uT_items[1][1]]
        tf = Ring(sb, "tf", [128, 512], F32, 3)
        tbf = Ring(sb, "tb", [128, 512], BF16, 3)
        tfG = Ring(sb, "tfG", [128, 512], F32, 4)
        tbG = Ring(sb, "tbG", [128, 512], BF16, 8)
        tsG = Ring(sb, "tsG", [128, 16], F32, 8)
        gd = {n: (sb("gd_" + n, [128, 512], BF16), Region("gd_" + n)) for n in
              ("qs", "kh", "kb", "kbT", "EM", "EMT", "ECT", "r", "vn")}
        gdp = [{n: (sb(f"gdp{p_}_" + n, [128, 512], BF16), Region(f"gdp{p_}_" + n)) for n in
                ("khT", "qsT", "kt", "vs", "qkT", "Pm0", "PT0", "X0")} for p_ in range(2)]
        hgf = {n: (sb("hgf_" + n, [128, 512], F32), Region("hgf_" + n)) for n in ("sg", "lf", "b", "bm", "bt")}
        tsm = Ring(sb, "ts", [128, 16], F32, 12)
        hsb = Ring(sb, "hsb", [128, D], BF16, 2)
        junk = hsb
        gU = Ring(sb, "gU", [128, 512], F32, 1)
        gall = sb("gall", [128, NT, 16], F32)
        egall = sb("egall", [128, NT, 16], F32)
        gall_reg = Region("gall")
        identb4 = identb[:].unsqueeze(1).to_broadcast([128, 4, 128])

        R0 = Region("init")
        P.op("pool", lambda e: e.memset(halo[:], 0.0), writes=[halo_reg])
        P.op("pool", lambda e: e.memset(Sg[:], 0.0), writes=Sg_reg)
        P.op("pool", lambda e: e.memset(Sh[:], 0.0), writes=Sh_reg)
        P.op("pool", lambda e: e.memset(Sgb[:], 0.0), writes=Sgb_reg)
        P.op("pool", lambda e: e.memset(qh0[:], 0.0), writes=hg_reg)
        P.op("pool", lambda e: e.memset(qh1[:], 0.0), writes=hg_reg)
        for l in range(NL):
            t_, r_ = Shb[l][0]
            P.op("pool", lambda e, t_=t_: e.memset(t_[:], 0.0), writes=[r_])
            Shb_cur[l] = 0

        class Load:
            def __init__(self, src_ap):
                self.src = src_ap
                self.slot = None
                self.finished = False

        loads = []
        issued = [0]

        def _try_issue():
            while issued[0] < len(loads):
                k = issued[0]
                if k >= NSLOT and not loads[k - NSLOT].finished:
                    return
                ld = loads[k]
                s = wslots[k % NSLOT]
                s.n += 1
                src_ap = ld.src
                n_el = 1
                for v in src_ap.shape[1:]:
                    n_el *= v
                dst = s.t[:, 0:n_el].rearrange("p (a b) -> p a b", a=src_ap.shape[1])
                P.dma("pool", lambda e: e.dma_start(out=dst, in_=src_ap), s.sem, s.n, writes=[s.reg])
                ld.slot = s
                issued[0] += 1

        def wreq(src_ap):
            ld = Load(src_ap)
            loads.append(ld)
            return ld

        def wget(ld):
            _try_issue()
            assert ld.slot is not None, "weight slot ring too small for this access pattern"
            return ld.slot

        def wdone(ld):
            ld.finished = True
            _try_issue()

        def gload(row):
            g = gring[gring_i[0] % 1]
            gring_i[0] += 1
            g.n += 1
            P.dma("sp", lambda e: e.dma_start(out=g.t[:], in_=gains_d[row, :].partition_broadcast(128)), g.sem, g.n,
                  writes=[g.reg])
            return g

        def rstd_from_ss(ss_ap, n, dim, rss):
            P.op("act", lambda e: e.activation(out=ss_ap, in_=ss_ap, func=AF.Ln, scale=1.0 / dim, bias=EPS),
                 reads=[rss], writes=[rss])
            P.op("act", lambda e: e.activation(out=ss_ap, in_=ss_ap, func=AF.Exp, scale=-0.5),
                 reads=[rss], writes=[rss])

        def head_scale(eng, out4, in4, sc, reads, writes, nh=4):
            if eng != "act":
                P.op(eng, lambda e: e.tensor_tensor(out=out4, in0=in4, in1=sc.unsqueeze(2).to_broadcast([128, nh, 128]), op=ALU.mult),
                     reads=reads, writes=writes)
                return
            for hh in range(nh):
                if eng == "act":
                    P.op("act", lambda e, hh=hh: e.activation(out=out4[:, hh, :], in_=in4[:, hh, :], func=AF.Copy,
                                                              scale=sc[:, hh:hh + 1]), reads=reads, writes=writes)
                else:
                    P.op(eng, lambda e, hh=hh: e.tensor_scalar(out=out4[:, hh, :], in0=in4[:, hh, :],
                                                               scalar1=sc[:, hh:hh + 1], scalar2=None, op0=ALU.mult),
                         reads=reads, writes=writes)

        def to_actT(src_ap_fn, src_regs, gain=None, norm=True):
            if norm:
                ss_t, ss_r = tsm.next()
                for t in range(NT):
                    jk, jr = hsb.next()
                    P.op("act", lambda e, t=t, jk=jk: e.activation(out=jk[:], in_=src_ap_fn(t), func=AF.Square, accum_out=ss_t[:, t:t + 1]),
                         reads=[src_regs[t]], writes=[jr, ss_r])
                rstd_from_ss(ss_t[:, 0:NT], NT, D, ss_r)
            for t in range(NT):
                src = src_ap_fn(t)
                hs_t, hs_r = hsb.next()
                if norm:
                    P.op("dve", lambda e, src=src, hs_t=hs_t, t=t: e.scalar_tensor_tensor(
                        out=hs_t[:], in0=src, scalar=ss_t[:, t:t + 1], in1=gain.t[:], op0=ALU.mult, op1=ALU.mult),
                        reads=[src_regs[t], ss_r, gain.reg], writes=[hs_r])
                else:
                    P.op("act" if t % 2 else "pool", (lambda e, src=src, hs_t=hs_t: e.activation(out=hs_t[:], in_=src, func=AF.Copy)) if t % 2 else
                         (lambda e, src=src, hs_t=hs_t: e.tensor_copy(out=hs_t[:], in_=src)),
                         reads=[src_regs[t]], writes=[hs_r])
                bk, br = psum("dense" if t % 2 == 0 else "mix")
                pv = bk[:].bitcast(BF16).rearrange("p (c n) -> p c n", c=8)
                for c in range(8):
                    P.op("pe", lambda e, c=c, pv=pv, hs_t=hs_t: e.transpose(pv[:, c, :], hs_t[:, c * 128:(c + 1) * 128], identb[:]),
                         reads=[hs_r], writes=[br])
                eng = "act" if t % 2 == 0 else "dve"
                if eng == "act":
                    P.op("act", lambda e, pv=pv, t=t: e.activation(out=actT[:, :, t * 128:(t + 1) * 128], in_=pv, func=AF.Copy),
                         reads=[br], writes=[actT_reg[t]])
                else:
                    P.op("dve", lambda e, pv=pv, t=t: e.tensor_copy(out=actT[:, :, t * 128:(t + 1) * 128], in_=pv),
                         reads=[br], writes=[actT_reg[t]])

        def mm_fm(slot, w3, cchunk, out_fn):
            bk, br = psum("dense")
            for kc in range(8):
                P.op("pe", lambda e, kc=kc, bk=bk: e.matmul(bk[:], lhsT=w3[:, kc, cchunk * 128:(cchunk + 1) * 128],
                                                          rhs=actT[:, kc, :], start=(kc == 0), stop=(kc == 7)),
                     reads=[slot.reg] + actT_reg, writes=[br])
            out_fn(bk, br)

        def mm_tm(slot, w3, t, ncols, out_fn, kch=8, lhs=None, lhs_regs=None, col0=0, pool="dense"):
            bk, br = psum(pool)
            for kc in range(kch):
                lt = actT[:, kc, t * 128:(t + 1) * 128] if lhs is None else lhs(kc)
                P.op("pe", lambda e, kc=kc, bk=bk, lt=lt: e.matmul(bk[:, 0:ncols], lhsT=lt, rhs=w3[:, kc, col0:col0 + ncols],
                                                                  start=(kc == 0), stop=(kc == kch - 1)),
                     reads=[slot.reg] + ([actT_reg[t]] if lhs is None else lhs_regs), writes=[br])
            out_fn(bk, br)

        v4 = lambda tt_: tt_[:].rearrange("p (h n) -> p h n", h=4)
        dbg_n = [0]
        marks = []

        def mark(name):
            marks.append((name, dict(P.cnt)))

        def dbg_dump(ap, regs, width):
            if not dbg:
                return
            c0 = dbg_n[0]
            dbg_n[0] += width
            tt_, tr_ = tf.next()
            P.op("dve", lambda e: e.tensor_copy(out=tt_[:, 0:width], in_=ap), reads=regs, writes=[tr_])
            out_n[0] += 1
            P.dma("sp", lambda e: e.dma_start(out=dbg_d[:, c0:c0 + width], in_=tt_[:, 0:width]), out_sem, out_n[0],
                  reads=[tr_])

        for blk in range(nblk):
            tok0 = blk * TB
            for t in range(NT):
                hb = h_ld[t]
                hb.n += 1
                P.dma("sp", lambda e, t=t: e.dma_start(out=h[:, t, :], in_=x_d[tok0 + t * 128: tok0 + (t + 1) * 128, :]),
                      hb.sem, hb.n, writes=[h_reg[t]])
            for l in layers:
                mark(f"b{blk}l{l}:norm1")
                g_pre = gload(l * 5 + 0)
                to_actT(lambda t: h[:, t, :], h_reg, gain=g_pre)
                mark(f"b{blk}l{l}:proj")
                wabn = wab
                wabn.n += 1
                P.dma("pool", lambda e, l=l: e.dma_start(
                    out=wab.t[:], in_=w_in_d[l, :, 2048:2056].rearrange("(c p) n -> p c n", p=128)),
                    wab.sem, wab.n, writes=[wab.reg])
                win = lambda c0: w_in_d[l, :, c0:c0 + 512].rearrange("(c p) n -> p c n", p=128)
                w8 = lambda ap: ap.rearrange("(c p) n -> p c n", p=128)
                L_qkv = [wreq(win(g_ * 512)) for g_ in range(3)]
                L_z = wreq(win(1536))
                L_qh = wreq(win(3080))
                L_f = wreq(win(2056))
                L_i = wreq(win(2568))
                L_g = wreq(win(3592))
                L_o = [wreq(w8(w_out_d[l, :, 0:512])), wreq(w8(w_out_d[l, :, 512:1024]))]
                L_ud = []
                for g_ in range(8):
                    L_ud.append((wreq(w8(w_up_d[l, :, g_ * 512:(g_ + 1) * 512])), wreq(w8(w_dn_d[l, g_ * 512:(g_ + 1) * 512, :]))))
                L_p = wreq(w8(w_ple_d[l]))
                L_g0 = wreq(w8(w_gate_d[l, :, 0:512]))
                L_g1 = wreq(w8(w_gate_d[l, :, 512:1024]))
                for grp in range(3):
                    s = wget(L_qkv[grp])
                    w3 = s.t[:].rearrange("p (a b) -> p a b", a=8)
                    P.op("pool", lambda e, grp=grp: e.tensor_copy(out=raw[:, :, 0:3], in_=halo[:, l * 12 + grp * 4:l * 12 + grp * 4 + 4, :]),
                         reads=[halo_reg], writes=[raw_reg] + AL_MLP)
                    for hh in range(4):
                        def ev(bk, br, hh=hh):
                            P.op("act", lambda e: e.activation(out=raw[:, hh, 3:3 + TB], in_=bk[:], func=AF.Copy),
                                 reads=[br], writes=[raw_reg])
                        mm_fm(s, w3, hh, ev)
                    wdone(L_qkv[grp])
                    P.op("pool", lambda e, grp=grp: e.tensor_copy(out=halo[:, l * 12 + grp * 4:l * 12 + grp * 4 + 4, :], in_=raw[:, :, TB:TB + 3]),
                         reads=[raw_reg], writes=[halo_reg])
                    for hh in range(4):
                        ch = grp * 4 + hh
                        acc_t, acc_r = tf.next()
                        wc = lambda k: convw[:, l * 48 + ch * 4 + k: l * 48 + ch * 4 + k + 1]
                        P.op("dve", lambda e, hh=hh, acc_t=acc_t, wc=wc: e.tensor_scalar(
                            out=acc_t[:], in0=raw[:, hh, 0:TB], scalar1=wc(0), scalar2=None, op0=ALU.mult),
                            reads=[raw_reg], writes=[acc_r])
                        for k in range(1, 4):
                            P.op("dve", lambda e, hh=hh, k=k, acc_t=acc_t, wc=wc: e.scalar_tensor_tensor(
                                out=acc_t[:], in0=raw[:, hh, k:k + TB], scalar=wc(k), in1=acc_t[:], op0=ALU.mult, op1=ALU.add),
                                reads=[raw_reg, acc_r], writes=[acc_r])
                        P.op("act", lambda e, ch=ch, acc_t=acc_t: e.activation(out=cs[:, ch, :], in_=acc_t[:], func=AF.Silu),
                             reads=[acc_r], writes=[cs_reg[grp]] + AL_MLP)
                s = wget(L_z)
                w3 = s.t[:].rearrange("p (a b) -> p a b", a=8)
                for t in range(NT):
                    def ev(bk, br, t=t):
                        zt, zr = tf.next()
                        P.op("act", lambda e: e.activation(out=zt[:], in_=bk[:], func=AF.Silu), reads=[br], writes=[zr])
                        P.op("pool", lambda e: e.tensor_tensor(out=zw[:, t, :], in0=zt[:], in1=hwb[:, (l * 2) * 512:(l * 2 + 1) * 512], op=ALU.mult),
                             reads=[zr], writes=[zw_reg[t]])
                    mm_tm(s, w3, t, 512, ev)
                wdone(L_z)
                for t in range(NT):
                    def ev(bk, br, t=t):
                        P.op("act", lambda e: e.activation(out=ab[:, t, :], in_=bk[:, 0:8], func=AF.Copy), reads=[br], writes=[ab_reg[t]])
                    mm_tm(wab, wab.t[:], t, 8, ev)
                P.op("act", lambda e: e.activation(out=gall[:, :, 0:4], in_=ab[:, :, 4:8], func=AF.Exp, scale=-1.0), reads=ab_reg, writes=[gall_reg])
                P.op("dve", lambda e: e.tensor_scalar(out=gall[:, :, 0:4], in0=gall[:, :, 0:4], scalar1=1.0, scalar2=None, op0=ALU.add), reads=[gall_reg], writes=[gall_reg])
                P.op("dve", lambda e: e.reciprocal(out=gall[:, :, 0:4], in_=gall[:, :, 0:4]), reads=[gall_reg], writes=[gall_reg])
                P.op("dve", lambda e: e.tensor_tensor(out=gall[:, :, 8:12], in0=ab[:, :, 0:4], in1=smb[:, l * 8 + 4:l * 8 + 8].unsqueeze(1).to_broadcast([128, NT, 4]), op=ALU.add),
                     reads=ab_reg + [gall_reg], writes=[gall_reg])
                P.op("act", lambda e: e.activation(out=gall[:, :, 8:12], in_=gall[:, :, 8:12], func=AF.Exp), reads=[gall_reg], writes=[gall_reg])
                P.op("act", lambda e: e.activation(out=gall[:, :, 8:12], in_=gall[:, :, 8:12], func=AF.Ln, bias=1.0), reads=[gall_reg], writes=[gall_reg])
                P.op("dve", lambda e: e.tensor_tensor(out=gall[:, :, 4:8], in0=gall[:, :, 8:12], in1=negA[:, l * 4:l * 4 + 4].unsqueeze(1).to_broadcast([128, NT, 4]), op=ALU.mult),
                     reads=[gall_reg], writes=[gall_reg])
                pgA_b, pgA_r = psum("mix")
                for t_ in range(NT):
                    for i_, m_ in enumerate((Umat, SUmat, ones)):
                        P.op("pe", lambda e, i_=i_, m_=m_, t_=t_: e.matmul(pgA_b[:, t_ * 12 + i_ * 4:t_ * 12 + i_ * 4 + 4], lhsT=m_, rhs=gall[:, t_, 4:8], start=True, stop=True),
                             reads=[gall_reg], writes=[pgA_r])
                P.op("act", lambda e: e.activation(out=egall[:, :, 0:12], in_=pgA_b[:, 0:NT * 12].rearrange("p (t c) -> p t c", t=NT), func=AF.Exp),
                     reads=[pgA_r, gall_reg], writes=[gall_reg])
                P.op("dve", lambda e: e.tensor_scalar(out=egall[:, :, 12:16], in0=egall[:, :, 0:4], scalar1=-1.0, scalar2=None, op0=ALU.mult), reads=[gall_reg], writes=[gall_reg])

                def head_norm(t, o_ap4, o_regs, wmul, wmul_reg, col0, tfr, tsr):
                    sq_t, sq_r = tfr.next()
                    P.op("act", lambda e: e.activation(out=sq_t[:].rearrange("p (h n) -> p h n", h=4), in_=o_ap4, func=AF.Square), reads=o_regs, writes=[sq_r])
                    so, sor = tsr.next()
                    P.op("dve", lambda e: e.tensor_reduce(out=so[:, 0:4], in_=sq_t[:].rearrange("p (h n) -> p h n", h=4), axis=AX.X, op=ALU.add), reads=[sq_r], writes=[sor])
                    rstd_from_ss(so[:, 0:4], 4, 128.0, sor)
                    a_, ar = tfr.next()
                    head_scale("act", a_[:].rearrange("p (h n) -> p h n", h=4), o_ap4, so[:, 0:4], o_regs + [sor], [ar])
                    P.op("pool", lambda e: e.tensor_tensor(out=mix[:, t, col0:col0 + 512], in0=a_[:], in1=wmul, op=ALU.mult), reads=[ar, wmul_reg], writes=[mix_reg[t], raw_reg])

                hand = {}

                def gdn_A(t):
                    tsl = slice(t * 128, (t + 1) * 128)
                    mark(f"b{blk}l{l}:gdn{t}")
                    pq_b, pq_r = psum("ga")
                    pk_b, pk_r = psum("ga")
                    pv_b, pv_r = psum("ga")
                    for grp, (pb, pr) in enumerate(((pq_b, pq_r), (pk_b, pk_r), (pv_b, pv_r))):
                        for hh in range(4):
                            P.op("pe", lambda e, grp=grp, hh=hh, pb=pb: e.transpose(psb(pb)[:, hh, :], cs[:, grp * 4 + hh, tsl], identb[:]),
                                 reads=[cs_reg[grp]], writes=[pr])
                    sc, scr = tsG.next()
                    for pb, pr, c0 in ((pq_b, pq_r, 0), (pk_b, pk_r, 4)):
                        sq_t, sq_r = (gd["qs"] if c0 == 0 else gd["kh"])
                        for hh in range(4):
                            P.op("act", lambda e, pb=pb, sq_t=sq_t, hh=hh, c0=c0: e.activation(out=sq_t[:, hh * 128:(hh + 1) * 128], in_=psb(pb)[:, hh, :], func=AF.Square,
                                                                                      accum_out=sc[:, c0 + hh:c0 + hh + 1]),
                                 reads=[pr], writes=[sq_r, scr])
                    rstd_from_ss(sc[:, 0:8], 8, 1.0, scr)
                    yield
                    g1, g1r = gall[:, t, :], gall_reg
                    eg, egr = egall[:, t, :], gall_reg
                    s2, s2r = tsG.next()
                    P.op("dve", lambda e: e.tensor_scalar(out=s2[:, 0:4], in0=sc[:, 0:4], scalar1=128.0 ** -0.5, scalar2=None, op0=ALU.mult), reads=[scr], writes=[s2r])
                    P.op("dve", lambda e: e.scalar_tensor_tensor(out=s2[:, 4:8], in0=sc[:, 4:8], scalar=-1.0, in1=g1[:, 0:4], op0=ALU.mult, op1=ALU.mult),
                         reads=[scr, g1r, s2r], writes=[s2r])
                    P.op("dve", lambda e: e.tensor_tensor(out=s2[:, 8:12], in0=sc[:, 4:8], in1=eg[:, 4:8], op=ALU.mult), reads=[scr, egr, s2r], writes=[s2r])
                    yield
                    qs, qsr = gd["qs"]
                    kh, khr = gd["kh"]
                    kb, kbr = gd["kb"]
                    kt, ktr = gdp[t % 2]["kt"]
                    vs, vsr = gdp[t % 2]["vs"]
                    v4 = lambda tt_: tt_[:].rearrange("p (h n) -> p h n", h=4)
                    head_scale("dve", v4(qs), psb(pq_b), s2[:, 0:4], [pq_r, s2r], [qsr])
                    head_scale("dve", v4(kh), psb(pk_b), sc[:, 4:8], [pk_r, scr], [khr])
                    head_scale("dve", v4(kb), psb(pk_b), s2[:, 4:8], [pk_r, s2r], [kbr])
                    head_scale("dve", v4(kt), psb(pk_b), s2[:, 8:12], [pk_r, s2r], [ktr])
                    P.op("act", lambda e: e.activation(out=v4(vs), in_=psb(pv_b), func=AF.Copy), reads=[pv_r], writes=[vsr])
                    yield
                    fm = []
                    for (src, srcr), dn in (((kh, khr), "khT"), ((kb, kbr), "kbT"), ((qs, qsr), "qsT")):
                        pb, pr = psum("ga")
                        for hh in range(4):
                            P.op("pe", lambda e, hh=hh, pb=pb, src=src: e.transpose(psb(pb)[:, hh, :], src[:, hh * 128:(hh + 1) * 128], identb[:]),
                                 reads=[srcr], writes=[pr])
                        dst, dstr = (gd if dn == "kbT" else gdp[t % 2])[dn]
                        P.op("dve" if len(fm) != 1 else "act",
                             (lambda e, dst=dst, pb=pb: e.tensor_copy(out=v4(dst), in_=psb(pb))) if len(fm) != 1 else
                             (lambda e, dst=dst, pb=pb: e.activation(out=v4(dst), in_=psb(pb), func=AF.Copy)),
                             reads=[pr], writes=[dstr])
                        fm.append((dst, dstr))
                    (khT_, khTr), (kbT_, kbTr), (qsT_, qsTr) = fm
                    yield
                    gu, gur = gU.next()
                    P.op("dve", lambda e: e.tensor_tensor(out=v4(gu), in0=Umat.unsqueeze(1).to_broadcast([128, 4, 128]),
                                                          in1=g1[:, 4:8].unsqueeze(2).to_broadcast([128, 4, 128]), op=ALU.mult),
                         reads=[g1r], writes=[gur])
                    pD_b, pD_r = psum("ga")
                    pDT_b, pDT_r = psum("ga")
                    for hh in range(4):
                        hs_ = slice(hh * 128, (hh + 1) * 128)
                        P.op("pe", lambda e, hh=hh, hs_=hs_: e.matmul(ps4(pD_b)[:, hh, :], lhsT=gu[:, hs_], rhs=SUmat, start=True, stop=False),
                             reads=[gur], writes=[pD_r])
                        P.op("pe", lambda e, hh=hh: e.matmul(ps4(pD_b)[:, hh, :], lhsT=identf, rhs=NEGU, start=False, stop=True),
                             reads=[gur], writes=[pD_r])
                        P.op("pe", lambda e, hh=hh, hs_=hs_: e.matmul(ps4(pDT_b)[:, hh, :], lhsT=SUmat, rhs=gu[:, hs_], start=True, stop=False),
                             reads=[gur], writes=[pDT_r])
                        P.op("pe", lambda e, hh=hh: e.matmul(ps4(pDT_b)[:, hh, :], lhsT=identf, rhs=NEGL, start=False, stop=True),
                             reads=[gur], writes=[pDT_r])
                    yield
                    EM, EMr = gd["EM"]
                    EMT, EMTr = gd["EMT"]
                    ECT, ECTr = gd["ECT"]
                    P.op("act", lambda e: e.activation(out=EM[:], in_=pD_b[:], func=AF.Exp), reads=[pD_r], writes=[EMr])
                    P.op("act", lambda e: e.activation(out=EMT[:], in_=pDT_b[:], func=AF.Exp), reads=[pDT_r], writes=[EMTr])
                    P.op("pool", lambda e: e.tensor_tensor(out=v4(ECT), in0=v4(EMT), in1=identb4, op=ALU.add), reads=[EMTr], writes=[ECTr])
                    yield
                    pG_b, pG_r = psum("ga")
                    pGT_b, pGT_r = psum("ga")
                    pQK_b, pQK_r = psum("ga")
                    for hh in range(4):
                        hs_ = slice(hh * 128, (hh + 1) * 128)
                        P.op("pe", lambda e, hh=hh, hs_=hs_: e.matmul(ps4(pG_b)[:, hh, :], lhsT=khT_[:, hs_], rhs=kbT_[:, hs_], start=True, stop=True),
                             reads=[khTr, kbTr], writes=[pG_r])
                        P.op("pe", lambda e, hh=hh, hs_=hs_: e.matmul(ps4(pGT_b)[:, hh, :], lhsT=kbT_[:, hs_], rhs=khT_[:, hs_], start=True, stop=True),
                             reads=[khTr, kbTr], writes=[pGT_r])
                        P.op("pe", lambda e, hh=hh, hs_=hs_: e.matmul(ps4(pQK_b)[:, hh, :], lhsT=khT_[:, hs_], rhs=qsT_[:, hs_], start=True, stop=True),
                             reads=[khTr, qsTr], writes=[pQK_r])
                    yield
                    Pm, Pmr = gdp[t % 2]["Pm0"]
                    PT, PTr = gdp[t % 2]["PT0"]
                    qkT, qkTr = gdp[t % 2]["qkT"]
                    P.op("dve", lambda e: e.tensor_tensor(out=Pm[:], in0=pG_b[:], in1=EM[:], op=ALU.mult), reads=[pG_r, EMr], writes=[Pmr])
                    P.op("dve", lambda e: e.tensor_tensor(out=PT[:], in0=pGT_b[:], in1=EMT[:], op=ALU.mult), reads=[pGT_r, EMTr], writes=[PTr])
                    P.op("dve", lambda e: e.tensor_tensor(out=qkT[:], in0=pQK_b[:], in1=ECT[:], op=ALU.mult), reads=[pQK_r, ECTr], writes=[qkTr])
                    X, Xr = gdp[t % 2]["X0"]
                    P.op("dve", lambda e, X=X, PT=PT: e.tensor_tensor(out=v4(X), in0=v4(PT), in1=identb4, op=ALU.add), reads=[PTr], writes=[Xr])
                    hand[t] = dict(khT_=khT_, khTr=khTr, qsT_=qsT_, qsTr=qsTr, kt=kt, ktr=ktr, vs=vs, vsr=vsr, qkT=qkT, qkTr=qkTr, Pm=Pm, Pmr=Pmr, PT=PT, PTr=PTr, X=X, Xr=Xr, g1=g1, g1r=g1r, eg=eg, egr=egr)
                    yield

                def gdn_DB(t):
                    hd = hand[t]
                    khT_, khTr, qsT_, qsTr, kt, ktr, vs, vsr, qkT, qkTr, Pm, Pmr, PT, PTr, X, Xr, g1, g1r, eg, egr = hd["khT_"], hd["khTr"], hd["qsT_"], hd["qsTr"], hd["kt"], hd["ktr"], hd["vs"], hd["vsr"], hd["qkT"], hd["qkTr"], hd["Pm"], hd["Pmr"], hd["PT"], hd["PTr"], hd["X"], hd["Xr"], hd["g1"], hd["g1r"], hd["eg"], hd["egr"]
                    for n_ in range(1, 7):
                        pP_b, pP_r = psum("gdb")
                        for hh in range(4):
                            hs_ = slice(hh * 128, (hh + 1) * 128)
                            P.op("pe", lambda e, hh=hh, hs_=hs_, PT=PT, Pm=Pm, pP_b=pP_b: e.matmul(ps4(pP_b)[:, hh, :], lhsT=PT[:, hs_], rhs=Pm[:, hs_], start=True, stop=True),
                                 reads=[PTr, Pmr], writes=[pP_r])
                        if n_ < 6:
                            pPT_b, pPT_r = psum("gdb")
                            for hh in range(4):
                                hs_ = slice(hh * 128, (hh + 1) * 128)
                                P.op("pe", lambda e, hh=hh, hs_=hs_, PT=PT, Pm=Pm, pPT_b=pPT_b: e.matmul(ps4(pPT_b)[:, hh, :], lhsT=Pm[:, hs_], rhs=PT[:, hs_], start=True, stop=True),
                                     reads=[PTr, Pmr], writes=[pPT_r])
                        Pn, Pnr = tbG.next()
                        P.op("act", lambda e, Pn=Pn, pP_b=pP_b: e.activation(out=Pn[:], in_=pP_b[:], func=AF.Copy), reads=[pP_r], writes=[Pnr])
                        if n_ < 6:
                            PTn, PTnr = tbG.next()
                            P.op("dve", lambda e, PTn=PTn, pPT_b=pPT_b: e.tensor_copy(out=PTn[:], in_=pPT_b[:]), reads=[pPT_r], writes=[PTnr])
                        pX_b, pX_r = psum("gdb")
                        for hh in range(4):
                            hs_ = slice(hh * 128, (hh + 1) * 128)
                            P.op("pe", lambda e, hh=hh, hs_=hs_, Pn=Pn, X=X, pX_b=pX_b: e.matmul(ps4(pX_b)[:, hh, :], lhsT=Pn[:, hs_], rhs=X[:, hs_], start=True, stop=False),
                                 reads=[Pnr, Xr], writes=[pX_r])
                            P.op("pe", lambda e, hh=hh, hs_=hs_, X=X, pX_b=pX_b: e.matmul(ps4(pX_b)[:, hh, :], lhsT=identb[:], rhs=X[:, hs_], start=False, stop=True),
                                 reads=[Xr], writes=[pX_r])
                        Xn, Xnr = tbG.next()
                        P.op("act" if n_ % 2 else "dve",
                             (lambda e, Xn=Xn, pX_b=pX_b: e.activation(out=Xn[:], in_=pX_b[:], func=AF.Copy)) if n_ % 2 else
                             (lambda e, Xn=Xn, pX_b=pX_b: e.tensor_copy(out=Xn[:], in_=pX_b[:])), reads=[pX_r], writes=[Xnr])
                        X, Xr = Xn, Xnr
                        yield
                        Pm, Pmr = Pn, Pnr
                        if n_ < 6:
                            PT, PTr = PTn, PTnr
                    yield
                    pKS_b, pKS_r = psum("gdb")
                    pO1_b, pO1_r = psum("gdb")
                    for hh in range(4):
                        hs_ = slice(hh * 128, (hh + 1) * 128)
                        P.op("pe", lambda e, hh=hh, hs_=hs_: e.matmul(ps4(pKS_b)[:, hh, :], lhsT=khT_[:, hs_], rhs=Sgb[:, l, hs_], start=True, stop=True),
                             reads=[khTr, Sgb_reg[l]], writes=[pKS_r])
                        P.op("pe", lambda e, hh=hh, hs_=hs_: e.matmul(ps4(pO1_b)[:, hh, :], lhsT=qsT_[:, hs_], rhs=Sgb[:, l, hs_], start=True, stop=True),
                             reads=[qsTr, Sgb_reg[l]], writes=[pO1_r])
                    r_, rr = gd["r"]
                    for hh in range(4):
                        hs_ = slice(hh * 128, (hh + 1) * 128)
                        P.op("dve", lambda e, hh=hh, hs_=hs_: e.scalar_tensor_tensor(out=r_[:, hs_], in0=ps4(pKS_b)[:, hh, :], scalar=eg[:, 12 + hh:13 + hh], in1=vs[:, hs_],
                                                                                    op0=ALU.mult, op1=ALU.add), reads=[pKS_r, egr, vsr], writes=[rr])
                    t1, t1r = tfG.next()
                    head_scale("dve", t1[:].rearrange("p (h n) -> p h n", h=4), ps4(pO1_b), eg[:, 0:4], [pO1_r, egr], [t1r])
                    yield
                    pV_b, pV_r = psum("gdb")
                    for hh in range(4):
                        hs_ = slice(hh * 128, (hh + 1) * 128)
                        P.op("pe", lambda e, hh=hh, hs_=hs_, X=X: e.matmul(ps4(pV_b)[:, hh, :], lhsT=X[:, hs_], rhs=r_[:, hs_], start=True, stop=True),
                             reads=[Xr, rr], writes=[pV_r])
                    vn, vnr = gd["vn"]
                    head_scale("dve", v4(vn), ps4(pV_b), g1[:, 0:4], [pV_r, g1r], [vnr])
                    yield
                    pO2_b, pO2_r = psum("gdb")
                    pS_b, pS_r = psum("gdb")
                    for hh in range(4):
                        hs_ = slice(hh * 128, (hh + 1) * 128)
                        P.op("pe", lambda e, hh=hh, hs_=hs_: e.matmul(ps4(pO2_b)[:, hh, :], lhsT=qkT[:, hs_], rhs=vn[:, hs_], start=True, stop=True),
                             reads=[qkTr, vnr], writes=[pO2_r])
                        P.op("pe", lambda e, hh=hh, hs_=hs_: e.matmul(ps4(pS_b)[:, hh, :], lhsT=kt[:, hs_], rhs=vn[:, hs_], start=True, stop=True),
                             reads=[ktr, vnr], writes=[pS_r])
                    yield
                    og, ogr = tfG.next()
                    P.op("dve", lambda e: e.tensor_tensor(out=og[:], in0=pO2_b[:], in1=t1[:], op=ALU.add), reads=[pO2_r, t1r], writes=[ogr])
                    for hh in range(4):
                        hs_ = slice(hh * 128, (hh + 1) * 128)
                        P.op("dve", lambda e, hh=hh, hs_=hs_: e.scalar_tensor_tensor(out=Sg[:, l, hs_], in0=Sg[:, l, hs_], scalar=eg[:, 8 + hh:9 + hh], in1=ps4(pS_b)[:, hh, :],
                                                                                    op0=ALU.mult, op1=ALU.add), reads=[pS_r, egr, Sg_reg[l]], writes=[Sg_reg[l]])
                    P.op("act", lambda e: e.activation(out=Sgb[:, l, :], in_=Sg[:, l, :], func=AF.Copy), reads=[Sg_reg[l]], writes=[Sgb_reg[l]])

                    yield
                    head_norm(t, og[:].rearrange("p (h n) -> p h n", h=4), [ogr], zw[:, t, :], zw_reg[t], 0, tfG, tsG)
                    yield


                def gdn_strand():
                    yield from gdn_A(0)
                    for t in range(NT):
                        subs = [gdn_DB(t)] + ([gdn_A(t + 1)] if t + 1 < NT else [])
                        while subs:
                            for g_ in list(subs):
                                try:
                                    next(g_)
                                except StopIteration:
                                    subs.remove(g_)
                            yield

                def hgrn_strand():
                    s_q = wget(L_qh)
                    w3q = s_q.t[:].rearrange("p (a b) -> p a b", a=8)
                    for hh in range(4):
                        def ev(bk, br, hh=hh):
                            P.op("act", lambda e: e.activation(out=qhs[:, hh, :], in_=bk[:], func=AF.Silu), reads=[br], writes=[qhs_reg[hh]])
                        mm_fm(s_q, w3q, hh, ev)
                        yield
                    wdone(L_qh)
                    s_f = wget(L_f)
                    w3f = s_f.t[:].rearrange("p (a b) -> p a b", a=8)
                    for hh in range(4):
                        def ev(bk, br, hh=hh):
                            lcol = l * 4 + hh
                            sg, sgr = hgf["sg"]
                            lf, lfr = hgf["lf"]
                            b_, b_r = hgf["b"]
                            bm, bmr = hgf["bm"]
                            bt, btr = hgf["bt"]
                            P.op("act", lambda e: e.activation(out=sg[:], in_=bk[:], func=AF.Sigmoid), reads=[br], writes=[sgr])
                            P.op("dve", lambda e: e.tensor_scalar(out=sg[:], in0=sg[:], scalar1=oml[:, lcol:lcol + 1], scalar2=lb[:, lcol:lcol + 1],
                                                                  op0=ALU.mult, op1=ALU.add), reads=[sgr], writes=[sgr])
                            P.op("act", lambda e: e.activation(out=lf[:], in_=sg[:], func=AF.Ln), reads=[sgr], writes=[lfr])
                            P.op("pool", lambda e: e.tensor_scalar(out=sg[:], in0=sg[:], scalar1=-1.0, scalar2=1.0, op0=ALU.mult, op1=ALU.add),
                                 reads=[sgr, lfr], writes=[sgr])
                            P.op("dve", lambda e: e.tensor_tensor_scan(out=b_[:], data0=rst, data1=lf[:], initial=0.0, op0=ALU.mult, op1=ALU.add),
                                 reads=[lfr], writes=[b_r])
                            b3 = b_[:].rearrange("p (c n) -> p c n", n=64)
                            eb, ebr = lf, lfr
                            P.op("act", lambda e: e.activation(out=eb[:], in_=b_[:], func=AF.Exp), reads=[b_r], writes=[ebr])
                            eb4 = eb[:].rearrange("p (t c n) -> p t c n", t=NT, c=2)
                            q4 = qhs[:, hh, :].rearrange("p (t c n) -> p t c n", t=NT, c=2)
                            P.op("dve", lambda e: e.tensor_tensor(out=qh0[:, hh, :, 0:64], in0=q4[:, :, 0, :], in1=eb4[:, :, 0, :], op=ALU.mult),
                                 reads=[ebr, qhs_reg[hh]], writes=[hg_reg[hh]])
                            P.op("dve", lambda e: e.tensor_tensor(out=qh1[:, hh, :, 64:128], in0=q4[:, :, 1, :], in1=eb4[:, :, 1, :], op=ALU.mult),
                                 reads=[ebr, qhs_reg[hh]], writes=[hg_reg[hh]])
                            P.op("act", lambda e: e.activation(out=dS[:, hh, :], in_=b3[:, :, 63], func=AF.Exp), reads=[b_r], writes=[hg_reg[hh]])
                            bm3 = bm[:].rearrange("p (c n) -> p c n", n=64)
                            bt3 = bt[:].rearrange("p (c n) -> p c n", n=64)
                            P.op("dve", lambda e: e.tensor_tensor(out=bm3, in0=b3, in1=b3[:, :, 31:32].to_broadcast([128, 8, 64]), op=ALU.subtract),
                                 reads=[b_r], writes=[bmr])
                            P.op("pool", lambda e: e.tensor_tensor(out=bt3, in0=b3, in1=b3[:, :, 63:64].to_broadcast([128, 8, 64]), op=ALU.subtract),
                                 reads=[b_r], writes=[btr])
                            P.op("act", lambda e: e.activation(out=bt[:], in_=bt[:], func=AF.Exp, scale=-1.0), reads=[btr], writes=[btr])
                            P.op("dve", lambda e: e.tensor_tensor(out=khT[:, hh, :], in0=sg[:], in1=bt[:], op=ALU.mult),
                                 reads=[btr, sgr], writes=[hg_reg[hh]])
                            P.op("act", lambda e: e.activation(out=bt[:], in_=bm[:], func=AF.Exp), reads=[bmr, btr], writes=[btr])
                            P.op("dve", lambda e: e.tensor_tensor(out=qtT[:, hh, :], in0=qhs[:, hh, :], in1=bt[:], op=ALU.mult),
                                 reads=[btr, qhs_reg[hh]], writes=[hg_reg[hh]] + AL_UT)
                            P.op("act", lambda e: e.activation(out=bm[:], in_=bm[:], func=AF.Exp, scale=-1.0), reads=[bmr], writes=[bmr])
                            P.op("dve", lambda e: e.tensor_tensor(out=ktT[:, hh, :], in0=sg[:], in1=bm[:], op=ALU.mult),
                                 reads=[bmr, sgr], writes=[hg_reg[hh]] + AL_UT)
                        mm_fm(s_f, w3f, hh, ev)
                        yield
                    wdone(L_f)
                    s_i = wget(L_i)
                    w3i = s_i.t[:].rearrange("p (a b) -> p a b", a=8)
                    for t in range(NT):
                        def ev(bk, br, t=t):
                            P.op("act", lambda e: e.activation(out=iv[:, t, :], in_=bk[:], func=AF.Copy), reads=[br], writes=[iv_reg[t]])
                        mm_tm(s_i, w3i, t, 512, ev)
                        yield
                    wdone(L_i)
                    s = wget(L_g)
                    w3 = s.t[:].rearrange("p (a b) -> p a b", a=8)
                    for t in range(NT):
                        def ev(bk, br, t=t):
                            zt, zr = tf.next()
                            P.op("act", lambda e: e.activation(out=zt[:], in_=bk[:], func=AF.Silu), reads=[br], writes=[zr])
                            P.op("pool", lambda e: e.tensor_tensor(out=gw[:, t, :], in0=zt[:], in1=hwb[:, (l * 2 + 1) * 512:(l * 2 + 2) * 512], op=ALU.mult),
                                 reads=[zr], writes=[gw_reg[t]])
                        mm_tm(s, w3, t, 512, ev)
                        yield
                    wdone(L_g)

                    for t in range(NT):
                        tsl = slice(t * 128, (t + 1) * 128)
                        mark(f"b{blk}l{l}:hgrn{t}")
                        pA_b, pA_r = psum("dense")
                        pKH_b, pKH_r = psum("dense")
                        for hh in range(4):
                            P.op("pe", lambda e, hh=hh: e.matmul(ps4(pA_b)[:, hh, :], lhsT=ktT[:, hh, tsl], rhs=qtT[:, hh, tsl], start=True, stop=True),
                                 reads=[hg_reg[hh]], writes=[pA_r])
                            P.op("pe", lambda e, hh=hh: e.transpose(psb(pKH_b)[:, hh, :], khT[:, hh, tsl], identb[:]), reads=[hg_reg[hh]], writes=[pKH_r])
                        yield
                        at_, atr = tf.next()
                        P.op("act", lambda e: e.activation(out=at_[:], in_=pA_b[:], func=AF.Copy), reads=[pA_r], writes=[atr])
                        aT, aTr = tbf.next()
                        P.op("pool", lambda e: e.affine_select(out=v4(aT), in_=at_[:].rearrange("p (h n) -> p h n", h=4), pattern=[[0, 4], [1, 128]],
                                                               compare_op=ALU.is_ge, fill=0.0, base=0, channel_multiplier=-1), reads=[atr], writes=[aTr])
                        P.op("pool", lambda e: e.memset(v4(aT)[0:64, :, 64:128], 0.0), reads=[aTr], writes=[aTr])
                        khm, khmr = tbf.next()
                        P.op("dve", lambda e: e.tensor_copy(out=v4(khm), in_=psb(pKH_b)), reads=[pKH_r], writes=[khmr])
                        Sprev, Sprevr = Shb[l][Shb_cur[l]]
                        yield
                        pU_b, pU_r = psum("dense")
                        for hh in range(4):
                            hs_ = slice(hh * 128, (hh + 1) * 128)
                            P.op("pe", lambda e, hh=hh, hs_=hs_: e.matmul(ps4(pU_b)[:, hh, :], lhsT=khm[0:64, hs_], rhs=iv[0:64, t, hs_], start=True, stop=True),
                                 reads=[khmr, iv_reg[t]], writes=[pU_r])
                        for hh in range(4):
                            hs_ = slice(hh * 128, (hh + 1) * 128)
                            P.op("dve", lambda e, hh=hh, hs_=hs_: e.scalar_tensor_tensor(out=Sh[:, l, hs_], in0=Sh[:, l, hs_], scalar=dS[:, hh, 2 * t:2 * t + 1], in1=ps4(pU_b)[:, hh, :],
                                                                                        op0=ALU.mult, op1=ALU.add), reads=[pU_r, hg_reg[hh], Sh_reg[l]], writes=[Sh_reg[l]])
                        Smid, Smidr = Shb[l][1 - Shb_cur[l]]
                        P.op("act", lambda e: e.activation(out=Smid[:], in_=Sh[:, l, :], func=AF.Copy), reads=[Sh_reg[l]], writes=[Smidr])
                        yield
                        pU2_b, pU2_r = psum("dense")
                        for hh in range(4):
                            hs_ = slice(hh * 128, (hh + 1) * 128)
                            P.op("pe", lambda e, hh=hh, hs_=hs_: e.matmul(ps4(pU2_b)[:, hh, :], lhsT=khm[64:128, hs_], rhs=iv[64:128, t, hs_], start=True, stop=True),
                                 reads=[khmr, iv_reg[t]], writes=[pU2_r])
                        for hh in range(4):
                            hs_ = slice(hh * 128, (hh + 1) * 128)
                            P.op("dve", lambda e, hh=hh, hs_=hs_: e.scalar_tensor_tensor(out=Sh[:, l, hs_], in0=Sh[:, l, hs_], scalar=dS[:, hh, 2 * t + 1:2 * t + 2], in1=ps4(pU2_b)[:, hh, :],
                                                                                        op0=ALU.mult, op1=ALU.add), reads=[pU2_r, hg_reg[hh], Sh_reg[l], Smidr], writes=[Sh_reg[l]])
                        yield
                        pO_b, pO_r = psum("dense")
                        for hh in range(4):
                            hs_ = slice(hh * 128, (hh + 1) * 128)
                            P.op("pe", lambda e, hh=hh, hs_=hs_: e.matmul(ps4(pO_b)[:, hh, :], lhsT=aT[:, hs_], rhs=iv[:, t, hs_], start=True, stop=False),
                                 reads=[aTr, iv_reg[t]], writes=[pO_r])
                            P.op("pe", lambda e, hh=hh, hs_=hs_: e.matmul(ps4(pO_b)[:, hh, :], lhsT=qh0[:, hh, t, :], rhs=Sprev[:, hs_], start=False, stop=False),
                                 reads=[hg_reg[hh], Sprevr], writes=[pO_r])
                            P.op("pe", lambda e, hh=hh, hs_=hs_: e.matmul(ps4(pO_b)[:, hh, :], lhsT=qh1[:, hh, t, :], rhs=Smid[:, hs_], start=False, stop=True),
                                 reads=[hg_reg[hh], Smidr], writes=[pO_r])
                        Snew, Snewr = Sprev, Sprevr
                        P.op("act", lambda e: e.activation(out=Snew[:], in_=Sh[:, l, :], func=AF.Copy), reads=[Sh_reg[l]], writes=[Snewr])
                        yield
                        head_norm(t, ps4(pO_b), [pO_r], gw[:, t, :], gw_reg[t], 512, tf, tsm)

                        yield

                strands = [gdn_strand(), hgrn_strand()]
                while strands:
                    for s_ in list(strands):
                        try:
                            next(s_)
                        except StopIteration:
                            strands.remove(s_)

                mark(f"b{blk}l{l}:wout")
                g_pm = gload(l * 5 + 1)
                to_actT(lambda t: mix[:, t, :], mix_reg, norm=False)

                def post_norm(src_fn, src_regs_fn, gain, t, eng2="pool"):
                    ss_t, ss_r = tsm.next()
                    for hf in range(2):
                        jk, jr = junk.next()
                        P.op("act", lambda e, hf=hf, jk=jk: e.activation(out=jk[:, 0:512], in_=src_fn(hf), func=AF.Square, accum_out=ss_t[:, hf:hf + 1]),
                             reads=src_regs_fn(hf), writes=[jr, ss_r])
                    P.op("dve", lambda e: e.tensor_tensor(out=ss_t[:, 0:1], in0=ss_t[:, 0:1], in1=ss_t[:, 1:2], op=ALU.add), reads=[ss_r], writes=[ss_r])
                    rstd_from_ss(ss_t[:, 0:1], 1, D, ss_r)
                    for hf in range(2):
                        tm_, tmr = tf.next()
                        P.op("dve", lambda e, hf=hf, tm_=tm_: e.scalar_tensor_tensor(out=tm_[:], in0=src_fn(hf), scalar=ss_t[:, 0:1], in1=gain.t[:, hf * 512:(hf + 1) * 512],
                                                                                    op0=ALU.mult, op1=ALU.mult), reads=src_regs_fn(hf) + [ss_r, gain.reg], writes=[tmr])
                        P.op(eng2, lambda e, hf=hf, tm_=tm_: e.tensor_tensor(out=h[:, t, hf * 512:(hf + 1) * 512], in0=h[:, t, hf * 512:(hf + 1) * 512], in1=tm_[:], op=ALU.add),
                             reads=[tmr, h_reg[t]], writes=[h_reg[t]])

                s_o0 = wget(L_o[0])
                s_o1 = wget(L_o[1])
                for t in range(NT):
                    res = []
                    for hf, s_ in enumerate((s_o0, s_o1)):
                        mm_tm(s_, s_.t[:].rearrange("p (a b) -> p a b", a=8), t, 512, lambda bk, br: res.append((bk, br)),
                              pool=("dense" if t % 2 == 0 else "mix"))
                    post_norm(lambda hf: res[hf][0][:], lambda hf: [res[hf][1]], g_pm, t)
                wdone(L_o[0])
                wdone(L_o[1])

                mark(f"b{blk}l{l}:mlp")
                g_pl = gload(l * 5 + 2)
                to_actT(lambda t: h[:, t, :], h_reg, gain=g_pl)
                for g in range(8):
                    s_u = wget(L_ud[g][0])
                    w3u = s_u.t[:].rearrange("p (a b) -> p a b", a=8)
                    u_t, u_r = uT_items[uT_i[0] % 2]
                    uT_i[0] += 1
                    for hc in range(4):
                        def ev(bk, br, hc=hc):
                            sq_t, sq_r = tf.next()
                            P.op("act", lambda e: e.activation(out=sq_t[:], in_=bk[:], func=AF.Square), reads=[br], writes=[sq_r])
                            P.op("dve", lambda e: e.scalar_tensor_tensor(out=u_t[:, hc, :], in0=bk[:], scalar=0.0, in1=sq_t[:], op0=ALU.is_gt, op1=ALU.mult),
                                 reads=[br, sq_r], writes=[u_r] + hg_reg)
                        mm_fm(s_u, w3u, hc, ev)
                    wdone(L_ud[g][0])
                    s_d = wget(L_ud[g][1])
                    w3d = s_d.t[:].rearrange("p (a b) -> p a b", a=4)
                    for t in range(NT):
                        for hf in range(2):
                            def ev(bk, br, t=t, hf=hf):
                                dst = yacc[:, t, hf * 512:(hf + 1) * 512]
                                if g == 0:
                                    P.op("act", lambda e: e.activation(out=dst, in_=bk[:], func=AF.Copy), reads=[br], writes=[yacc_reg[t]] + AL_MIX)
                                else:
                                    P.op("dve", lambda e: e.tensor_tensor(out=dst, in0=bk[:], in1=dst, op=ALU.add), reads=[br, yacc_reg[t]], writes=[yacc_reg[t]])
                            mm_tm(s_d, w3d, t, 512, ev, kch=4, lhs=lambda kc, t=t: u_t[:, kc, t * 128:(t + 1) * 128], lhs_regs=[u_r], col0=hf * 512, pool="mix")
                    wdone(L_ud[g][1])
                g_pml = gload(l * 5 + 3)
                for t in range(NT):
                    post_norm(lambda hf, t=t: yacc[:, t, hf * 512:(hf + 1) * 512], lambda hf, t=t: [yacc_reg[t]], g_pml, t)

                mark(f"b{blk}l{l}:ple")
                g_ple = gload(l * 5 + 4)
                pbuf.n += 1
                P.dma("pool", lambda e, l=l: e.dma_start(out=pbuf.t[:], in_=p_d[l, tok0:tok0 + TB, :].rearrange("(t p) e -> p t e", p=128)),
                      pbuf.sem, pbuf.n, writes=[pbuf.reg])
                s_p = wget(L_p)
                s_g0 = wget(L_g0)
                s_g1 = wget(L_g1)
                to_actT(lambda t: h[:, t, :], h_reg, norm=False)
                w3p = s_p.t[:, 0:2048].rearrange("p (a b) -> p a b", a=2)
                for t in range(NT):
                    pb, pr = psum("mix")
                    for c in range(2):
                        P.op("pe", lambda e, c=c, pb=pb: e.transpose(psb(pb)[:, c, :], pbuf.t[:, t, c * 128:(c + 1) * 128], identb[:]), reads=[pbuf.reg], writes=[pr])
                    pT_, pTr = tbf.next()
                    P.op("dve", lambda e, pb=pb, pT_=pT_: e.tensor_copy(out=pT_[:, 0:256], in_=pb[:].bitcast(BF16)[:, 0:256]), reads=[pr], writes=[pTr])
                    eres = []
                    for hf in range(2):
                        mm_tm(s_p, w3p, t, 512, lambda bk, br: eres.append((bk, br)), kch=2,
                              lhs=lambda kc, pT_=pT_: pT_[:, kc * 128:(kc + 1) * 128], lhs_regs=[pTr], col0=hf * 512,
                              pool=("dense" if t % 2 == 0 else "mix"))
                    ss_t, ss_r = tsm.next()
                    for hf in range(2):
                        jk, jr = junk.next()
                        P.op("act", lambda e, hf=hf, jk=jk: e.activation(out=jk[:, 0:512], in_=eres[hf][0][:], func=AF.Square, accum_out=ss_t[:, hf:hf + 1]),
                             reads=[eres[hf][1]], writes=[jr, ss_r])
                    P.op("dve", lambda e: e.tensor_tensor(out=ss_t[:, 0:1], in0=ss_t[:, 0:1], in1=ss_t[:, 1:2], op=ALU.add), reads=[ss_r], writes=[ss_r])
                    rstd_from_ss(ss_t[:, 0:1], 1, D, ss_r)
                    en = []
                    for hf in range(2):
                        tm_, tmr = tf.next()
                        P.op("dve", lambda e, hf=hf, tm_=tm_: e.scalar_tensor_tensor(out=tm_[:], in0=eres[hf][0][:], scalar=ss_t[:, 0:1], in1=g_ple.t[:, hf * 512:(hf + 1) * 512],
                                                                                    op0=ALU.mult, op1=ALU.mult), reads=[eres[hf][1], ss_r, g_ple.reg], writes=[tmr])
                        en.append((tm_, tmr))
                    for hf, s_ in enumerate((s_g0, s_g1)):
                        def ev(bk, br, hf=hf):
                            gt, gtr = tf.next()
                            P.op("act", lambda e: e.activation(out=gt[:], in_=bk[:], func=AF.Sigmoid), reads=[br], writes=[gtr])
                            P.op("pool", lambda e: e.tensor_tensor(out=gt[:], in0=gt[:], in1=en[hf][0][:], op=ALU.mult), reads=[gtr, en[hf][1]], writes=[gtr])
                            P.op("pool", lambda e: e.tensor_tensor(out=h[:, t, hf * 512:(hf + 1) * 512], in0=h[:, t, hf * 512:(hf + 1) * 512], in1=gt[:], op=ALU.add),
                                 reads=[gtr, h_reg[t]], writes=[h_reg[t]])
                        mm_tm(s_, s_.t[:].rearrange("p (a b) -> p a b", a=8), t, 512, ev, pool=("mix" if t % 2 == 0 else "dense"))
                wdone(L_p)
                wdone(L_g0)
                wdone(L_g1)
            for t in range(NT):
                st_n[t] += 1
                P.dma("sp", lambda e, t=t: e.dma_start(out=out_d[tok0 + t * 128: tok0 + (t + 1) * 128, :], in_=h[:, t, :]), st_sem[t], st_n[t],
                      reads=[h_reg[t]])
        P.wait_all("sp", [(st_sem[t], st_n[t] * 16, "dma") for t in range(NT)])
        mark("end")
        P.emit()
    nc._marks = marks
    return nc


def host_consts():
    idx = np.arange(128)
    U = (idx[:, None] <= idx[None, :]).astype(np.float32)
    SU = (idx[:, None] > idx[None, :]).astype(np.float32)
    ones = np.ones((128, 128), np.float32)
    rst = np.ones((128, 512), np.float32)
    rst[:, ::64] = 0.0
    return {
        "c_identb": np.eye(128, dtype=np.float32).astype(ml_dtypes.bfloat16),
        "c_f32": np.ascontiguousarray(np.concatenate([U, SU, ones, rst, np.eye(128, dtype=np.float32),
                                                      np.where(idx[None, :] >= idx[:, None], -30000.0, 0.0).astype(np.float32),
                                                      np.where(idx[None, :] <= idx[:, None], -30000.0, 0.0).astype(np.float32)], axis=1)),
    }


def make_in_maps(inp):
    f = lambda a: np.ascontiguousarray(np.asarray(a, dtype=np.float32))
    NL = NLAYER
    gains = np.stack([np.stack([f(inp[k])[l] for k in ("pre_mix_norm", "post_mix_norm", "pre_mlp_norm", "post_mlp_norm", "ple_norm")])
                      for l in range(NL)]).reshape(NL * 5, D)
    hw = np.stack([np.stack([np.tile(f(inp["gdn_norm"])[l], 4), np.tile(f(inp["hgrn_norm"])[l], 4)]) for l in range(NL)]).reshape(NL * 2, 512)
    sm = np.stack([np.stack([f(inp["gdn_a_log"])[l], f(inp["gdn_dt_bias"])[l]]) for l in range(NL)]).reshape(NL * 2, 4)
    cw = f(inp["gdn_conv"]).reshape(NL, 4, 12, 128).transpose(3, 0, 2, 1).reshape(128, NL * 48)
    lbl = f(inp["hgrn_lb_logits"]).reshape(NL, 4, 128).transpose(2, 0, 1).reshape(128, NL * 4)
    shared = {
        "w_in": f(inp["w_in"]), "w_out": f(inp["w_out"]), "w_up": f(inp["w_mlp_up"]), "w_dn": f(inp["w_mlp_down"]),
        "w_ple": f(inp["w_ple_proj"]), "w_gate": f(inp["w_ple_gate"]),
        "gains": np.ascontiguousarray(gains), "hw": np.ascontiguousarray(hw), "sm": np.ascontiguousarray(sm),
        "convw": np.ascontiguousarray(cw), "lbl": np.ascontiguousarray(lbl),
    }
    shared.update(host_consts())
    x = f(inp["x"])
    p = f(inp["p"])
    maps = []
    for b in range(x.shape[0]):
        m = dict(shared)
        m["x"] = np.ascontiguousarray(x[b])
        m["p"] = np.ascontiguousarray(p[:, b])
        maps.append(m)
    return maps


def kernel(**inputs):
    nc = build()
    maps = make_in_maps(inputs)
    res = run_bass_kernel_spmd(nc, maps, core_ids=list(range(len(maps))))
    return np.stack([np.asarray(r["out"], dtype=np.float32) for r in res.results], axis=0)
```

```python
import contextlib
import numpy as np
import ml_dtypes
import concourse.bass as bass
import concourse.mybir as mybir
from concourse.bass_utils import run_bass_kernel_spmd

F32 = mybir.dt.float32
BF16 = mybir.dt.bfloat16
AF = mybir.ActivationFunctionType
ALU = mybir.AluOpType
AX = mybir.AxisListType

D = 1024
T = 4096
NLAYER = 2
TB = 512
NT = TB // 128
EPS = 1e-6
DFF = 4096
PLE = 256
INC = 4104


class Region:
    __slots__ = ("name", "w", "r", "excl")

    def __init__(self, name, excl=False):
        self.name = name
        self.w = None
        self.r = []
        self.excl = excl


class _Rec:
    def __getattr__(self, name):
        def f(*a, **k):
            return (name, a, k)
        return f


_REC = _Rec()
DEBUG_MAP = None


class Prog:
    ENG = ("pe", "act", "dve", "pool", "sp")

    def __init__(self, nc, stack):
        self.nc = nc
        self.stack = stack
        self.q = {e: [] for e in self.ENG}
        self.sem = {e: stack.enter_context(nc.semaphore("s_" + e)) for e in self.ENG}
        self.cnt = {e: 0 for e in self.ENG}
        self.seen = {e: {} for e in self.ENG}

    def new_sem(self, name):
        return self.stack.enter_context(self.nc.semaphore(name))

    def _deps(self, eng, reads, writes):
        deps = {}

        def add(tok):
            if tok is None:
                return
            sem, val, teng = tok
            if teng == "pe" and eng == "pe":
                return
            k = id(sem)
            if k not in deps or deps[k][1] < val:
                deps[k] = (sem, val)
        for b in reads:
            add(b.w)
            if b.excl:
                for t in b.r:
                    if t[2] != eng:
                        add(t)
        for b in writes:
            add(b.w)
            for t in b.r:
                add(t)
        out = []
        seen = self.seen[eng]
        for k, (sem, val) in deps.items():
            if k in seen and seen[k] >= val:
                continue
            seen[k] = val
            out.append((sem, val))
        return out

    def _mark(self, tok, reads, writes):
        for b in reads:
            b.r.append(tok)
            if len(b.r) > 64:
                best = {}
                for t in b.r:
                    k = id(t[0])
                    if k not in best or best[k][1] < t[1]:
                        best[k] = t
                b.r = list(best.values())
        for b in writes:
            b.w = tok
            b.r = []

    def op(self, eng, fn, reads=(), writes=()):
        waits = self._deps(eng, reads, writes)
        self.cnt[eng] += 1
        tok = (self.sem[eng], self.cnt[eng], eng)
        self.q[eng].append((waits, fn(_REC), self.sem[eng], 1))
        self._mark(tok, reads, writes)
        return tok

    def dma(self, eng, fn, dsem, dcount, reads=(), writes=()):
        waits = self._deps(eng, reads, writes)
        tok = (dsem, dcount * 16, "dma")
        self.q[eng].append((waits, fn(_REC), dsem, 16))
        self._mark(tok, reads, writes)
        return tok

    def wait_all(self, eng, toks):
        self.q[eng].append(([(s, v) for (s, v, _) in toks], None, None, 0))

    def emit(self):
        nc = self.nc
        engmap = {"pe": "tensor", "act": "scalar", "dve": "vector", "pool": "gpsimd", "sp": "sync"}
        with nc.Block() as block:
            for e in self.ENG:
                q = self.q[e]

                def body(engine, q=q):
                    for waits, fn, sem, inc in q:
                        for (s, v) in waits:
                            engine.wait_ge(s, v)
                        if fn is not None:
                            name, a, k = fn
                            ins = getattr(engine, name)(*a, **k)
                            ins.then_inc(sem, inc)
                            if DEBUG_MAP is not None:
                                try:
                                    DEBUG_MAP.append((str(getattr(ins, "name", None) or getattr(getattr(ins, "ins", None), "name", None)), e, name,
                                                      str(k.get("out", a[0] if a else ""))[:120]))
                                except Exception as ex:
                                    DEBUG_MAP.append(("?", e, name, str(ex)))
                getattr(block, engmap[e])(body)


class Ring:
    def __init__(self, alloc, name, shape, dtype, n):
        self.items = [(alloc(f"{name}{i}", shape, dtype), Region(f"{name}{i}")) for i in range(n)]
        self.i = 0

    def next(self):
        it = self.items[self.i % len(self.items)]
        self.i += 1
        return it


class DmaBuf:
    def __init__(self, P, tensor, name):
        self.t = tensor
        self.reg = Region(name)
        self.sem = P.new_sem("d_" + name)
        self.n = 0


def build(nblk=T // TB, layers=(0, 1), dbg=False):
    nc = bass.Bass("TRN2", target_bir_lowering=False)
    NL = NLAYER
    dram = lambda n, s, d, k="ExternalInput": nc.dram_tensor(n, s, d, kind=k).ap()
    x_d = dram("x", [T, D], F32)
    p_d = dram("p", [NL, T, PLE], F32)
    w_in_d = dram("w_in", [NL, D, INC], F32)
    w_out_d = dram("w_out", [NL, D, D], F32)
    w_up_d = dram("w_up", [NL, D, DFF], F32)
    w_dn_d = dram("w_dn", [NL, DFF, D], F32)
    w_ple_d = dram("w_ple", [NL, PLE, D], F32)
    w_gate_d = dram("w_gate", [NL, D, D], F32)
    gains_d = dram("gains", [NL * 5, D], F32)
    hw_d = dram("hw", [NL * 2, 512], F32)
    sm_d = dram("sm", [NL * 2, 4], F32)
    convw_d = dram("convw", [128, NL * 48], F32)
    lbl_d = dram("lbl", [128, NL * 4], F32)
    c_identb = dram("c_identb", [128, 128], BF16)
    c_f32 = dram("c_f32", [128, 6 * 128 + 512], F32)
    out_d = dram("out", [T, D], F32, "ExternalOutput")
    dbg_d = dram("dbg", [128, 4096], F32, "ExternalOutput") if dbg else None

    with contextlib.ExitStack() as st:
        P = Prog(nc, st)
        sb = lambda n, s, d: st.enter_context(nc.sbuf_tensor(n, s, d))
        banks = [st.enter_context(nc.psum_tensor(f"bank{i}", [128, 512], F32)) for i in range(8)]
        bank_reg = [Region(f"bank{i}", excl=True) for i in range(8)]
        pool_i = {"dense": 0, "mix": 0, "ga": 0, "gdb": 0}
        pool_banks = {"dense": (0, 1), "mix": (2, 3, 4, 5, 6, 7), "ga": (2, 3, 4), "gdb": (5, 6, 7)}

        def psum(kind):
            bl = pool_banks[kind]
            i = bl[pool_i[kind] % len(bl)]
            pool_i[kind] += 1
            return banks[i], bank_reg[i]

        def psb(bank):
            return bank[:].bitcast(BF16)[:, 0:512].rearrange("p (h n) -> p h n", h=4)

        def ps4(bank):
            return bank[:].rearrange("p (h n) -> p h n", h=4)

        identb = sb("identb", [128, 128], BF16)
        cf = sb("cf", [128, 6 * 128 + 512], F32)
        Umat = cf[:, 0:128]
        SUmat = cf[:, 128:256]
        ones = cf[:, 256:384]
        rst = cf[:, 384:896]
        identf = cf[:, 896:1024]
        NEGU = cf[:, 1024:1152]
        NEGL = cf[:, 1152:1280]
        convw = sb("convw_sb", [128, NL * 48], F32)
        lbl = sb("lbl_sb", [128, NL * 4], F32)
        smb = sb("smb", [128, NL * 2 * 4], F32)
        hwb = sb("hwb", [128, NL * 2 * 512], F32)
        negA = sb("negA", [128, NL * 4], F32)
        lb = sb("lb", [128, NL * 4], F32)
        oml = sb("oml", [128, NL * 4], F32)
        cst_sem = P.new_sem("cst")
        ncst = 0
        cst_toks = []

        def cload(dst, src):
            nonlocal ncst
            ncst += 1
            cst_toks.append(P.dma("sp", lambda e: e.dma_start(out=dst, in_=src), cst_sem, ncst))
        cload(identb[:], c_identb)
        cload(cf[:], c_f32)
        cload(convw[:], convw_d)
        cload(lbl[:], lbl_d)
        cload(smb[:], sm_d.rearrange("a b -> (a b)").partition_broadcast(128))
        cload(hwb[:], hw_d.rearrange("a b -> (a b)").partition_broadcast(128))
        for e in ("pe", "act", "dve", "pool"):
            P.wait_all(e, [cst_toks[-1]])
        R_c = Region("consts_derived")
        for l in range(NL):
            P.op("act", lambda e, l=l: e.activation(out=negA[:, l * 4:l * 4 + 4], in_=smb[:, l * 8:l * 8 + 4], func=AF.Exp),
                 writes=[R_c])
        P.op("dve", lambda e: e.tensor_scalar(out=negA[:], in0=negA[:], scalar1=-1.0, scalar2=None, op0=ALU.mult),
             reads=[R_c], writes=[R_c])
        elb = sb("elb", [128, NL * 4], F32)
        slb = sb("slb", [128, 4], F32)
        P.op("act", lambda e: e.activation(out=elb[:], in_=lbl[:], func=AF.Exp), writes=[R_c])
        P.op("dve", lambda e: e.tensor_copy(out=slb[:], in_=elb[:, 0:4]), reads=[R_c], writes=[R_c])
        for l in range(1, NL):
            P.op("dve", lambda e, l=l: e.tensor_tensor(out=slb[:], in0=slb[:], in1=elb[:, l * 4:l * 4 + 4], op=ALU.add),
                 reads=[R_c], writes=[R_c])
        P.op("dve", lambda e: e.reciprocal(out=slb[:], in_=slb[:]), reads=[R_c], writes=[R_c])
        P.op("dve", lambda e: e.memset(lb[:, 0:4], 0.0), reads=[R_c], writes=[R_c])
        for l in range(1, NL):
            P.op("dve", lambda e, l=l: e.tensor_tensor(out=elb[:, l * 4:l * 4 + 4], in0=elb[:, l * 4:l * 4 + 4], in1=slb[:], op=ALU.mult),
                 reads=[R_c], writes=[R_c])
            P.op("dve", lambda e, l=l: e.tensor_tensor(out=lb[:, l * 4:l * 4 + 4], in0=lb[:, (l - 1) * 4:l * 4], in1=elb[:, l * 4:l * 4 + 4], op=ALU.add),
                 reads=[R_c], writes=[R_c])
        P.op("dve", lambda e: e.tensor_scalar(out=oml[:], in0=lb[:], scalar1=-1.0, scalar2=1.0, op0=ALU.mult, op1=ALU.add),
             reads=[R_c], writes=[R_c])
        for e in ("act", "pool"):
            P.op(e, (lambda en: en.memset(slb[:, 0:1], 0.0)) if e == "pool" else
                 (lambda en: en.activation(out=slb[:, 0:1], in_=slb[:, 0:1], func=AF.Copy)), reads=[R_c], writes=[R_c])

        h = sb("h", [128, NT, D], F32)
        h_reg = [Region(f"h{t}") for t in range(NT)]
        h_ld = [DmaBuf(P, None, f"hld{t}") for t in range(NT)]
        actT = sb("actT", [128, 8, TB], BF16)
        actT_reg = [Region(f"actT{t}") for t in range(NT)]
        big = sb("big", [128, 5136], F32)
        yacc = big[:, 0:4096].rearrange("p (t d) -> p t d", t=NT)
        yacc_reg = [Region(f"yacc{t}") for t in range(NT)]
        mix_reg = [Region(f"mix{t}") for t in range(NT)]
        NSLOT = 3
        wslots = [DmaBuf(P, sb(f"wslot{i}", [128, 4096], BF16), f"wslot{i}") for i in range(NSLOT)]
        wslot_i = [0]
        wab = DmaBuf(P, sb("wab", [128, 8, 8], BF16), "wab")
        gring = [DmaBuf(P, sb(f"gain{i}", [128, D], F32), f"gain{i}") for i in range(1)]
        gring_i = [0]
        pbuf = DmaBuf(P, sb("pbuf", [128, NT, PLE], BF16), "pbuf")
        out_sem = P.new_sem("outst")
        out_n = [0]
        st_sem = [P.new_sem(f"st{t}") for t in range(NT)]
        st_n = [0] * NT

        raw = big[:, 0:4 * (TB + 3)].rearrange("p (a b) -> p a b", a=4)
        raw_reg = Region("raw")
        halo = sb("halo", [128, NL * 12, 3], F32)
        halo_reg = Region("halo")
        cs = big[:, 2064:2064 + 3072].bitcast(BF16).rearrange("p (a b) -> p a b", a=12)
        cs_reg = [Region(f"cs{g}") for g in range(3)]
        mix = big[:, 0:2048].bitcast(BF16).rearrange("p (t d) -> p t d", t=NT)
        AL_MIX = [raw_reg] + cs_reg + mix_reg
        AL_MLP = yacc_reg + mix_reg
        zw = sb("zw", [128, NT, 512], BF16)
        zw_reg = [Region(f"zw{t}") for t in range(NT)]
        gw = sb("gw", [128, NT, 512], BF16)
        gw_reg = [Region(f"gw{t}") for t in range(NT)]
        iv = sb("iv", [128, NT, 512], BF16)
        iv_reg = [Region(f"iv{t}") for t in range(NT)]
        ab = sb("ab", [128, NT, 8], F32)
        ab_reg = [Region(f"ab{t}") for t in range(NT)]
        qhs = sb("qhs", [128, 4, TB], BF16)
        qhs_reg = [Region(f"qhs{i}") for i in range(4)]
        big2 = sb("big2", [128, 4096], BF16)
        qtT = big2[:, 0:2048].rearrange("p (a b) -> p a b", a=4)
        ktT = big2[:, 2048:4096].rearrange("p (a b) -> p a b", a=4)
        khT = sb("khT", [128, 4, TB], BF16)
        qh0 = sb("qh0", [128, 4, NT, 128], BF16)
        qh1 = sb("qh1", [128, 4, NT, 128], BF16)
        dS = sb("dS", [128, 4, 8], F32)
        hg_reg = [Region(f"hg{i}") for i in range(4)]
        Sg = sb("Sg", [128, NL, 512], F32)
        Sh = sb("Sh", [128, NL, 512], F32)
        Sg_reg = [Region(f"Sg{l}") for l in range(NL)]
        Sh_reg = [Region(f"Sh{l}") for l in range(NL)]
        Sgb = sb("Sgb", [128, NL, 512], BF16)
        Sgb_reg = [Region(f"Sgb{l}") for l in range(NL)]
        Shb = [[(sb(f"Shb{l}_{i}", [128, 512], BF16), Region(f"Shb{l}_{i}")) for i in range(2)] for l in range(NL)]
        Shb_cur = [None] * NL
        uT_items = [(big2[:, 0:2048].rearrange("p (a b) -> p a b", a=4), Region("uT0")),
                    (big2[:, 2048:4096].rearrange("p (a b) -> p a b", a=4), Region("uT1"))]
        uT_i = [0]
        AL_UT = [uT_items[0][1], uT_items[1][1]]
        tf = Ring(sb, "tf", [128, 512], F32, 3)
        tbf = Ring(sb, "tb", [128, 512], BF16, 3)
        tfG = Ring(sb, "tfG", [128, 512], F32, 4)
        tbG = Ring(sb, "tbG", [128, 512], BF16, 8)
        tsG = Ring(sb, "tsG", [128, 16], F32, 8)
        gd = {n: (sb("gd_" + n, [128, 512], BF16), Region("gd_" + n)) for n in
              ("qs", "kh", "kb", "kbT", "EM", "EMT", "ECT", "r", "vn")}
        gdp = [{n: (sb(f"gdp{p_}_" + n, [128, 512], BF16), Region(f"gdp{p_}_" + n)) for n in
                ("khT", "qsT", "kt", "vs", "qkT", "Pm0", "PT0", "X0")} for p_ in range(2)]
        hgf = {n: (sb("hgf_" + n, [128, 512], F32), Region("hgf_" + n)) for n in ("sg", "lf", "b", "bm", "bt")}
        tsm = Ring(sb, "ts", [128, 16], F32, 12)
        hsb = Ring(sb, "hsb", [128, D], BF16, 2)
        junk = hsb
        gU = Ring(sb, "gU", [128, 512], F32, 1)
        gall = sb("gall", [128, NT, 16], F32)
        egall = sb("egall", [128, NT, 16], F32)
        gall_reg = Region("gall")
        identb4 = identb[:].unsqueeze(1).to_broadcast([128, 4, 128])

        R0 = Region("init")
        P.op("pool", lambda e: e.memset(halo[:], 0.0), writes=[halo_reg])
        P.op("pool", lambda e: e.memset(Sg[:], 0.0), writes=Sg_reg)
        P.op("pool", lambda e: e.memset(Sh[:], 0.0), writes=Sh_reg)
        P.op("pool", lambda e: e.memset(Sgb[:], 0.0), writes=Sgb_reg)
        P.op("pool", lambda e: e.memset(qh0[:], 0.0), writes=hg_reg)
        P.op("pool", lambda e: e.memset(qh1[:], 0.0), writes=hg_reg)
        for l in range(NL):
            t_, r_ = Shb[l][0]
            P.op("pool", lambda e, t_=t_: e.memset(t_[:], 0.0), writes=[r_])
            Shb_cur[l] = 0

        class Load:
            def __init__(self, src_ap):
                self.src = src_ap
                self.slot = None
                self.finished = False

        loads = []
        issued = [0]

        def _try_issue():
            while issued[0] < len(loads):
                k = issued[0]
                if k >= NSLOT and not loads[k - NSLOT].finished:
                    return
                ld = loads[k]
                s = wslots[k % NSLOT]
                s.n += 1
                src_ap = ld.src
                n_el = 1
                for v in src_ap.shape[1:]:
                    n_el *= v
                dst = s.t[:, 0:n_el].rearrange("p (a b) -> p a b", a=src_ap.shape[1])
                P.dma("pool", lambda e: e.dma_start(out=dst, in_=src_ap), s.sem, s.n, writes=[s.reg])
                ld.slot = s
                issued[0] += 1

        def wreq(src_ap):
            ld = Load(src_ap)
            loads.append(ld)
            return ld

        def wget(ld):
            _try_issue()
            assert ld.slot is not None, "weight slot ring too small for this access pattern"
            return ld.slot

        def wdone(ld):
            ld.finished = True
            _try_issue()

        def gload(row):
            g = gring[gring_i[0] % 1]
            gring_i[0] += 1
            g.n += 1
            P.dma("sp", lambda e: e.dma_start(out=g.t[:], in_=gains_d[row, :].partition_broadcast(128)), g.sem, g.n,
                  writes=[g.reg])
            return g

        def rstd_from_ss(ss_ap, n, dim, rss):
            P.op("act", lambda e: e.activation(out=ss_ap, in_=ss_ap, func=AF.Ln, scale=1.0 / dim, bias=EPS),
                 reads=[rss], writes=[rss])
            P.op("act", lambda e: e.activation(out=ss_ap, in_=ss_ap, func=AF.Exp, scale=-0.5),
                 reads=[rss], writes=[rss])

        def head_scale(eng, out4, in4, sc, reads, writes, nh=4):
            if eng != "act":
                P.op(eng, lambda e: e.tensor_tensor(out=out4, in0=in4, in1=sc.unsqueeze(2).to_broadcast([128, nh, 128]), op=ALU.mult),
                     reads=reads, writes=writes)
                return
            for hh in range(nh):
                if eng == "act":
                    P.op("act", lambda e, hh=hh: e.activation(out=out4[:, hh, :], in_=in4[:, hh, :], func=AF.Copy,
                                                              scale=sc[:, hh:hh + 1]), reads=reads, writes=writes)
                else:
                    P.op(eng, lambda e, hh=hh: e.tensor_scalar(out=out4[:, hh, :], in0=in4[:, hh, :],
                                                               scalar1=sc[:, hh:hh + 1], scalar2=None, op0=ALU.mult),
                         reads=reads, writes=writes)

        def to_actT(src_ap_fn, src_regs, gain=None, norm=True, direct=False):
            if norm:
                ss_t, ss_r = tsm.next()
                for t in range(NT):
                    jk, jr = hsb.next()
                    P.op("act", lambda e, t=t, jk=jk: e.activation(out=jk[:], in_=src_ap_fn(t), func=AF.Square, accum_out=ss_t[:, t:t + 1]),
                         reads=[src_regs[t]], writes=[jr, ss_r])
                rstd_from_ss(ss_t[:, 0:NT], NT, D, ss_r)
            for t in range(NT):
                src = src_ap_fn(t)
                if direct:
                    bk, br = psum("dense" if t % 2 == 0 else "mix")
                    pv = bk[:].bitcast(BF16).rearrange("p (c n) -> p c n", c=8)
                    for c in range(8):
                        P.op("pe", lambda e, c=c, pv=pv, src=src: e.transpose(pv[:, c, :], src[:, c * 128:(c + 1) * 128], identb[:]),
                             reads=[src_regs[t]], writes=[br])
                    if t % 2 == 0:
                        P.op("act", lambda e, pv=pv, t=t: e.activation(out=actT[:, :, t * 128:(t + 1) * 128], in_=pv, func=AF.Copy),
                             reads=[br], writes=[actT_reg[t]])
                    else:
                        P.op("dve", lambda e, pv=pv, t=t: e.tensor_copy(out=actT[:, :, t * 128:(t + 1) * 128], in_=pv),
                             reads=[br], writes=[actT_reg[t]])
                    continue
                hs_t, hs_r = hsb.next()
                if norm:
                    P.op("dve", lambda e, src=src, hs_t=hs_t, t=t: e.scalar_tensor_tensor(
                        out=hs_t[:], in0=src, scalar=ss_t[:, t:t + 1], in1=gain.t[:], op0=ALU.mult, op1=ALU.mult),
                        reads=[src_regs[t], ss_r, gain.reg], writes=[hs_r])
                else:
                    P.op("act" if t % 2 else "pool", (lambda e, src=src, hs_t=hs_t: e.activation(out=hs_t[:], in_=src, func=AF.Copy)) if t % 2 else
                         (lambda e, src=src, hs_t=hs_t: e.tensor_copy(out=hs_t[:], in_=src)),
                         reads=[src_regs[t]], writes=[hs_r])
                bk, br = psum("dense" if t % 2 == 0 else "mix")
                pv = bk[:].bitcast(BF16).rearrange("p (c n) -> p c n", c=8)
                for c in range(8):
                    P.op("pe", lambda e, c=c, pv=pv, hs_t=hs_t: e.transpose(pv[:, c, :], hs_t[:, c * 128:(c + 1) * 128], identb[:]),
                         reads=[hs_r], writes=[br])
                eng = "act" if t % 2 == 0 else "dve"
                if eng == "act":
                    P.op("act", lambda e, pv=pv, t=t: e.activation(out=actT[:, :, t * 128:(t + 1) * 128], in_=pv, func=AF.Copy),
                         reads=[br], writes=[actT_reg[t]])
                else:
                    P.op("dve", lambda e, pv=pv, t=t: e.tensor_copy(out=actT[:, :, t * 128:(t + 1) * 128], in_=pv),
                         reads=[br], writes=[actT_reg[t]])

        def mm_fm(slot, w3, cchunk, out_fn):
            bk, br = psum("dense")
            for kc in range(8):
                P.op("pe", lambda e, kc=kc, bk=bk: e.matmul(bk[:], lhsT=w3[:, kc, cchunk * 128:(cchunk + 1) * 128],
                                                          rhs=actT[:, kc, :], start=(kc == 0), stop=(kc == 7)),
                     reads=[slot.reg] + actT_reg, writes=[br])
            out_fn(bk, br)

        def mm_tm(slot, w3, t, ncols, out_fn, kch=8, lhs=None, lhs_regs=None, col0=0, pool="dense"):
            bk, br = psum(pool)
            for kc in range(kch):
                lt = actT[:, kc, t * 128:(t + 1) * 128] if lhs is None else lhs(kc)
                P.op("pe", lambda e, kc=kc, bk=bk, lt=lt: e.matmul(bk[:, 0:ncols], lhsT=lt, rhs=w3[:, kc, col0:col0 + ncols],
                                                                  start=(kc == 0), stop=(kc == kch - 1)),
                     reads=[slot.reg] + ([actT_reg[t]] if lhs is None else lhs_regs), writes=[br])
            out_fn(bk, br)

        v4 = lambda tt_: tt_[:].rearrange("p (h n) -> p h n", h=4)
        dbg_n = [0]
        marks = []

        def mark(name):
            marks.append((name, dict(P.cnt)))

        def dbg_dump(ap, regs, width):
            if not dbg:
                return
            c0 = dbg_n[0]
            dbg_n[0] += width
            tt_, tr_ = tf.next()
            P.op("dve", lambda e: e.tensor_copy(out=tt_[:, 0:width], in_=ap), reads=regs, writes=[tr_])
            out_n[0] += 1
            P.dma("sp", lambda e: e.dma_start(out=dbg_d[:, c0:c0 + width], in_=tt_[:, 0:width]), out_sem, out_n[0],
                  reads=[tr_])

        for blk in range(nblk):
            tok0 = blk * TB
            for t in range(NT):
                hb = h_ld[t]
                hb.n += 1
                P.dma("sp", lambda e, t=t: e.dma_start(out=h[:, t, :], in_=x_d[tok0 + t * 128: tok0 + (t + 1) * 128, :]),
                      hb.sem, hb.n, writes=[h_reg[t]])
            for l in layers:
                mark(f"b{blk}l{l}:norm1")
                g_pre = gload(l * 5 + 0)
                to_actT(lambda t: h[:, t, :], h_reg, gain=g_pre)
                mark(f"b{blk}l{l}:proj")
                wabn = wab
                wabn.n += 1
                P.dma("pool", lambda e, l=l: e.dma_start(
                    out=wab.t[:], in_=w_in_d[l, :, 2048:2056].rearrange("(c p) n -> p c n", p=128)),
                    wab.sem, wab.n, writes=[wab.reg])
                win = lambda c0: w_in_d[l, :, c0:c0 + 512].rearrange("(c p) n -> p c n", p=128)
                w8 = lambda ap: ap.rearrange("(c p) n -> p c n", p=128)
                L_qkv = [wreq(win(g_ * 512)) for g_ in range(3)]
                L_z = wreq(win(1536))
                L_qh = wreq(win(3080))
                L_f = wreq(win(2056))
                L_i = wreq(win(2568))
                L_g = wreq(win(3592))
                L_o = [wreq(w8(w_out_d[l, :, 0:512])), wreq(w8(w_out_d[l, :, 512:1024]))]
                L_ud = []
                for g_ in range(8):
                    L_ud.append((wreq(w8(w_up_d[l, :, g_ * 512:(g_ + 1) * 512])), wreq(w8(w_dn_d[l, g_ * 512:(g_ + 1) * 512, :]))))
                L_p = wreq(w8(w_ple_d[l]))
                L_g0 = wreq(w8(w_gate_d[l, :, 0:512]))
                L_g1 = wreq(w8(w_gate_d[l, :, 512:1024]))
                for grp in range(3):
                    s = wget(L_qkv[grp])
                    w3 = s.t[:].rearrange("p (a b) -> p a b", a=8)
                    P.op("pool", lambda e, grp=grp: e.tensor_copy(out=raw[:, :, 0:3], in_=halo[:, l * 12 + grp * 4:l * 12 + grp * 4 + 4, :]),
                         reads=[halo_reg], writes=[raw_reg] + AL_MLP)
                    for hh in range(4):
                        def ev(bk, br, hh=hh):
                            P.op("act", lambda e: e.activation(out=raw[:, hh, 3:3 + TB], in_=bk[:], func=AF.Copy),
                                 reads=[br], writes=[raw_reg])
                        mm_fm(s, w3, hh, ev)
                    wdone(L_qkv[grp])
                    P.op("pool", lambda e, grp=grp: e.tensor_copy(out=halo[:, l * 12 + grp * 4:l * 12 + grp * 4 + 4, :], in_=raw[:, :, TB:TB + 3]),
                         reads=[raw_reg], writes=[halo_reg])
                    for hh in range(4):
                        ch = grp * 4 + hh
                        acc_t, acc_r = tf.next()
                        wc = lambda k: convw[:, l * 48 + ch * 4 + k: l * 48 + ch * 4 + k + 1]
                        P.op("dve", lambda e, hh=hh, acc_t=acc_t, wc=wc: e.tensor_scalar(
                            out=acc_t[:], in0=raw[:, hh, 0:TB], scalar1=wc(0), scalar2=None, op0=ALU.mult),
                            reads=[raw_reg], writes=[acc_r])
                        for k in range(1, 4):
                            P.op("dve", lambda e, hh=hh, k=k, acc_t=acc_t, wc=wc: e.scalar_tensor_tensor(
                                out=acc_t[:], in0=raw[:, hh, k:k + TB], scalar=wc(k), in1=acc_t[:], op0=ALU.mult, op1=ALU.add),
                                reads=[raw_reg, acc_r], writes=[acc_r])
                        P.op("act", lambda e, ch=ch, acc_t=acc_t: e.activation(out=cs[:, ch, :], in_=acc_t[:], func=AF.Silu),
                             reads=[acc_r], writes=[cs_reg[grp]] + AL_MLP)
                s = wget(L_z)
                w3 = s.t[:].rearrange("p (a b) -> p a b", a=8)
                for t in range(NT):
                    def ev(bk, br, t=t):
                        zt, zr = tf.next()
                        P.op("act", lambda e: e.activation(out=zt[:], in_=bk[:], func=AF.Silu), reads=[br], writes=[zr])
                        P.op("pool", lambda e: e.tensor_tensor(out=zw[:, t, :], in0=zt[:], in1=hwb[:, (l * 2) * 512:(l * 2 + 1) * 512], op=ALU.mult),
                             reads=[zr], writes=[zw_reg[t]])
                    mm_tm(s, w3, t, 512, ev)
                wdone(L_z)
                for t in range(NT):
                    def ev(bk, br, t=t):
                        P.op("act", lambda e: e.activation(out=ab[:, t, :], in_=bk[:, 0:8], func=AF.Copy), reads=[br], writes=[ab_reg[t]])
                    mm_tm(wab, wab.t[:], t, 8, ev)
                P.op("act", lambda e: e.activation(out=gall[:, :, 0:4], in_=ab[:, :, 4:8], func=AF.Exp, scale=-1.0), reads=ab_reg, writes=[gall_reg])
                P.op("dve", lambda e: e.tensor_scalar(out=gall[:, :, 0:4], in0=gall[:, :, 0:4], scalar1=1.0, scalar2=None, op0=ALU.add), reads=[gall_reg], writes=[gall_reg])
                P.op("dve", lambda e: e.reciprocal(out=gall[:, :, 0:4], in_=gall[:, :, 0:4]), reads=[gall_reg], writes=[gall_reg])
                P.op("dve", lambda e: e.tensor_tensor(out=gall[:, :, 8:12], in0=ab[:, :, 0:4], in1=smb[:, l * 8 + 4:l * 8 + 8].unsqueeze(1).to_broadcast([128, NT, 4]), op=ALU.add),
                     reads=ab_reg + [gall_reg], writes=[gall_reg])
                P.op("act", lambda e: e.activation(out=gall[:, :, 8:12], in_=gall[:, :, 8:12], func=AF.Exp), reads=[gall_reg], writes=[gall_reg])
                P.op("act", lambda e: e.activation(out=gall[:, :, 8:12], in_=gall[:, :, 8:12], func=AF.Ln, bias=1.0), reads=[gall_reg], writes=[gall_reg])
                P.op("dve", lambda e: e.tensor_tensor(out=gall[:, :, 4:8], in0=gall[:, :, 8:12], in1=negA[:, l * 4:l * 4 + 4].unsqueeze(1).to_broadcast([128, NT, 4]), op=ALU.mult),
                     reads=[gall_reg], writes=[gall_reg])
                pgA_b, pgA_r = psum("mix")
                for t_ in range(NT):
                    for i_, m_ in enumerate((Umat, SUmat, ones)):
                        P.op("pe", lambda e, i_=i_, m_=m_, t_=t_: e.matmul(pgA_b[:, t_ * 12 + i_ * 4:t_ * 12 + i_ * 4 + 4], lhsT=m_, rhs=gall[:, t_, 4:8], start=True, stop=True),
                             reads=[gall_reg], writes=[pgA_r])
                P.op("act", lambda e: e.activation(out=egall[:, :, 0:12], in_=pgA_b[:, 0:NT * 12].rearrange("p (t c) -> p t c", t=NT), func=AF.Exp),
                     reads=[pgA_r, gall_reg], writes=[gall_reg])
                P.op("dve", lambda e: e.tensor_scalar(out=egall[:, :, 12:16], in0=egall[:, :, 0:4], scalar1=-1.0, scalar2=None, op0=ALU.mult), reads=[gall_reg], writes=[gall_reg])

                def head_norm(t, o_ap4, o_regs, wmul, wmul_reg, col0, tfr, tsr):
                    sq_t, sq_r = tfr.next()
                    P.op("act", lambda e: e.activation(out=sq_t[:].rearrange("p (h n) -> p h n", h=4), in_=o_ap4, func=AF.Square), reads=o_regs, writes=[sq_r])
                    so, sor = tsr.next()
                    P.op("dve", lambda e: e.tensor_reduce(out=so[:, 0:4], in_=sq_t[:].rearrange("p (h n) -> p h n", h=4), axis=AX.X, op=ALU.add), reads=[sq_r], writes=[sor])
                    rstd_from_ss(so[:, 0:4], 4, 128.0, sor)
                    a_, ar = tfr.next()
                    head_scale("act", a_[:].rearrange("p (h n) -> p h n", h=4), o_ap4, so[:, 0:4], o_regs + [sor], [ar])
                    P.op("pool", lambda e: e.tensor_tensor(out=mix[:, t, col0:col0 + 512], in0=a_[:], in1=wmul, op=ALU.mult), reads=[ar, wmul_reg], writes=[mix_reg[t], raw_reg])

                hand = {}

                def gdn_A(t):
                    tsl = slice(t * 128, (t + 1) * 128)
                    mark(f"b{blk}l{l}:gdn{t}")
                    pq_b, pq_r = psum("ga")
                    pk_b, pk_r = psum("ga")
                    pv_b, pv_r = psum("ga")
                    for grp, (pb, pr) in enumerate(((pq_b, pq_r), (pk_b, pk_r), (pv_b, pv_r))):
                        for hh in range(4):
                            P.op("pe", lambda e, grp=grp, hh=hh, pb=pb: e.transpose(psb(pb)[:, hh, :], cs[:, grp * 4 + hh, tsl], identb[:]),
                                 reads=[cs_reg[grp]], writes=[pr])
                    sc, scr = tsG.next()
                    for pb, pr, c0 in ((pq_b, pq_r, 0), (pk_b, pk_r, 4)):
                        sq_t, sq_r = (gd["qs"] if c0 == 0 else gd["kh"])
                        for hh in range(4):
                            P.op("act", lambda e, pb=pb, sq_t=sq_t, hh=hh, c0=c0: e.activation(out=sq_t[:, hh * 128:(hh + 1) * 128], in_=psb(pb)[:, hh, :], func=AF.Square,
                                                                                      accum_out=sc[:, c0 + hh:c0 + hh + 1]),
                                 reads=[pr], writes=[sq_r, scr])
                    rstd_from_ss(sc[:, 0:8], 8, 1.0, scr)
                    yield
                    g1, g1r = gall[:, t, :], gall_reg
                    eg, egr = egall[:, t, :], gall_reg
                    s2, s2r = tsG.next()
                    P.op("dve", lambda e: e.tensor_scalar(out=s2[:, 0:4], in0=sc[:, 0:4], scalar1=128.0 ** -0.5, scalar2=None, op0=ALU.mult), reads=[scr], writes=[s2r])
                    P.op("dve", lambda e: e.scalar_tensor_tensor(out=s2[:, 4:8], in0=sc[:, 4:8], scalar=-1.0, in1=g1[:, 0:4], op0=ALU.mult, op1=ALU.mult),
                         reads=[scr, g1r, s2r], writes=[s2r])
                    P.op("dve", lambda e: e.tensor_tensor(out=s2[:, 8:12], in0=sc[:, 4:8], in1=eg[:, 4:8], op=ALU.mult), reads=[scr, egr, s2r], writes=[s2r])
                    yield
                    qs, qsr = gd["qs"]
                    kh, khr = gd["kh"]
                    kb, kbr = gd["kb"]
                    kt, ktr = gdp[t % 2]["kt"]
                    vs, vsr = gdp[t % 2]["vs"]
                    v4 = lambda tt_: tt_[:].rearrange("p (h n) -> p h n", h=4)
                    head_scale("dve", v4(qs), psb(pq_b), s2[:, 0:4], [pq_r, s2r], [qsr])
                    head_scale("dve", v4(kh), psb(pk_b), sc[:, 4:8], [pk_r, scr], [khr])
                    head_scale("dve", v4(kb), psb(pk_b), s2[:, 4:8], [pk_r, s2r], [kbr])
                    head_scale("dve", v4(kt), psb(pk_b), s2[:, 8:12], [pk_r, s2r], [ktr])
                    P.op("act", lambda e: e.activation(out=v4(vs), in_=psb(pv_b), func=AF.Copy), reads=[pv_r], writes=[vsr])
                    yield
                    fm = []
                    for (src, srcr), dn in (((kh, khr), "khT"), ((kb, kbr), "kbT"), ((qs, qsr), "qsT")):
                        pb, pr = psum("ga")
                        for hh in range(4):
                            P.op("pe", lambda e, hh=hh, pb=pb, src=src: e.transpose(psb(pb)[:, hh, :], src[:, hh * 128:(hh + 1) * 128], identb[:]),
                                 reads=[srcr], writes=[pr])
                        dst, dstr = (gd if dn == "kbT" else gdp[t % 2])[dn]
                        P.op("dve" if len(fm) != 1 else "act",
                             (lambda e, dst=dst, pb=pb: e.tensor_copy(out=v4(dst), in_=psb(pb))) if len(fm) != 1 else
                             (lambda e, dst=dst, pb=pb: e.activation(out=v4(dst), in_=psb(pb), func=AF.Copy)),
                             reads=[pr], writes=[dstr])
                        fm.append((dst, dstr))
                    (khT_, khTr), (kbT_, kbTr), (qsT_, qsTr) = fm
                    yield
                    gu, gur = gU.next()
                    P.op("dve", lambda e: e.tensor_tensor(out=v4(gu), in0=Umat.unsqueeze(1).to_broadcast([128, 4, 128]),
                                                          in1=g1[:, 4:8].unsqueeze(2).to_broadcast([128, 4, 128]), op=ALU.mult),
                         reads=[g1r], writes=[gur])
                    pD_b, pD_r = psum("ga")
                    pDT_b, pDT_r = psum("ga")
                    for hh in range(4):
                        hs_ = slice(hh * 128, (hh + 1) * 128)
                        P.op("pe", lambda e, hh=hh, hs_=hs_: e.matmul(ps4(pD_b)[:, hh, :], lhsT=gu[:, hs_], rhs=SUmat, start=True, stop=False),
                             reads=[gur], writes=[pD_r])
                        P.op("pe", lambda e, hh=hh: e.matmul(ps4(pD_b)[:, hh, :], lhsT=identf, rhs=NEGU, start=False, stop=True),
                             reads=[gur], writes=[pD_r])
                        P.op("pe", lambda e, hh=hh, hs_=hs_: e.matmul(ps4(pDT_b)[:, hh, :], lhsT=SUmat, rhs=gu[:, hs_], start=True, stop=False),
                             reads=[gur], writes=[pDT_r])
                        P.op("pe", lambda e, hh=hh: e.matmul(ps4(pDT_b)[:, hh, :], lhsT=identf, rhs=NEGL, start=False, stop=True),
                             reads=[gur], writes=[pDT_r])
                    yield
                    EM, EMr = gd["EM"]
                    EMT, EMTr = gd["EMT"]
                    ECT, ECTr = gd["ECT"]
                    P.op("act", lambda e: e.activation(out=EM[:], in_=pD_b[:], func=AF.Exp), reads=[pD_r], writes=[EMr])
                    P.op("act", lambda e: e.activation(out=EMT[:], in_=pDT_b[:], func=AF.Exp), reads=[pDT_r], writes=[EMTr])
                    P.op("pool", lambda e: e.tensor_tensor(out=v4(ECT), in0=v4(EMT), in1=identb4, op=ALU.add), reads=[EMTr], writes=[ECTr])
                    yield
                    pG_b, pG_r = psum("ga")
                    pGT_b, pGT_r = psum("ga")
                    pQK_b, pQK_r = psum("ga")
                    for hh in range(4):
                        hs_ = slice(hh * 128, (hh + 1) * 128)
                        P.op("pe", lambda e, hh=hh, hs_=hs_: e.matmul(ps4(pG_b)[:, hh, :], lhsT=khT_[:, hs_], rhs=kbT_[:, hs_], start=True, stop=True),
                             reads=[khTr, kbTr], writes=[pG_r])
                        P.op("pe", lambda e, hh=hh, hs_=hs_: e.matmul(ps4(pGT_b)[:, hh, :], lhsT=kbT_[:, hs_], rhs=khT_[:, hs_], start=True, stop=True),
                             reads=[khTr, kbTr], writes=[pGT_r])
                        P.op("pe", lambda e, hh=hh, hs_=hs_: e.matmul(ps4(pQK_b)[:, hh, :], lhsT=khT_[:, hs_], rhs=qsT_[:, hs_], start=True, stop=True),
                             reads=[khTr, qsTr], writes=[pQK_r])
                    yield
                    Pm, Pmr = gdp[t % 2]["Pm0"]
                    PT, PTr = gdp[t % 2]["PT0"]
                    qkT, qkTr = gdp[t % 2]["qkT"]
                    P.op("dve", lambda e: e.tensor_tensor(out=Pm[:], in0=pG_b[:], in1=EM[:], op=ALU.mult), reads=[pG_r, EMr], writes=[Pmr])
                    P.op("dve", lambda e: e.tensor_tensor(out=PT[:], in0=pGT_b[:], in1=EMT[:], op=ALU.mult), reads=[pGT_r, EMTr], writes=[PTr])
                    P.op("dve", lambda e: e.tensor_tensor(out=qkT[:], in0=pQK_b[:], in1=ECT[:], op=ALU.mult), reads=[pQK_r, ECTr], writes=[qkTr])
                    X, Xr = gdp[t % 2]["X0"]
                    P.op("dve", lambda e, X=X, PT=PT: e.tensor_tensor(out=v4(X), in0=v4(PT), in1=identb4, op=ALU.add), reads=[PTr], writes=[Xr])
                    hand[t] = dict(khT_=khT_, khTr=khTr, qsT_=qsT_, qsTr=qsTr, kt=kt, ktr=ktr, vs=vs, vsr=vsr, qkT=qkT, qkTr=qkTr, Pm=Pm, Pmr=Pmr, PT=PT, PTr=PTr, X=X, Xr=Xr, g1=g1, g1r=g1r, eg=eg, egr=egr)
                    yield

                def gdn_DB(t):
                    hd = hand[t]
                    khT_, khTr, qsT_, qsTr, kt, ktr, vs, vsr, qkT, qkTr, Pm, Pmr, PT, PTr, X, Xr, g1, g1r, eg, egr = hd["khT_"], hd["khTr"], hd["qsT_"], hd["qsTr"], hd["kt"], hd["ktr"], hd["vs"], hd["vsr"], hd["qkT"], hd["qkTr"], hd["Pm"], hd["Pmr"], hd["PT"], hd["PTr"], hd["X"], hd["Xr"], hd["g1"], hd["g1r"], hd["eg"], hd["egr"]
                    for n_ in range(1, 7):
                        pP_b, pP_r = psum("gdb")
                        for hh in range(4):
                            hs_ = slice(hh * 128, (hh + 1) * 128)
                            P.op("pe", lambda e, hh=hh, hs_=hs_, PT=PT, Pm=Pm, pP_b=pP_b: e.matmul(ps4(pP_b)[:, hh, :], lhsT=PT[:, hs_], rhs=Pm[:, hs_], start=True, stop=True),
                                 reads=[PTr, Pmr], writes=[pP_r])
                        if n_ < 6:
                            pPT_b, pPT_r = psum("gdb")
                            for hh in range(4):
                                hs_ = slice(hh * 128, (hh + 1) * 128)
                                P.op("pe", lambda e, hh=hh, hs_=hs_, PT=PT, Pm=Pm, pPT_b=pPT_b: e.matmul(ps4(pPT_b)[:, hh, :], lhsT=Pm[:, hs_], rhs=PT[:, hs_], start=True, stop=True),
                                     reads=[PTr, Pmr], writes=[pPT_r])
                        Pn, Pnr = tbG.next()
                        P.op("act", lambda e, Pn=Pn, pP_b=pP_b: e.activation(out=Pn[:], in_=pP_b[:], func=AF.Copy), reads=[pP_r], writes=[Pnr])
                        if n_ < 6:
                            PTn, PTnr = tbG.next()
                            P.op("dve", lambda e, PTn=PTn, pPT_b=pPT_b: e.tensor_copy(out=PTn[:], in_=pPT_b[:]), reads=[pPT_r], writes=[PTnr])
                        pX_b, pX_r = psum("gdb")
                        for hh in range(4):
                            hs_ = slice(hh * 128, (hh + 1) * 128)
                            P.op("pe", lambda e, hh=hh, hs_=hs_, Pn=Pn, X=X, pX_b=pX_b: e.matmul(ps4(pX_b)[:, hh, :], lhsT=Pn[:, hs_], rhs=X[:, hs_], start=True, stop=False),
                                 reads=[Pnr, Xr], writes=[pX_r])
                            P.op("pe", lambda e, hh=hh, hs_=hs_, X=X, pX_b=pX_b: e.matmul(ps4(pX_b)[:, hh, :], lhsT=identb[:], rhs=X[:, hs_], start=False, stop=True),
                                 reads=[Xr], writes=[pX_r])
                        Xn, Xnr = tbG.next()
                        P.op("act" if n_ % 2 else "dve",
                             (lambda e, Xn=Xn, pX_b=pX_b: e.activation(out=Xn[:], in_=pX_b[:], func=AF.Copy)) if n_ % 2 else
                             (lambda e, Xn=Xn, pX_b=pX_b: e.tensor_copy(out=Xn[:], in_=pX_b[:])), reads=[pX_r], writes=[Xnr])
                        X, Xr = Xn, Xnr
                        yield
                        Pm, Pmr = Pn, Pnr
                        if n_ < 6:
                            PT, PTr = PTn, PTnr
                    yield
                    pKS_b, pKS_r = psum("gdb")
                    pO1_b, pO1_r = psum("gdb")
                    for hh in range(4):
                        hs_ = slice(hh * 128, (hh + 1) * 128)
                        P.op("pe", lambda e, hh=hh, hs_=hs_: e.matmul(ps4(pKS_b)[:, hh, :], lhsT=khT_[:, hs_], rhs=Sgb[:, l, hs_], start=True, stop=True),
                             reads=[khTr, Sgb_reg[l]], writes=[pKS_r])
                        P.op("pe", lambda e, hh=hh, hs_=hs_: e.matmul(ps4(pO1_b)[:, hh, :], lhsT=qsT_[:, hs_], rhs=Sgb[:, l, hs_], start=True, stop=True),
                             reads=[qsTr, Sgb_reg[l]], writes=[pO1_r])
                    r_, rr = gd["r"]
                    for hh in range(4):
                        hs_ = slice(hh * 128, (hh + 1) * 128)
                        P.op("dve", lambda e, hh=hh, hs_=hs_: e.scalar_tensor_tensor(out=r_[:, hs_], in0=ps4(pKS_b)[:, hh, :], scalar=eg[:, 12 + hh:13 + hh], in1=vs[:, hs_],
                                                                                    op0=ALU.mult, op1=ALU.add), reads=[pKS_r, egr, vsr], writes=[rr])
                    t1, t1r = tfG.next()
                    head_scale("dve", t1[:].rearrange("p (h n) -> p h n", h=4), ps4(pO1_b), eg[:, 0:4], [pO1_r, egr], [t1r])
                    yield
                    pV_b, pV_r = psum("gdb")
                    for hh in range(4):
                        hs_ = slice(hh * 128, (hh + 1) * 128)
                        P.op("pe", lambda e, hh=hh, hs_=hs_, X=X: e.matmul(ps4(pV_b)[:, hh, :], lhsT=X[:, hs_], rhs=r_[:, hs_], start=True, stop=True),
                             reads=[Xr, rr], writes=[pV_r])
                    vn, vnr = gd["vn"]
                    head_scale("dve", v4(vn), ps4(pV_b), g1[:, 0:4], [pV_r, g1r], [vnr])
                    yield
                    pO2_b, pO2_r = psum("gdb")
                    pS_b, pS_r = psum("gdb")
                    for hh in range(4):
                        hs_ = slice(hh * 128, (hh + 1) * 128)
                        P.op("pe", lambda e, hh=hh, hs_=hs_: e.matmul(ps4(pO2_b)[:, hh, :], lhsT=qkT[:, hs_], rhs=vn[:, hs_], start=True, stop=True),
                             reads=[qkTr, vnr], writes=[pO2_r])
                        P.op("pe", lambda e, hh=hh, hs_=hs_: e.matmul(ps4(pS_b)[:, hh, :], lhsT=kt[:, hs_], rhs=vn[:, hs_], start=True, stop=True),
                             reads=[ktr, vnr], writes=[pS_r])
                    yield
                    og, ogr = tfG.next()
                    P.op("dve", lambda e: e.tensor_tensor(out=og[:], in0=pO2_b[:], in1=t1[:], op=ALU.add), reads=[pO2_r, t1r], writes=[ogr])
                    for hh in range(4):
                        hs_ = slice(hh * 128, (hh + 1) * 128)
                        P.op("dve", lambda e, hh=hh, hs_=hs_: e.scalar_tensor_tensor(out=Sg[:, l, hs_], in0=Sg[:, l, hs_], scalar=eg[:, 8 + hh:9 + hh], in1=ps4(pS_b)[:, hh, :],
                                                                                    op0=ALU.mult, op1=ALU.add), reads=[pS_r, egr, Sg_reg[l]], writes=[Sg_reg[l]])
                    P.op("act", lambda e: e.activation(out=Sgb[:, l, :], in_=Sg[:, l, :], func=AF.Copy), reads=[Sg_reg[l]], writes=[Sgb_reg[l]])

                    yield
                    head_norm(t, og[:].rearrange("p (h n) -> p h n", h=4), [ogr], zw[:, t, :], zw_reg[t], 0, tfG, tsG)
                    yield


                def gdn_strand():
                    yield from gdn_A(0)
                    for t in range(NT):
                        subs = [gdn_DB(t)] + ([gdn_A(t + 1)] if t + 1 < NT else [])
                        while subs:
                            for g_ in list(subs):
                                try:
                                    next(g_)
                                except StopIteration:
                                    subs.remove(g_)
                            yield

                def hgrn_strand():
                    s_q = wget(L_qh)
                    w3q = s_q.t[:].rearrange("p (a b) -> p a b", a=8)
                    for hh in range(4):
                        def ev(bk, br, hh=hh):
                            P.op("act", lambda e: e.activation(out=qhs[:, hh, :], in_=bk[:], func=AF.Silu), reads=[br], writes=[qhs_reg[hh]])
                        mm_fm(s_q, w3q, hh, ev)
                        yield
                    wdone(L_qh)
                    s_f = wget(L_f)
                    w3f = s_f.t[:].rearrange("p (a b) -> p a b", a=8)
                    for hh in range(4):
                        def ev(bk, br, hh=hh):
                            lcol = l * 4 + hh
                            sg, sgr = hgf["sg"]
                            lf, lfr = hgf["lf"]
                            b_, b_r = hgf["b"]
                            bm, bmr = hgf["bm"]
                            bt, btr = hgf["bt"]
                            P.op("act", lambda e: e.activation(out=sg[:], in_=bk[:], func=AF.Sigmoid), reads=[br], writes=[sgr])
                            P.op("dve", lambda e: e.tensor_scalar(out=sg[:], in0=sg[:], scalar1=oml[:, lcol:lcol + 1], scalar2=lb[:, lcol:lcol + 1],
                                                                  op0=ALU.mult, op1=ALU.add), reads=[sgr], writes=[sgr])
                            P.op("act", lambda e: e.activation(out=lf[:], in_=sg[:], func=AF.Ln), reads=[sgr], writes=[lfr])
                            P.op("pool", lambda e: e.tensor_scalar(out=sg[:], in0=sg[:], scalar1=-1.0, scalar2=1.0, op0=ALU.mult, op1=ALU.add),
                                 reads=[sgr, lfr], writes=[sgr])
                            P.op("dve", lambda e: e.tensor_tensor_scan(out=b_[:], data0=rst, data1=lf[:], initial=0.0, op0=ALU.mult, op1=ALU.add),
                                 reads=[lfr], writes=[b_r])
                            b3 = b_[:].rearrange("p (c n) -> p c n", n=64)
                            eb, ebr = lf, lfr
                            P.op("act", lambda e: e.activation(out=eb[:], in_=b_[:], func=AF.Exp), reads=[b_r], writes=[ebr])
                            eb4 = eb[:].rearrange("p (t c n) -> p t c n", t=NT, c=2)
                            q4 = qhs[:, hh, :].rearrange("p (t c n) -> p t c n", t=NT, c=2)
                            P.op("dve", lambda e: e.tensor_tensor(out=qh0[:, hh, :, 0:64], in0=q4[:, :, 0, :], in1=eb4[:, :, 0, :], op=ALU.mult),
                                 reads=[ebr, qhs_reg[hh]], writes=[hg_reg[hh]])
                            P.op("dve", lambda e: e.tensor_tensor(out=qh1[:, hh, :, 64:128], in0=q4[:, :, 1, :], in1=eb4[:, :, 1, :], op=ALU.mult),
                                 reads=[ebr, qhs_reg[hh]], writes=[hg_reg[hh]])
                            P.op("act", lambda e: e.activation(out=dS[:, hh, :], in_=b3[:, :, 63], func=AF.Exp), reads=[b_r], writes=[hg_reg[hh]])
                            bm3 = bm[:].rearrange("p (c n) -> p c n", n=64)
                            bt3 = bt[:].rearrange("p (c n) -> p c n", n=64)
                            P.op("dve", lambda e: e.tensor_tensor(out=bm3, in0=b3, in1=b3[:, :, 31:32].to_broadcast([128, 8, 64]), op=ALU.subtract),
                                 reads=[b_r], writes=[bmr])
                            P.op("pool", lambda e: e.tensor_tensor(out=bt3, in0=b3, in1=b3[:, :, 63:64].to_broadcast([128, 8, 64]), op=ALU.subtract),
                                 reads=[b_r], writes=[btr])
                            P.op("act", lambda e: e.activation(out=bt[:], in_=bt[:], func=AF.Exp, scale=-1.0), reads=[btr], writes=[btr])
                            P.op("dve", lambda e: e.tensor_tensor(out=khT[:, hh, :], in0=sg[:], in1=bt[:], op=ALU.mult),
                                 reads=[btr, sgr], writes=[hg_reg[hh]])
                            P.op("act", lambda e: e.activation(out=bt[:], in_=bm[:], func=AF.Exp), reads=[bmr, btr], writes=[btr])
                            P.op("dve", lambda e: e.tensor_tensor(out=qtT[:, hh, :], in0=qhs[:, hh, :], in1=bt[:], op=ALU.mult),
                                 reads=[btr, qhs_reg[hh]], writes=[hg_reg[hh]] + AL_UT)
                            P.op("act", lambda e: e.activation(out=bm[:], in_=bm[:], func=AF.Exp, scale=-1.0), reads=[bmr], writes=[bmr])
                            P.op("dve", lambda e: e.tensor_tensor(out=ktT[:, hh, :], in0=sg[:], in1=bm[:], op=ALU.mult),
                                 reads=[bmr, sgr], writes=[hg_reg[hh]] + AL_UT)
                        mm_fm(s_f, w3f, hh, ev)
                        yield
                    wdone(L_f)
                    s_i = wget(L_i)
                    w3i = s_i.t[:].rearrange("p (a b) -> p a b", a=8)
                    for t in range(NT):
                        def ev(bk, br, t=t):
                            P.op("act", lambda e: e.activation(out=iv[:, t, :], in_=bk[:], func=AF.Copy), reads=[br], writes=[iv_reg[t]])
                        mm_tm(s_i, w3i, t, 512, ev)
                        yield
                    wdone(L_i)
                    s = wget(L_g)
                    w3 = s.t[:].rearrange("p (a b) -> p a b", a=8)
                    for t in range(NT):
                        def ev(bk, br, t=t):
                            zt, zr = tf.next()
                            P.op("act", lambda e: e.activation(out=zt[:], in_=bk[:], func=AF.Silu), reads=[br], writes=[zr])
                            P.op("pool", lambda e: e.tensor_tensor(out=gw[:, t, :], in0=zt[:], in1=hwb[:, (l * 2 + 1) * 512:(l * 2 + 2) * 512], op=ALU.mult),
                                 reads=[zr], writes=[gw_reg[t]])
                        mm_tm(s, w3, t, 512, ev)
                        yield
                    wdone(L_g)

                    for t in range(NT):
                        tsl = slice(t * 128, (t + 1) * 128)
                        mark(f"b{blk}l{l}:hgrn{t}")
                        pA_b, pA_r = psum("dense")
                        pKH_b, pKH_r = psum("dense")
                        for hh in range(4):
                            P.op("pe", lambda e, hh=hh: e.matmul(ps4(pA_b)[:, hh, :], lhsT=ktT[:, hh, tsl], rhs=qtT[:, hh, tsl], start=True, stop=True),
                                 reads=[hg_reg[hh]], writes=[pA_r])
                            P.op("pe", lambda e, hh=hh: e.transpose(psb(pKH_b)[:, hh, :], khT[:, hh, tsl], identb[:]), reads=[hg_reg[hh]], writes=[pKH_r])
                        yield
                        at_, atr = tf.next()
                        P.op("act", lambda e: e.activation(out=at_[:], in_=pA_b[:], func=AF.Copy), reads=[pA_r], writes=[atr])
                        aT, aTr = tbf.next()
                        P.op("pool", lambda e: e.affine_select(out=v4(aT), in_=at_[:].rearrange("p (h n) -> p h n", h=4), pattern=[[0, 4], [1, 128]],
                                                               compare_op=ALU.is_ge, fill=0.0, base=0, channel_multiplier=-1), reads=[atr], writes=[aTr])
                        P.op("pool", lambda e: e.memset(v4(aT)[0:64, :, 64:128], 0.0), reads=[aTr], writes=[aTr])
                        khm, khmr = tbf.next()
                        P.op("dve", lambda e: e.tensor_copy(out=v4(khm), in_=psb(pKH_b)), reads=[pKH_r], writes=[khmr])
                        Sprev, Sprevr = Shb[l][Shb_cur[l]]
                        yield
                        pU_b, pU_r = psum("dense")
                        for hh in range(4):
                            hs_ = slice(hh * 128, (hh + 1) * 128)
                            P.op("pe", lambda e, hh=hh, hs_=hs_: e.matmul(ps4(pU_b)[:, hh, :], lhsT=khm[0:64, hs_], rhs=iv[0:64, t, hs_], start=True, stop=True),
                                 reads=[khmr, iv_reg[t]], writes=[pU_r])
                        for hh in range(4):
                            hs_ = slice(hh * 128, (hh + 1) * 128)
                            P.op("dve", lambda e, hh=hh, hs_=hs_: e.scalar_tensor_tensor(out=Sh[:, l, hs_], in0=Sh[:, l, hs_], scalar=dS[:, hh, 2 * t:2 * t + 1], in1=ps4(pU_b)[:, hh, :],
                                                                                        op0=ALU.mult, op1=ALU.add), reads=[pU_r, hg_reg[hh], Sh_reg[l]], writes=[Sh_reg[l]])
                        Smid, Smidr = Shb[l][1 - Shb_cur[l]]
                        P.op("act", lambda e: e.activation(out=Smid[:], in_=Sh[:, l, :], func=AF.Copy), reads=[Sh_reg[l]], writes=[Smidr])
                        yield
                        pU2_b, pU2_r = psum("dense")
                        for hh in range(4):
                            hs_ = slice(hh * 128, (hh + 1) * 128)
                            P.op("pe", lambda e, hh=hh, hs_=hs_: e.matmul(ps4(pU2_b)[:, hh, :], lhsT=khm[64:128, hs_], rhs=iv[64:128, t, hs_], start=True, stop=True),
                                 reads=[khmr, iv_reg[t]], writes=[pU2_r])
                        for hh in range(4):
                            hs_ = slice(hh * 128, (hh + 1) * 128)
                            P.op("dve", lambda e, hh=hh, hs_=hs_: e.scalar_tensor_tensor(out=Sh[:, l, hs_], in0=Sh[:, l, hs_], scalar=dS[:, hh, 2 * t + 1:2 * t + 2], in1=ps4(pU2_b)[:, hh, :],
                                                                                        op0=ALU.mult, op1=ALU.add), reads=[pU2_r, hg_reg[hh], Sh_reg[l], Smidr], writes=[Sh_reg[l]])
                        yield
                        pO_b, pO_r = psum("dense")
                        for hh in range(4):
                            hs_ = slice(hh * 128, (hh + 1) * 128)
                            P.op("pe", lambda e, hh=hh, hs_=hs_: e.matmul(ps4(pO_b)[:, hh, :], lhsT=aT[:, hs_], rhs=iv[:, t, hs_], start=True, stop=False),
                                 reads=[aTr, iv_reg[t]], writes=[pO_r])
                            P.op("pe", lambda e, hh=hh, hs_=hs_: e.matmul(ps4(pO_b)[:, hh, :], lhsT=qh0[:, hh, t, :], rhs=Sprev[:, hs_], start=False, stop=False),
                                 reads=[hg_reg[hh], Sprevr], writes=[pO_r])
                            P.op("pe", lambda e, hh=hh, hs_=hs_: e.matmul(ps4(pO_b)[:, hh, :], lhsT=qh1[:, hh, t, :], rhs=Smid[:, hs_], start=False, stop=True),
                                 reads=[hg_reg[hh], Smidr], writes=[pO_r])
                        Snew, Snewr = Sprev, Sprevr
                        P.op("act", lambda e: e.activation(out=Snew[:], in_=Sh[:, l, :], func=AF.Copy), reads=[Sh_reg[l]], writes=[Snewr])
                        yield
                        head_norm(t, ps4(pO_b), [pO_r], gw[:, t, :], gw_reg[t], 512, tf, tsm)

                        yield

                strands = [gdn_strand(), hgrn_strand()]
                while strands:
                    for s_ in list(strands):
                        try:
                            next(s_)
                        except StopIteration:
                            strands.remove(s_)

                mark(f"b{blk}l{l}:wout")
                g_pm = gload(l * 5 + 1)
                to_actT(lambda t: mix[:, t, :], mix_reg, norm=False, direct=True)

                def post_norm(src_fn, src_regs_fn, gain, t, eng2="pool"):
                    ss_t, ss_r = tsm.next()
                    for hf in range(2):
                        jk, jr = junk.next()
                        P.op("act", lambda e, hf=hf, jk=jk: e.activation(out=jk[:, 0:512], in_=src_fn(hf), func=AF.Square, accum_out=ss_t[:, hf:hf + 1]),
                             reads=src_regs_fn(hf), writes=[jr, ss_r])
                    P.op("dve", lambda e: e.tensor_tensor(out=ss_t[:, 0:1], in0=ss_t[:, 0:1], in1=ss_t[:, 1:2], op=ALU.add), reads=[ss_r], writes=[ss_r])
                    rstd_from_ss(ss_t[:, 0:1], 1, D, ss_r)
                    for hf in range(2):
                        tm_, tmr = tf.next()
                        P.op("dve", lambda e, hf=hf, tm_=tm_: e.scalar_tensor_tensor(out=tm_[:], in0=src_fn(hf), scalar=ss_t[:, 0:1], in1=gain.t[:, hf * 512:(hf + 1) * 512],
                                                                                    op0=ALU.mult, op1=ALU.mult), reads=src_regs_fn(hf) + [ss_r, gain.reg], writes=[tmr])
                        P.op(eng2, lambda e, hf=hf, tm_=tm_: e.tensor_tensor(out=h[:, t, hf * 512:(hf + 1) * 512], in0=h[:, t, hf * 512:(hf + 1) * 512], in1=tm_[:], op=ALU.add),
                             reads=[tmr, h_reg[t]], writes=[h_reg[t]])

                s_o0 = wget(L_o[0])
                s_o1 = wget(L_o[1])
                for t in range(NT):
                    res = []
                    for hf, s_ in enumerate((s_o0, s_o1)):
                        mm_tm(s_, s_.t[:].rearrange("p (a b) -> p a b", a=8), t, 512, lambda bk, br: res.append((bk, br)),
                              pool=("dense" if t % 2 == 0 else "mix"))
                    post_norm(lambda hf: res[hf][0][:], lambda hf: [res[hf][1]], g_pm, t)
                wdone(L_o[0])
                wdone(L_o[1])

                mark(f"b{blk}l{l}:mlp")
                g_pl = gload(l * 5 + 2)
                to_actT(lambda t: h[:, t, :], h_reg, gain=g_pl)
                for g in range(8):
                    s_u = wget(L_ud[g][0])
                    w3u = s_u.t[:].rearrange("p (a b) -> p a b", a=8)
                    u_t, u_r = uT_items[uT_i[0] % 2]
                    uT_i[0] += 1
                    for hc in range(4):
                        def ev(bk, br, hc=hc):
                            sq_t, sq_r = tf.next()
                            P.op("act", lambda e: e.activation(out=sq_t[:], in_=bk[:], func=AF.Square), reads=[br], writes=[sq_r])
                            P.op("dve", lambda e: e.scalar_tensor_tensor(out=u_t[:, hc, :], in0=bk[:], scalar=0.0, in1=sq_t[:], op0=ALU.is_gt, op1=ALU.mult),
                                 reads=[br, sq_r], writes=[u_r] + hg_reg)
                        mm_fm(s_u, w3u, hc, ev)
                    wdone(L_ud[g][0])
                    s_d = wget(L_ud[g][1])
                    w3d = s_d.t[:].rearrange("p (a b) -> p a b", a=4)
                    for t in range(NT):
                        for hf in range(2):
                            def ev(bk, br, t=t, hf=hf):
                                dst = yacc[:, t, hf * 512:(hf + 1) * 512]
                                if g == 0:
                                    P.op("act", lambda e: e.activation(out=dst, in_=bk[:], func=AF.Copy), reads=[br], writes=[yacc_reg[t]] + AL_MIX)
                                else:
                                    P.op("dve", lambda e: e.tensor_tensor(out=dst, in0=bk[:], in1=dst, op=ALU.add), reads=[br, yacc_reg[t]], writes=[yacc_reg[t]])
                            mm_tm(s_d, w3d, t, 512, ev, kch=4, lhs=lambda kc, t=t: u_t[:, kc, t * 128:(t + 1) * 128], lhs_regs=[u_r], col0=hf * 512, pool="mix")
                    wdone(L_ud[g][1])
                g_pml = gload(l * 5 + 3)
                for t in range(NT):
                    post_norm(lambda hf, t=t: yacc[:, t, hf * 512:(hf + 1) * 512], lambda hf, t=t: [yacc_reg[t]], g_pml, t)

                mark(f"b{blk}l{l}:ple")
                g_ple = gload(l * 5 + 4)
                pbuf.n += 1
                P.dma("pool", lambda e, l=l: e.dma_start(out=pbuf.t[:], in_=p_d[l, tok0:tok0 + TB, :].rearrange("(t p) e -> p t e", p=128)),
                      pbuf.sem, pbuf.n, writes=[pbuf.reg])
                s_p = wget(L_p)
                s_g0 = wget(L_g0)
                s_g1 = wget(L_g1)
                to_actT(lambda t: h[:, t, :], h_reg, norm=False)
                w3p = s_p.t[:, 0:2048].rearrange("p (a b) -> p a b", a=2)
                for t in range(NT):
                    pb, pr = psum("mix")
                    for c in range(2):
                        P.op("pe", lambda e, c=c, pb=pb: e.transpose(psb(pb)[:, c, :], pbuf.t[:, t, c * 128:(c + 1) * 128], identb[:]), reads=[pbuf.reg], writes=[pr])
                    pT_, pTr = tbf.next()
                    P.op("dve", lambda e, pb=pb, pT_=pT_: e.tensor_copy(out=pT_[:, 0:256], in_=pb[:].bitcast(BF16)[:, 0:256]), reads=[pr], writes=[pTr])
                    eres = []
                    for hf in range(2):
                        mm_tm(s_p, w3p, t, 512, lambda bk, br: eres.append((bk, br)), kch=2,
                              lhs=lambda kc, pT_=pT_: pT_[:, kc * 128:(kc + 1) * 128], lhs_regs=[pTr], col0=hf * 512,
                              pool=("dense" if t % 2 == 0 else "mix"))
                    ss_t, ss_r = tsm.next()
                    for hf in range(2):
                        jk, jr = junk.next()
                        P.op("act", lambda e, hf=hf, jk=jk: e.activation(out=jk[:, 0:512], in_=eres[hf][0][:], func=AF.Square, accum_out=ss_t[:, hf:hf + 1]),
                             reads=[eres[hf][1]], writes=[jr, ss_r])
                    P.op("dve", lambda e: e.tensor_tensor(out=ss_t[:, 0:1], in0=ss_t[:, 0:1], in1=ss_t[:, 1:2], op=ALU.add), reads=[ss_r], writes=[ss_r])
                    rstd_from_ss(ss_t[:, 0:1], 1, D, ss_r)
                    en = []
                    for hf in range(2):
                        tm_, tmr = tf.next()
                        P.op("dve", lambda e, hf=hf, tm_=tm_: e.scalar_tensor_tensor(out=tm_[:], in0=eres[hf][0][:], scalar=ss_t[:, 0:1], in1=g_ple.t[:, hf * 512:(hf + 1) * 512],
                                                                                    op0=ALU.mult, op1=ALU.mult), reads=[eres[hf][1], ss_r, g_ple.reg], writes=[tmr])
                        en.append((tm_, tmr))
                    for hf, s_ in enumerate((s_g0, s_g1)):
                        def ev(bk, br, hf=hf):
                            gt, gtr = tf.next()
                            P.op("act", lambda e: e.activation(out=gt[:], in_=bk[:], func=AF.Sigmoid), reads=[br], writes=[gtr])
                            P.op("pool", lambda e: e.tensor_tensor(out=gt[:], in0=gt[:], in1=en[hf][0][:], op=ALU.mult), reads=[gtr, en[hf][1]], writes=[gtr])
                            P.op("pool", lambda e: e.tensor_tensor(out=h[:, t, hf * 512:(hf + 1) * 512], in0=h[:, t, hf * 512:(hf + 1) * 512], in1=gt[:], op=ALU.add),
                                 reads=[gtr, h_reg[t]], writes=[h_reg[t]])
                        mm_tm(s_, s_.t[:].rearrange("p (a b) -> p a b", a=8), t, 512, ev, pool=("mix" if t % 2 == 0 else "dense"))
                wdone(L_p)
                wdone(L_g0)
                wdone(L_g1)
            for t in range(NT):
                st_n[t] += 1
                P.dma("sp", lambda e, t=t: e.dma_start(out=out_d[tok0 + t * 128: tok0 + (t + 1) * 128, :], in_=h[:, t, :]), st_sem[t], st_n[t],
                      reads=[h_reg[t]])
        P.wait_all("sp", [(st_sem[t], st_n[t] * 16, "dma") for t in range(NT)])
        mark("end")
        P.emit()
    nc._marks = marks
    return nc


def host_consts():
    idx = np.arange(128)
    U = (idx[:, None] <= idx[None, :]).astype(np.float32)
    SU = (idx[:, None] > idx[None, :]).astype(np.float32)
    ones = np.ones((128, 128), np.float32)
    rst = np.ones((128, 512), np.float32)
    rst[:, ::64] = 0.0
    return {
        "c_identb": np.eye(128, dtype=np.float32).astype(ml_dtypes.bfloat16),
        "c_f32": np.ascontiguousarray(np.concatenate([U, SU, ones, rst, np.eye(128, dtype=np.float32),
                                                      np.where(idx[None, :] >= idx[:, None], -30000.0, 0.0).astype(np.float32),
                                                      np.where(idx[None, :] <= idx[:, None], -30000.0, 0.0).astype(np.float32)], axis=1)),
    }


def make_in_maps(inp):
    f = lambda a: np.ascontiguousarray(np.asarray(a, dtype=np.float32))
    NL = NLAYER
    gains = np.stack([np.stack([f(inp[k])[l] for k in ("pre_mix_norm", "post_mix_norm", "pre_mlp_norm", "post_mlp_norm", "ple_norm")])
                      for l in range(NL)]).reshape(NL * 5, D)
    hw = np.stack([np.stack([np.tile(f(inp["gdn_norm"])[l], 4), np.tile(f(inp["hgrn_norm"])[l], 4)]) for l in range(NL)]).reshape(NL * 2, 512)
    sm = np.stack([np.stack([f(inp["gdn_a_log"])[l], f(inp["gdn_dt_bias"])[l]]) for l in range(NL)]).reshape(NL * 2, 4)
    cw = f(inp["gdn_conv"]).reshape(NL, 4, 12, 128).transpose(3, 0, 2, 1).reshape(128, NL * 48)
    lbl = f(inp["hgrn_lb_logits"]).reshape(NL, 4, 128).transpose(2, 0, 1).reshape(128, NL * 4)
    shared = {
        "w_in": f(inp["w_in"]), "w_out": f(inp["w_out"]), "w_up": f(inp["w_mlp_up"]), "w_dn": f(inp["w_mlp_down"]),
        "w_ple": f(inp["w_ple_proj"]), "w_gate": f(inp["w_ple_gate"]),
        "gains": np.ascontiguousarray(gains), "hw": np.ascontiguousarray(hw), "sm": np.ascontiguousarray(sm),
        "convw": np.ascontiguousarray(cw), "lbl": np.ascontiguousarray(lbl),
    }
    shared.update(host_consts())
    x = f(inp["x"])
    p = f(inp["p"])
    maps = []
    for b in range(x.shape[0]):
        m = dict(shared)
        m["x"] = np.ascontiguousarray(x[b])
        m["p"] = np.ascontiguousarray(p[:, b])
        maps.append(m)
    return maps


def kernel(**inputs):
    nc = build()
    maps = make_in_maps(inputs)
    res = run_bass_kernel_spmd(nc, maps, core_ids=list(range(len(maps))))
    return np.stack([np.asarray(r["out"], dtype=np.float32) for r in res.results], axis=0)
```
